# Optimizing a Trainium2 kernel written in Bass

```python
import jax, jax.numpy as jnp
from jax import lax
import numpy as np

D_MODEL = 1024
BATCH = 8
SEQ = 2048
DEPTH = 4
DEC_BATCH = 128
DEC_SEQ = 8
PAST_LEN = 16384
PAGE_SIZE = 128

D_MIX = D_MODEL
D_POOL = D_MIX // 2
D_CONV = D_MIX - D_POOL
POOL_WINDOWS = (2, 4, 8, 16)
N_POOL_GROUPS = len(POOL_WINDOWS)
POOL_GROUP = D_POOL // N_POOL_GROUPS
POOL_HIST = max(POOL_WINDOWS) - 1
CONV_WIDTH = 31
CONV_HIST = CONV_WIDTH - 1
D_FF = ((8 * D_MODEL // 3 + 127) // 128) * 128
FFN_CONV_WIDTH = 3
FFN_HIST = FFN_CONV_WIDTH - 1
D_PLE = 256
RMS_EPS = 1e-6
LN_EPS = 1e-5

kernel_name = "hybrid_pool_conformer_decoder_step"


def rmsnorm(x, g):
    xf = x.astype(jnp.float32)
    y = xf * lax.rsqrt(jnp.mean(xf * xf, axis=-1, keepdims=True) + RMS_EPS)
    return (y * g.astype(jnp.float32)).astype(x.dtype)


def layernorm(x, g, b):
    xf = x.astype(jnp.float32)
    mu = jnp.mean(xf, axis=-1, keepdims=True)
    xc = xf - mu
    var = jnp.mean(xc * xc, axis=-1, keepdims=True)
    y = xc * lax.rsqrt(var + LN_EPS) * g.astype(jnp.float32) + b.astype(jnp.float32)
    return y.astype(x.dtype)


def causal_depthwise(ext, w, b):
    c = ext.shape[-1]
    out = lax.conv_general_dilated(ext, w[:, None, :].astype(ext.dtype), window_strides=(1,),
                                   padding='VALID', dimension_numbers=('NWC', 'WIO', 'NWC'),
                                   feature_group_count=c)
    return out + b.astype(ext.dtype)


def pool_mix(a_ext, pos0, w_pool, pool_scale):
    bsz, ext_len, _ = a_ext.shape
    t_new = ext_len - POOL_HIST
    af = a_ext.astype(jnp.float32)
    csum = jnp.concatenate([jnp.zeros_like(af[:, :1]), lax.cumsum(af, axis=1)], axis=1)
    end = csum[:, POOL_HIST + 1:]
    xt = af[:, POOL_HIST:]
    pos = pos0 + jnp.arange(t_new, dtype=jnp.int32)
    outs = []
    for g, w in enumerate(POOL_WINDOWS):
        lo, hi = g * POOL_GROUP, (g + 1) * POOL_GROUP
        start = csum[:, POOL_HIST + 1 - w: POOL_HIST + 1 - w + t_new, lo:hi]
        cnt = jnp.minimum(pos + 1, w).astype(jnp.float32)[None, :, None]
        outs.append((end[..., lo:hi] - start) / cnt - xt[..., lo:hi])
    z = jnp.stack(outs, axis=2)
    y = jnp.einsum('btgc,gcd->btgd', z, w_pool.astype(jnp.float32)).reshape(bsz, t_new, D_POOL)
    return (y * pool_scale.astype(jnp.float32)).astype(a_ext.dtype)


def trunk_layer(x, p, st_pool, st_conv, st_ffn, pos0,
                g_mix, w_in, w_pool, pool_scale, w_dw, b_dw, ln_g, ln_b, w_pw,
                g_out_a, g_out_b, w_out, g_ffn, w_ffn_in, w_ffn_dw, b_ffn_dw, w_ffn_out,
                w_ple, g_ple, w_ple_gate):
    n = rmsnorm(x, g_mix)
    z = n @ w_in
    a = z[..., :D_POOL]
    u = z[..., D_POOL:]
    a_ext = jnp.concatenate([st_pool.astype(a.dtype), a], axis=1)
    ya = pool_mix(a_ext, pos0, w_pool, pool_scale)
    new_pool = a_ext[:, -POOL_HIST:]
    glu = u[..., :D_CONV] * jax.nn.sigmoid(u[..., D_CONV:])
    c_ext = jnp.concatenate([st_conv.astype(glu.dtype), glu], axis=1)
    c = layernorm(causal_depthwise(c_ext, w_dw, b_dw), ln_g, ln_b)
    yb = jax.nn.silu(c) @ w_pw
    new_conv = c_ext[:, -CONV_HIST:]
    mix = jnp.concatenate([rmsnorm(ya, g_out_a), rmsnorm(yb, g_out_b)], axis=-1) @ w_out
    x = x + mix
    n2 = rmsnorm(x, g_ffn)
    gu = n2 @ w_ffn_in
    gate = gu[..., :D_FF]
    up = gu[..., D_FF:]
    g_ext = jnp.concatenate([st_ffn.astype(gate.dtype), gate], axis=1)
    gc = causal_depthwise(g_ext, w_ffn_dw, b_ffn_dw)
    x = x + (jax.nn.silu(gc) * up) @ w_ffn_out
    new_ffn = g_ext[:, -FFN_HIST:]
    e = rmsnorm(p.astype(x.dtype) @ w_ple, g_ple)
    x = x + jax.nn.sigmoid(x @ w_ple_gate) * e
    return x, new_pool, new_conv, new_ffn


def run_trunk(x, p, st_pool, st_conv, st_ffn, pos0, weights, final_norm):
    pools, convs, ffns = [], [], []
    for i in range(DEPTH):
        lw = [w[i] for w in weights]
        x, npool, nconv, nffn = trunk_layer(x, p[i], st_pool[i], st_conv[i], st_ffn[i], pos0, *lw)
        pools.append(npool)
        convs.append(nconv)
        ffns.append(nffn)
    return rmsnorm(x, final_norm), jnp.stack(pools), jnp.stack(convs), jnp.stack(ffns)


def setup_inputs(seed: int = 0) -> dict:
    key = jax.random.key(seed)
    ks = jax.random.split(key, 32)
    f32 = jnp.float32
    nrm = lambda k, shape, s: jax.random.normal(k, shape, f32) * s
    gain = lambda k, shape: 1.0 + 0.05 * jax.random.normal(k, shape, f32)
    return {
        "x_prompt": nrm(ks[0], (BATCH, SEQ, D_MODEL), 1.0),
        "x_sample": nrm(ks[1], (DEC_BATCH, DEC_SEQ, D_MODEL), 1.0),
        "state_pool": nrm(ks[2], (DEPTH, DEC_BATCH, POOL_HIST, D_POOL), 1.0),
        "state_conv": nrm(ks[3], (DEPTH, DEC_BATCH, CONV_HIST, D_CONV), 0.5),
        "state_ffn": nrm(ks[4], (DEPTH, DEC_BATCH, FFN_HIST, D_FF), 1.0),
        "p_prompt": nrm(ks[5], (DEPTH, BATCH, SEQ, D_PLE), 1.0),
        "p_sample": nrm(ks[6], (DEPTH, DEC_BATCH, DEC_SEQ, D_PLE), 1.0),
        "g_mix": gain(ks[7], (DEPTH, D_MODEL)),
        "w_in": nrm(ks[8], (DEPTH, D_MODEL, D_POOL + 2 * D_CONV), D_MODEL ** -0.5),
        "w_pool": nrm(ks[9], (DEPTH, N_POOL_GROUPS, POOL_GROUP, POOL_GROUP), POOL_GROUP ** -0.5),
        "pool_scale": 1.0 + 0.1 * jax.random.normal(ks[10], (DEPTH, D_POOL), f32),
        "w_dw": nrm(ks[11], (DEPTH, CONV_WIDTH, D_CONV), CONV_WIDTH ** -0.5),
        "b_dw": nrm(ks[12], (DEPTH, D_CONV), 0.01),
        "ln_g": gain(ks[13], (DEPTH, D_CONV)),
        "ln_b": nrm(ks[14], (DEPTH, D_CONV), 0.01),
        "w_pw": nrm(ks[15], (DEPTH, D_CONV, D_CONV), D_CONV ** -0.5),
        "g_out_a": gain(ks[16], (DEPTH, D_POOL)),
        "g_out_b": gain(ks[17], (DEPTH, D_CONV)),
        "w_out": nrm(ks[18], (DEPTH, D_MIX, D_MODEL), D_MIX ** -0.5),
        "g_ffn": gain(ks[19], (DEPTH, D_MODEL)),
        "w_ffn_in": nrm(ks[20], (DEPTH, D_MODEL, 2 * D_FF), D_MODEL ** -0.5),
        "w_ffn_dw": nrm(ks[21], (DEPTH, FFN_CONV_WIDTH, D_FF), FFN_CONV_WIDTH ** -0.5),
        "b_ffn_dw": nrm(ks[22], (DEPTH, D_FF), 0.01),
        "w_ffn_out": nrm(ks[23], (DEPTH, D_FF, D_MODEL), D_FF ** -0.5),
        "w_ple": nrm(ks[24], (DEPTH, D_PLE, D_MODEL), D_PLE ** -0.5),
        "g_ple": gain(ks[25], (DEPTH, D_MODEL)),
        "w_ple_gate": nrm(ks[26], (DEPTH, D_MODEL, D_MODEL), D_MODEL ** -0.5),
        "final_norm": gain(ks[27], (D_MODEL,)),
    }


def reference(x_prompt, x_sample, state_pool, state_conv, state_ffn, p_prompt, p_sample,
              g_mix, w_in, w_pool, pool_scale, w_dw, b_dw, ln_g, ln_b, w_pw,
              g_out_a, g_out_b, w_out, g_ffn, w_ffn_in, w_ffn_dw, b_ffn_dw, w_ffn_out,
              w_ple, g_ple, w_ple_gate, final_norm):
    weights = (g_mix, w_in, w_pool, pool_scale, w_dw, b_dw, ln_g, ln_b, w_pw,
               g_out_a, g_out_b, w_out, g_ffn, w_ffn_in, w_ffn_dw, b_ffn_dw, w_ffn_out,
               w_ple, g_ple, w_ple_gate)
    dt = x_prompt.dtype
    zp_pool = jnp.zeros((DEPTH, BATCH, POOL_HIST, D_POOL), dt)
    zp_conv = jnp.zeros((DEPTH, BATCH, CONV_HIST, D_CONV), dt)
    zp_ffn = jnp.zeros((DEPTH, BATCH, FFN_HIST, D_FF), dt)
    y_prompt, pool_p, conv_p, ffn_p = run_trunk(x_prompt, p_prompt, zp_pool, zp_conv, zp_ffn, 0,
                                                weights, final_norm)
    y_sample, pool_s, conv_s, ffn_s = run_trunk(x_sample, p_sample, state_pool, state_conv, state_ffn,
                                                PAST_LEN, weights, final_norm)
    return (y_prompt, y_sample, pool_p, conv_p, ffn_p, pool_s, conv_s, ffn_s)
```

```python
import numpy as np
from contextlib import ExitStack
import concourse.bass as bass
import concourse.mybir as mybir
from concourse.bass_utils import run_bass_kernel_spmd

F32 = mybir.dt.float32
BF16 = mybir.dt.bfloat16
AF = mybir.ActivationFunctionType
ALU = mybir.AluOpType

L = 4
D = 1024
DC = 8
DP = 512
DFF = 2816
NJ = 22
DPLE = 256
PT = 1024
SS = 8
ST = 8
T = PT + SS * ST
NPASS = 2
NBLK = 3
HP, HC, HF = 15, 30, 2
RMS_EPS = 1e-6
LN_EPS = 1e-5
NSLOT = 4
SLOT_EL = 4096
FFN_GROUPS = [(0, 12), (12, 10)]
FO_W = 256

DEBUG = False
E_MIX = "dve"
E_DIAG = "pool"


def blk(b):
    return (0, 512) if b == 0 else ((512, 512) if b == 1 else (1024, 64))


class V:
    __slots__ = ("ap", "keys")

    def __init__(self, ap, keys):
        self.ap = ap
        self.keys = list(keys)


def as3(v):
    return V(v.ap.rearrange("p (s t) -> p s t", t=ST), v.keys)


class _Eng:
    def __init__(self, name, h, sem):
        self.name = name
        self.h = h
        self.sem = sem
        self.count = 0
        self.waited = {}


class Prog:
    def __init__(self, nc, stack, n_dma_sems=32):
        self.nc = nc
        self.e = {}
        for name, h in (("pe", nc.tensor), ("act", nc.scalar), ("dve", nc.vector),
                        ("pool", nc.gpsimd), ("sp", nc.sync)):
            sem = stack.enter_context(nc.semaphore("sem_" + name))
            self.e[name] = _Eng(name, h, sem)
        self.dsem = [stack.enter_context(nc.semaphore("dsem%d" % i)) for i in range(n_dma_sems)]
        self.dcnt = [0] * n_dma_sems
        self.dnext = {"sp": 0, "pool": n_dma_sems // 2}
        self.drange = {"sp": (0, n_dma_sems // 2), "pool": (n_dma_sems // 2, n_dma_sems)}
        self.res = {}
        self.fence = []
        self.scoped = set()
        self.out_toks = []
        self.ninst = {k: 0 for k in self.e}

    def _need(self, e, tok):
        sem, val = tok
        if e.waited.get(sem.num, 0) >= val:
            return
        e.h.wait_ge(sem, val)
        e.waited[sem.num] = val

    def alias_fence(self, names):
        f = [(o.sem, o.count) for o in self.e.values() if o.count > 0]
        f += [(self.dsem[k], self.dcnt[k]) for k in range(len(self.dsem)) if self.dcnt[k] > 0]
        self.fence = f
        self.scoped |= set(names)
        for k in [k for k in self.res if (k[0] if isinstance(k, tuple) else k) in set(names)]:
            del self.res[k]

    def _deps(self, e, reads, writes):
        toks = []
        for k in list(reads) + list(writes):
            if k not in self.res and (k[0] if isinstance(k, tuple) else k) in self.scoped:
                toks.extend(self.fence)
                break
        for k in reads:
            r = self.res.get(k)
            if r is not None and r[0] is not None:
                toks.append(r[0])
        for k in writes:
            r = self.res.get(k)
            if r is not None:
                if r[0] is not None:
                    toks.append(r[0])
                toks.extend(r[1].values())
        for t in toks:
            if t[0] is e.sem and e.name == "pe":
                continue
            self._need(e, t)

    def _record(self, rkey, tok, reads, writes):
        for k in reads:
            r = self.res.get(k)
            if r is None:
                r = [None, {}]
                self.res[k] = r
            r[1][rkey] = tok
        for k in writes:
            self.res[k] = [tok, {}]

    def op(self, ename, fn, reads, writes, signal=True):
        e = self.e[ename]
        self._deps(e, reads, writes)
        inst = fn(e.h)
        self.ninst[ename] += 1
        if signal:
            e.count += 1
            inst.then_inc(e.sem, 1)
            tok = (e.sem, e.count)
        else:
            tok = (e.sem, e.count + 1)
        self._record(ename, tok, reads, writes)
        return tok

    def dma(self, qname, out, in_, is_output=False):
        e = self.e[qname]
        reads = in_.keys
        writes = out.keys
        self._deps(e, reads, writes)
        k = self.dnext[qname]
        lo, hi = self.drange[qname]
        self.dnext[qname] = lo + (k + 1 - lo) % (hi - lo)
        sem = self.dsem[k]
        if self.dcnt[k] > 0:
            self._need(e, (sem, self.dcnt[k]))
        e.h.dma_start(out=out.ap, in_=in_.ap).then_inc(sem, 16)
        self.ninst[qname] += 1
        self.dcnt[k] += 16
        tok = (sem, self.dcnt[k])
        self._record(("dma", k), tok, reads, writes)
        if is_output:
            self.out_toks.append(tok)
        return tok

    def barrier(self):
        sp = self.e["sp"]
        for k in range(len(self.dsem)):
            if self.dcnt[k] > 0:
                self._need(sp, (self.dsem[k], self.dcnt[k]))
        for name in ("pe", "act", "dve", "pool"):
            o = self.e[name]
            if o.count > 0:
                self._need(sp, (o.sem, o.count))
        sp.count += 1
        sp.h.nop().then_inc(sp.sem, 1)
        for name in ("pe", "act", "dve", "pool"):
            self._need(self.e[name], (sp.sem, sp.count))
        self.res.clear()

    def finish(self):
        self.barrier()

    @staticmethod
    def _sk(x):
        return x.keys if isinstance(x, V) else []

    @staticmethod
    def _sa(x):
        return x.ap if isinstance(x, V) else x

    def act(self, out, in_, func, scale=1.0, bias=0.0):
        rd = in_.keys + self._sk(scale) + self._sk(bias)
        sc, bi = self._sa(scale), self._sa(bias)
        return self.op("act", lambda h: h.activation(out=out.ap, in_=in_.ap, func=func, scale=sc, bias=bi),
                       rd, out.keys)

    def tt(self, eng, out, in0, in1, op):
        return self.op(eng, lambda h: h.tensor_tensor(out=out.ap, in0=in0.ap, in1=in1.ap, op=op),
                       in0.keys + in1.keys, out.keys)

    def ts(self, eng, out, in0, s1, op0, s2=None, op1=None):
        rd = in0.keys + self._sk(s1) + self._sk(s2)
        a1, a2 = self._sa(s1), self._sa(s2)
        if op1 is None:
            return self.op(eng, lambda h: h.tensor_scalar(out=out.ap, in0=in0.ap, scalar1=a1, scalar2=None, op0=op0),
                           rd, out.keys)
        return self.op(eng, lambda h: h.tensor_scalar(out=out.ap, in0=in0.ap, scalar1=a1, scalar2=a2,
                                                      op0=op0, op1=op1), rd, out.keys)

    def stt(self, out, in0, scalar, in1, op0, op1):
        rd = in0.keys + in1.keys + self._sk(scalar)
        sc = self._sa(scalar)
        return self.op("dve", lambda h: h.scalar_tensor_tensor(out=out.ap, in0=in0.ap, scalar=sc, in1=in1.ap,
                                                               op0=op0, op1=op1), rd, out.keys)

    def copy(self, eng, out, in_):
        if eng == "act":
            return self.act(out, in_, AF.Copy)
        return self.op(eng, lambda h: h.tensor_copy(out=out.ap, in_=in_.ap), in_.keys, out.keys)

    def memset(self, eng, out, val):
        return self.op(eng, lambda h: h.memset(out.ap, val), [], out.keys)

    def mm(self, out, pairs):
        n = len(pairs)
        tok = None
        for i, (l, r) in enumerate(pairs):
            tok = self.op("pe", lambda h, l=l, r=r, i=i: h.matmul(out.ap, l.ap, r.ap, start=(i == 0),
                                                                    stop=(i == n - 1)),
                          l.keys + r.keys, out.keys, signal=(i == n - 1))
        return tok


class Plain:
    def __init__(self, t, name):
        self.t = t
        self.name = name

    def v(self, c, b, three=False):
        lo, n = blk(b)
        v = V(self.t[:, c, lo:lo + n], [(self.name, c, b)])
        if three and b == 2:
            v = as3(v)
        return v

    def cols(self, c, lo, n):
        keys = []
        for b in range(NBLK):
            blo, bn = blk(b)
            if lo < blo + bn and lo + n > blo:
                keys.append((self.name, c, b))
        return V(self.t[:, c, lo:lo + n], keys)


class Ext:
    def __init__(self, t, name, H, phys=None):
        self.t = t
        self.name = name
        self.H = H
        self.W = H + PT + SS * (H + ST)
        self.phys = phys

    def _c(self, c):
        return c if self.phys is None else c % self.phys

    def _regions(self, c, lo, hi):
        H = self.H
        keys = []
        if lo < H + 512 and hi > 0:
            keys.append((self.name, self._c(c), 0))
        if lo < H + PT and hi > H + 512:
            keys.append((self.name, self._c(c), 1))
        if hi > H + PT:
            keys.append((self.name, self._c(c), 2))
        return keys

    def new(self, c, b, shift=0):
        H = self.H
        cc = self._c(c)
        if b < 2:
            lo = H + 512 * b - shift
            return V(self.t[:, cc, lo:lo + 512], self._regions(c, lo, lo + 512))
        return self.samp(c, H - shift, ST)

    def samp(self, c, lo, n):
        H = self.H
        cc = self._c(c)
        ap = self.t[:, cc, H + PT:self.W].rearrange("p (s t) -> p s t", t=H + ST)[:, :, lo:lo + n]
        return V(ap, [(self.name, cc, 2)])

    def samp_flat(self, c):
        return V(self.t[:, self._c(c), self.H + PT:self.W], [(self.name, self._c(c), 2)])

    def pcols(self, c, lo, n):
        return V(self.t[:, self._c(c), lo:lo + n], self._regions(c, lo, lo + n))


class TilePool:
    def __init__(self, nc, st, name, n, width, dt):
        self.ts = [st.enter_context(nc.sbuf_tensor("%s%d" % (name, i), [128, width], dt)) for i in range(n)]
        self.name = name
        self.i = 0

    def get(self, n, three=False):
        i = self.i
        self.i = (self.i + 1) % len(self.ts)
        v = V(self.ts[i][:, 0:n], [(self.name, i)])
        if three:
            v = as3(v)
        return v


class WStream:
    def __init__(self, P, slots, plan):
        self.P = P
        self.slots = slots
        self.plan = plan
        self.issued = 0
        self.cur = 0

    def _issue(self, i):
        s = i % len(self.slots)
        off = 0
        for (ap, k, n) in self.plan[i]:
            dst = self.slots[s][:, off:off + k * n].rearrange("p (k n) -> p k n", n=n)
            self.P.dma("pool", V(dst, [("w", s)]), V(ap, []))
            off += k * n
        assert off <= SLOT_EL

    def acquire(self, tag_list):
        j0 = self.cur
        n = len(tag_list)
        assert n <= len(self.slots)
        for i, tag in enumerate(tag_list):
            assert self.plan_tags[j0 + i] == tag, (self.plan_tags[j0 + i], tag)
        self.cur += n
        while self.issued < min(len(self.plan), j0 + len(self.slots)):
            self._issue(self.issued)
            self.issued += 1
        return [Slab(self.slots[j % len(self.slots)], j % len(self.slots), self.plan[j]) for j in range(j0, j0 + n)]

    def next(self, tag):
        return self.acquire([tag])[0]


class Slab:
    def __init__(self, t, s, pieces):
        self.t = t
        self.s = s
        self.offs = []
        off = 0
        for (_, k, n) in pieces:
            self.offs.append((off, k, n))
            off += k * n

    def lhsT(self, piece, k, col0, ncol=128):
        off, K, n = self.offs[piece]
        o = off + k * n + col0
        return V(self.t[:, o:o + ncol], [("w", self.s)])


def build_nc():
    nc = bass.Bass("TRN2", target_bir_lowering=False)

    def din(name, shape):
        return nc.dram_tensor(name, list(shape), F32, kind="ExternalInput").ap()

    def dout(name, shape):
        return nc.dram_tensor(name, list(shape), F32, kind="ExternalOutput").ap()

    xT = din("xT", [NPASS, DC, 128, T])
    pTd = din("pT", [L, NPASS, 2, 128, T])
    stp = din("stp", [L, NPASS, 128, 4, SS * (HP + ST)])
    stc = din("stc", [L, NPASS, 128, 4, SS * (HC + ST)])
    stf = din("stf", [L, NPASS, 128, NJ, SS * HF])
    p8d = din("p8", [128, L * 4 * 8])
    p4d = din("p4", [128, L * 6 * 4])
    wdwd = din("wdw", [128, L * 4 * 31])
    wfdwd = din("wfdw", [128, L * NJ * 4])
    cstd = din("cst", [128, 128 + 16])
    w_in = din("w_in", [L, D, 1536])
    w_pool = din("w_pool", [L, 4, 128, 128])
    w_pw = din("w_pw", [L, DP, DP])
    w_out = din("w_out", [L, D, D])
    w_ffn_in = din("w_ffn_in", [L, D, 2 * DFF])
    w_ffn_out = din("w_ffn_out", [L, DFF, D])
    w_ple = din("w_ple", [L, DPLE, D])
    w_gate = din("w_ple_gate", [L, D, D])

    yT = dout("yT", [NPASS, 128, DC, T])
    o_pool_s = dout("o_pool_s", [L, NPASS, 128, 4, SS * (HP + ST)])
    o_pool_p = dout("o_pool_p", [L, 128, 4 * HP])
    o_conv_s = dout("o_conv_s", [L, NPASS, 128, 4 * SS * (HC + ST)])
    o_conv_p = dout("o_conv_p", [L, 128, 4 * HC])
    o_ffn_s = dout("o_ffn_s", [L, NPASS, 128, NJ * SS * HF])
    o_ffn_p = dout("o_ffn_p", [L, 128, NJ * HF])

    plan, tags = [], []

    def kp(ap):
        return ap.rearrange("(k p) n -> p k n", p=128)

    for ps_ in range(NPASS):
        for l in range(L):
            for nm, c0 in (("u1", 512), ("u2", 1024), ("a", 0)):
                plan.append([(kp(w_in[l, :, c0:c0 + 512]), 8, 512)]); tags.append((l, nm))
            plan.append([(kp(w_pw[l]), 4, 512)]); tags.append((l, "pw"))
            for h in range(2):
                plan.append([(kp(w_out[l, :, h * 512:(h + 1) * 512]), 8, 512)]); tags.append((l, "o%d" % h))
            for (j0, nj) in FFN_GROUPS:
                for j in range(j0, j0 + nj, 2):
                    plan.append([(kp(w_ffn_in[l, :, j * 128:j * 128 + 256]), 8, 256),
                                 (kp(w_ffn_in[l, :, DFF + j * 128:DFF + j * 128 + 256]), 8, 256)])
                    tags.append((l, "fi%d" % j))
                for q in range(D // FO_W):
                    plan.append([(kp(w_ffn_out[l, j0 * 128:(j0 + nj) * 128, q * FO_W:(q + 1) * FO_W]), nj, FO_W)])
                    tags.append((l, "fo%d_%d" % (j0, q)))
            plan.append([(kp(w_ple[l]), 2, 1024)]); tags.append((l, "ple"))
            for h in range(2):
                plan.append([(kp(w_gate[l, :, h * 512:(h + 1) * 512]), 8, 512)]); tags.append((l, "g%d" % h))

    with ExitStack() as st:
        P = Prog(nc, st)

        uniq = [0]

        def sb(name, shape, dt, stack=st):
            uniq[0] += 1
            return stack.enter_context(nc.sbuf_tensor("%s_%d" % (name, uniq[0]), list(shape), dt))

        xs = Plain(sb("xs", [128, DC, T], F32), "xs")
        nb = Plain(sb("nb", [128, DC, T], BF16), "nb")
        pTb = Plain(sb("pTb", [128, 2, T], BF16), "pTb")
        slots = [sb("wslot%d" % i, [128, SLOT_EL], BF16) for i in range(NSLOT)]
        W = WStream(P, slots, plan)
        W.plan_tags = tags
        p8 = sb("p8s", [128, L * 4 * 8], F32)
        p4 = sb("p4s", [128, L * 6 * 4], F32)
        wdw = sb("wdws", [128, L * 4 * 31], F32)
        wfdw = sb("wfdws", [128, L * NJ * 4], F32)
        cst = sb("csts", [128, 128 + 16], F32)
        identb = sb("identb", [128, 128], BF16)
        onesb = sb("onesb", [128, 128], BF16)
        wpool_all = sb("wpool_all", [128, L, 4 * 128], BF16)
        epsc = sb("epsc", [128, 2], F32)
        psga = sb("psga", [128, L * 4], F32)
        hp_pool = sb("hp_pool", [128, L * 4 * HP], F32)
        hp_conv = sb("hp_conv", [128, L * 4 * HC], F32)
        hp_ffn = sb("hp_ffn", [128, L * NJ * HF], F32)
        ps = st.enter_context(nc.psum_tensor("ps", [128, 8, 512], F32))
        tf = TilePool(nc, st, "tf", 6, 512, F32)
        tr = TilePool(nc, st, "tr", 4, 512, F32)
        tb = TilePool(nc, st, "tb", 8, 512, BF16)

        def bank(i, b, three=False):
            n = blk(b)[1]
            v = V(ps[:, i, 0:n], [("ps", i)])
            if three and b == 2:
                v = as3(v)
            return v

        class Ring:
            def __init__(self, banks):
                self.banks = banks
                self.i = 0

            def get(self, b, three=False):
                i = self.banks[self.i]
                self.i = (self.i + 1) % len(self.banks)
                return bank(i, b, three)

        ring6 = Ring([0, 1, 2, 3, 4, 5])
        ring8 = Ring([0, 1, 2, 3, 4, 5, 6, 7])
        statr = Ring([6, 7])

        def col(t, name, idx):
            return V(t[:, idx:idx + 1], [name])

        def g8(l, kind, c):
            return col(p8, "p8", (l * 4 + kind) * 8 + c)

        def g4(l, kind, c):
            return col(p4, "p4", (l * 6 + kind) * 4 + c)

        ones_v = V(onesb[:], ["onesb"])

        P.dma("sp", V(p8[:], ["p8"]), V(p8d, []))
        P.dma("sp", V(p4[:], ["p4"]), V(p4d, []))
        P.dma("sp", V(wdw[:], ["wdw"]), V(wdwd, []))
        P.dma("sp", V(wfdw[:], ["wfdw"]), V(wfdwd, []))
        P.dma("sp", V(cst[:], ["cst"]), V(cstd, []))
        P.copy("dve", V(identb[:], ["identb"]), V(cst[:, 0:128], ["cst"]))
        P.memset("dve", V(onesb[:], ["onesb"]), 1.0)
        for l_ in range(L):
            P.dma("pool", V(wpool_all[:, l_, :].rearrange("p (g c) -> p g c", c=128), ["wpool_all"]),
                  V(w_pool[l_].rearrange("g ci co -> ci g co"), []))
        P.memset("dve", V(epsc[:, 0:1], ["epsc"]), RMS_EPS)
        P.memset("dve", V(epsc[:, 1:2], ["epsc"]), LN_EPS)
        for l in range(L):
            P.tt("dve", V(psga[:, l * 4:(l + 1) * 4], ["psga"]),
                 V(p4[:, (l * 6 + 0) * 4:(l * 6 + 0) * 4 + 4], ["p4"]),
                 V(p4[:, (l * 6 + 4) * 4:(l * 6 + 4) * 4 + 4], ["p4"]), ALU.mult)

        def rstd_from_stat(sbank, b, inv_n, eps):
            n = blk(b)[1]
            t1 = tf.get(n)
            P.act(t1, sbank, AF.Ln, scale=inv_n, bias=col(epsc, "epsc", 0 if eps == RMS_EPS else 1))
            r = tr.get(n)
            P.act(r, t1, AF.Exp, scale=-0.5)
            return r

        def square_x(sq, xv, c):
            if c in (0, 4):
                P.tt("dve", sq, xv, xv, ALU.mult)
            else:
                P.act(sq, xv, AF.Square)

        def rmsnorm_blk(gfn, b, sbk=None, inplace=False, split=False):
            n = blk(b)[1]
            sqs = []
            for c in range(DC):
                sq = tb.get(n)
                square_x(sq, xs.v(c, b), c)
                sqs.append(sq)

            def rest(sbk=sbk):
                if sbk is None:
                    sbk = statr.get(b)
                P.mm(sbk, [(ones_v, sq) for sq in sqs])
                r = rstd_from_stat(sbk, b, 1.0 / D, RMS_EPS)
                for c in range(DC):
                    P.stt(xs.v(c, b) if inplace else nb.v(c, b), xs.v(c, b), gfn(c), r, ALU.mult, ALU.mult)

            if split:
                return rest
            rest()

        for ps_i in range(NPASS):
            for c in range(DC):
                P.dma("sp", V(xs.t[:, c, :], [("xs", c, b) for b in range(NBLK)]), V(xT[ps_i, c], []))

            for l in range(L):
                P.dma("pool", V(pTb.t[:], [("pTb", k, b) for k in range(2) for b in range(NBLK)]),
                      V(pTd[l, ps_i].rearrange("k p t -> p k t"), []))

                with ExitStack() as ms:
                    cext = Ext(sb("cext", [128, 4, HC + PT + SS * (HC + ST)], BF16, ms), "cext", HC)
                    cexs = sb("cexs", [128, 4 * SS * (HC + ST)], F32, ms)
                    cexp = sb("cexp", [128, 4 * HC], F32, ms)
                    aext = Ext(sb("aext", [128, 2, HP + PT + SS * (HP + ST)], F32, ms), "aext", HP, phys=2)
                    WA = HP + PT + SS * (HP + ST)
                    zb = Plain(sb("zb", [128, 4, T], BF16, ms), "zb")
                    cf = Plain(sb("cf", [128, 4, T], F32, ms), "cf")
                    diag = sb("diag", [128, 31, 128], BF16, ms)
                    opp = sb("opp", [128, 4 * HP], F32, ms)
                    actb = [sb("actb%d" % i, [128, 4, 512], BF16, ms) for i in range(2)]
                    tms = ExitStack()
                    tmpA = Ext(sb("tmpA", [128, 1, WA], F32, tms), "tmpA", HP)
                    tmpB = Ext(sb("tmpB", [128, 1, WA], F32, tms), "tmpB", HP)
                    P.alias_fence(["cext", "cexs", "cexp", "aext", "tmpA", "tmpB", "zb", "cf", "diag", "opp", "actb"])

                    if l == 0:
                        for b in range(NBLK):
                            rmsnorm_blk(lambda c: g8(0, 0, c), b)

                    CW = SS * (HC + ST)
                    P.dma("sp", V(cexs[:], ["cexs"]), V(stc[l, ps_i].rearrange("p c w -> p (c w)"), []))
                    for c in range(4):
                        if ps_i == 0:
                            P.memset("dve", cext.pcols(c, 0, HC), 0.0)
                        else:
                            P.copy("dve", cext.pcols(c, 0, HC),
                                   V(hp_conv[:, (l * 4 + c) * HC:(l * 4 + c + 1) * HC], ["hp_conv"]))
                        src = cexs[:, c * CW:(c + 1) * CW].rearrange("p (s t) -> p s t", t=HC + ST)[:, :, 0:HC]
                        P.copy("dve", cext.samp(c, 0, HC), V(src, ["cexs"]))

                    su1, su2 = W.acquire([(l, "u1"), (l, "u2")])
                    for c in range(4):
                        for b in range(NBLK):
                            n = blk(b)[1]
                            three = (b == 2)
                            pg = ring6.get(b)
                            P.mm(pg, [(su2.lhsT(0, k, c * 128), nb.v(k, b)) for k in range(DC)])
                            sg = tf.get(n)
                            P.act(sg, pg, AF.Sigmoid)
                            pu = ring6.get(b)
                            P.mm(pu, [(su1.lhsT(0, k, c * 128), nb.v(k, b)) for k in range(DC)])
                            if three:
                                P.tt("dve", cext.new(c, b), as3(pu), as3(sg), ALU.mult)
                                dst = cexs[:, c * CW:(c + 1) * CW].rearrange("p (s t) -> p s t", t=HC + ST)[:, :, HC:HC + ST]
                                P.tt("dve", V(dst, ["cexs"]), as3(pu), as3(sg), ALU.mult)
                            else:
                                P.tt("dve", cext.new(c, b), pu, sg, ALU.mult)
                                if b == 1:
                                    P.tt("dve", V(cexp[:, c * HC:(c + 1) * HC], ["cexp"]),
                                         V(pu.ap[:, 512 - HC:512], pu.keys), V(sg.ap[:, 512 - HC:512], sg.keys), ALU.mult)
                    P.dma("sp", V(o_conv_s[l, ps_i], []), V(cexs[:], ["cexs"]), is_output=True)
                    if ps_i == 0:
                        P.copy("dve", V(hp_conv[:, l * 4 * HC:(l + 1) * 4 * HC], ["hp_conv"]), V(cexp[:], ["cexp"]))
                    else:
                        P.dma("sp", V(o_conv_p[l], []), V(cexp[:], ["cexp"]), is_output=True)

                    sa = W.next((l, "a"))
                    for g in range(4):
                        w = 2 << g
                        P.dma("sp", aext.samp_flat(g), V(stp[l, ps_i, :, g, :], []))
                        if ps_i == 0:
                            P.memset("dve", aext.pcols(g, 0, HP), 0.0)
                        else:
                            P.copy("dve", aext.pcols(g, 0, HP),
                                   V(hp_pool[:, (l * 4 + g) * HP:(l * 4 + g + 1) * HP], ["hp_pool"]))
                        for b in range(NBLK):
                            pa = ring6.get(b)
                            P.mm(pa, [(sa.lhsT(0, k, g * 128), nb.v(k, b)) for k in range(DC)])
                            P.copy("act", aext.new(g, b), as3(pa) if b == 2 else pa)
                        P.dma("sp", V(o_pool_s[l, ps_i, :, g, :], []), aext.samp_flat(g), is_output=True)
                        tailv = aext.pcols(g, HP + PT - HP, HP)
                        if ps_i == 0:
                            P.copy("dve", V(hp_pool[:, (l * 4 + g) * HP:(l * 4 + g + 1) * HP], ["hp_pool"]), tailv)
                        else:
                            P.copy("dve", V(opp[:, g * HP:(g + 1) * HP], ["opp"]), tailv)
                        src = aext
                        srcc = g
                        bufs = [tmpA, tmpB]
                        for j in range(g + 1):
                            d = 1 << j
                            lo = 2 * d - 1
                            dst = bufs[j % 2]
                            npc = HP + PT - lo
                            P.tt(E_MIX, dst.pcols(0, lo, npc), src.pcols(srcc, lo, npc), src.pcols(srcc, lo - d, npc), ALU.add)
                            nsc = HP + ST - lo
                            P.tt(E_MIX, dst.samp(0, lo, nsc), src.samp(srcc, lo, nsc), src.samp(srcc, lo - d, nsc), ALU.add)
                            src, srcc = dst, 0
                        inv = 1.0 / w
                        P.stt(zb.cols(g, 0, PT), src.pcols(srcc, HP, PT), inv, aext.pcols(g, HP, PT), ALU.mult, ALU.subtract)
                        P.stt(zb.v(g, 2, three=True), src.samp(srcc, HP, ST), inv, aext.samp(g, HP, ST), ALU.mult, ALU.subtract)
                        if ps_i == 0:
                            tfx = tf.get(w - 1)
                            P.tt("dve", tfx, src.pcols(srcc, HP, w - 1), V(cst[:, 128:128 + w - 1], ["cst"]), ALU.mult)
                            P.tt("dve", zb.cols(g, 0, w - 1), tfx, aext.pcols(g, HP, w - 1), ALU.subtract)
                    if ps_i == 1:
                        P.dma("sp", V(o_pool_p[l], []), V(opp[:], ["opp"]), is_output=True)

                    tms.close()
                    diag2 = sb("diag2", [128, 31, 128], BF16, ms)
                    P.alias_fence(["diag2"])
                    dbufs = [(diag, "diag"), (diag2, "diag2")]
                    s1b = [bank(2, 0), bank(3, 1), bank(6, 2)]
                    s2b = [bank(4, 0), bank(5, 1), bank(7, 2)]
                    cring = Ring([0, 1])
                    idv = V(identb[:], ["identb"])
                    pend = []

                    def flush_stats():
                        while pend:
                            b_, c_, cb_, sq_ = pend.pop(0)
                            P.op("pe", lambda h, o=s1b[b_], r=cb_, c=c_: h.matmul(o.ap, onesb[:], r.ap, start=(c == 0), stop=(c == 3)),
                                 ["onesb"] + cb_.keys, s1b[b_].keys, signal=True)
                            P.op("pe", lambda h, o=s2b[b_], r=sq_, c=c_: h.matmul(o.ap, onesb[:], r.ap, start=(c == 0), stop=(c == 3)),
                                 ["onesb"] + sq_.keys, s2b[b_].keys, signal=True)

                    for c in range(4):
                        for k in range(31):
                            wv = col(wdw, "wdw", (l * 4 + c) * 31 + k)
                            dt_, dn_ = dbufs[c % 2]
                            dv = V(dt_[:, k, :], [(dn_, k)])
                            if E_DIAG == "pool":
                                P.ts("pool", dv, idv, wv, ALU.mult, 0.0, ALU.add)
                            else:
                                P.act(dv, idv, AF.Copy, scale=wv)
                        for b in range(NBLK):
                            n = blk(b)[1]
                            pc = cring.get(b, three=True)
                            P.mm(pc, [(V(dbufs[c % 2][0][:, k, :], [(dbufs[c % 2][1], k)]), cext.new(c, b, shift=HC - k))
                                      for k in range(31)])
                            flush_stats()
                            pcf = V(ps[:, pc.keys[0][1], 0:n], pc.keys)
                            bia = g4(l, 1, c)
                            P.act(cf.v(c, b), pcf, AF.Identity, bias=bia)
                            sq = tb.get(n)
                            P.act(sq, pcf, AF.Square, bias=bia)
                            cb = tb.get(n)
                            P.copy("pool", cb, cf.v(c, b))
                            pend.append((b, c, cb, sq))
                    flush_stats()

                    spw, so0, so1 = W.acquire([(l, "pw"), (l, "o0"), (l, "o1")])
                    oring = Ring([4, 5, 0, 1])

                    def st_E(b):
                        n = blk(b)[1]
                        mean = tr.get(n)
                        P.ts("dve", mean, s1b[b], 1.0 / DP, ALU.mult)
                        msq = tf.get(n)
                        P.tt("dve", msq, mean, mean, ALU.mult)
                        v2 = tf.get(n)
                        P.stt(v2, s2b[b], 1.0 / DP, msq, ALU.mult, ALU.subtract)
                        sd = tf.get(n)
                        P.act(sd, v2, AF.Ln, bias=col(epsc, "epsc", 1))
                        rs = tr.get(n)
                        P.act(rs, sd, AF.Exp, scale=-0.5)
                        ab = actb[b % 2]
                        for c in range(4):
                            P.tt("dve", cf.v(c, b), cf.v(c, b), mean, ALU.subtract)
                            P.tt("dve", cf.v(c, b), cf.v(c, b), rs, ALU.mult)
                            vv = tf.get(n)
                            P.act(vv, cf.v(c, b), AF.Identity, scale=g4(l, 2, c), bias=g4(l, 3, c))
                            sg = tf.get(n)
                            P.act(sg, cf.v(c, b), AF.Sigmoid, scale=g4(l, 2, c), bias=g4(l, 3, c))
                            P.tt("pool", V(ab[:, c, 0:n], [("actb", b % 2, c)]), vv, sg, ALU.mult)

                    def st_A(b):
                        n = blk(b)[1]
                        sqs = []
                        for g in range(4):
                            pb = cring.get(b)
                            P.mm(pb, [(V(wpool_all[:, l, g * 128:(g + 1) * 128], ["wpool_all"]), zb.v(g, b))])
                            sq = tb.get(n)
                            P.act(sq, pb, AF.Square, scale=g4(l, 0, g))
                            sqs.append(sq)
                            P.act(nb.v(g, b), pb, AF.Identity, scale=col(psga, "psga", l * 4 + g))
                        sbk = s2b[b]
                        P.mm(sbk, [(ones_v, sq) for sq in sqs])
                        r = rstd_from_stat(sbk, b, 1.0 / DP, RMS_EPS)
                        for g in range(4):
                            P.tt("pool" if g % 2 else "dve", nb.v(g, b), nb.v(g, b), r, ALU.mult)

                    def st_Pw(b):
                        n = blk(b)[1]
                        ab = actb[b % 2]
                        sqs = []
                        for m in range(4):
                            pb = cring.get(b)
                            P.mm(pb, [(spw.lhsT(0, k, m * 128), V(ab[:, k, 0:n], [("actb", b % 2, k)])) for k in range(4)])
                            sq = tb.get(n)
                            P.act(sq, pb, AF.Square)
                            sqs.append(sq)
                            P.act(nb.v(4 + m, b), pb, AF.Identity, scale=g4(l, 5, m))
                        sbk = s1b[b]
                        P.mm(sbk, [(ones_v, sq) for sq in sqs])
                        r = rstd_from_stat(sbk, b, 1.0 / DP, RMS_EPS)
                        for m in range(4):
                            P.tt("pool" if m % 2 else "dve", nb.v(4 + m, b), nb.v(4 + m, b), r, ALU.mult)

                    def st_O(b):
                        for m in range(DC):
                            so = so0 if m < 4 else so1
                            pb = oring.get(b)
                            P.mm(pb, [(so.lhsT(0, k, (m % 4) * 128), nb.v(k, b)) for k in range(DC)])
                            P.tt("dve", xs.v(m, b), pb, xs.v(m, b), ALU.add)

                    def st_N2(b):
                        rmsnorm_blk(lambda c: g8(l, 1, c), b, sbk=bank(2, b))

                    st_E(0); st_A(0); st_Pw(0); st_E(1); st_O(0); st_N2(0); st_A(1); st_Pw(1); st_E(2)
                    st_O(1); st_N2(1); st_A(2); st_Pw(2); st_O(2); st_N2(2)

                with ExitStack() as fs:
                    hb = Plain(sb("hb", [128, 12, T], BF16, fs), "hb")
                    gext = Ext(sb("gext", [128, 2, HF + PT + SS * (HF + ST)], F32, fs), "gext", HF, phys=2)
                    stfs = sb("stfs", [128, NJ * SS * HF], F32, fs)
                    ofs = sb("ofs", [128, NJ * SS * HF], F32, fs)
                    ofp = sb("ofp", [128, NJ * HF], F32, fs)
                    P.alias_fence(["hb", "gext", "stfs", "ofs", "ofp"])

                    P.dma("sp", V(stfs[:], ["stfs"]), V(stf[l, ps_i].rearrange("p j w -> p (j w)"), []))

                    for (j0, nj) in FFN_GROUPS:
                        for j in range(j0, j0 + nj, 2):
                            sl = W.next((l, "fi%d" % j))
                            for jj in range(2):
                                jc = j + jj
                                jl = jc - j0
                                wc = lambda t_: col(wfdw, "wfdw", (l * NJ + jc) * 4 + t_)
                                if ps_i == 0:
                                    P.memset("dve", gext.pcols(jc, 0, HF), 0.0)
                                else:
                                    P.copy("dve", gext.pcols(jc, 0, HF),
                                           V(hp_ffn[:, (l * NJ + jc) * HF:(l * NJ + jc + 1) * HF], ["hp_ffn"]))
                                hsrc = stfs[:, jc * SS * HF:(jc + 1) * SS * HF].rearrange("p (s t) -> p s t", t=HF)
                                P.copy("dve", gext.samp(jc, 0, HF), V(hsrc, ["stfs"]))
                                for b in range(NBLK):
                                    n = blk(b)[1]
                                    three = (b == 2)
                                    pg = ring8.get(b)
                                    P.mm(pg, [(sl.lhsT(0, k, jj * 128), nb.v(k, b)) for k in range(DC)])
                                    pu = ring8.get(b)
                                    P.mm(pu, [(sl.lhsT(1, k, jj * 128), nb.v(k, b)) for k in range(DC)])
                                    pg3 = as3(pg) if three else pg
                                    pu3 = as3(pu) if three else pu
                                    P.copy("act", gext.new(jc, b), pg3)
                                    t0 = tf.get(n, three)
                                    P.act(t0, pg3, AF.Identity, scale=wc(2), bias=wc(3))
                                    t1 = tf.get(n, three)
                                    P.stt(t1, gext.new(jc, b, shift=1), wc(1), t0, ALU.mult, ALU.add)
                                    t2 = tf.get(n, three)
                                    P.stt(t2, gext.new(jc, b, shift=2), wc(0), t1, ALU.mult, ALU.add)
                                    t3 = tf.get(n, three)
                                    P.tt("dve", t3, pu3, t2, ALU.mult)
                                    sg = tf.get(n, three)
                                    P.act(sg, t2, AF.Sigmoid)
                                    P.tt("dve", hb.v(jl, b, three=three), t3, sg, ALU.mult)
                                tailv = gext.pcols(jc, HF + PT - HF, HF)
                                if ps_i == 0:
                                    P.copy("dve", V(hp_ffn[:, (l * NJ + jc) * HF:(l * NJ + jc + 1) * HF], ["hp_ffn"]), tailv)
                                else:
                                    P.copy("dve", V(ofp[:, jc * HF:(jc + 1) * HF], ["ofp"]), tailv)
                                odst = ofs[:, jc * SS * HF:(jc + 1) * SS * HF].rearrange("p (s t) -> p s t", t=HF)
                                P.copy("dve", V(odst, ["ofs"]), gext.samp(jc, HF + ST - HF, HF))
                        for q in range(D // FO_W):
                            so = W.next((l, "fo%d_%d" % (j0, q)))
                            for mloc in range(FO_W // 128):
                                m = q * (FO_W // 128) + mloc
                                for b in range(NBLK):
                                    pb = ring6.get(b)
                                    P.mm(pb, [(so.lhsT(0, jl, mloc * 128), hb.v(jl, b)) for jl in range(nj)])
                                    P.tt("dve", xs.v(m, b), pb, xs.v(m, b), ALU.add)
                    P.dma("sp", V(o_ffn_s[l, ps_i], []), V(ofs[:], ["ofs"]), is_output=True)
                    if ps_i == 1:
                        P.dma("sp", V(o_ffn_p[l], []), V(ofp[:], ["ofp"]), is_output=True)

                with ExitStack() as es:
                    ef = sb("ef", [128, DC, 512], F32, es)
                    P.alias_fence(["ef"])
                    for c in range(DC):
                        for b in range(NBLK):
                            P.copy("act", nb.v(c, b), xs.v(c, b))
                    sple, sg0_, sg1_ = W.acquire([(l, "ple"), (l, "g0"), (l, "g1")])
                    sgs = [sg0_, sg1_]
                    pend_norm = []
                    for b in range(NBLK):
                        n = blk(b)[1]
                        for m in range(DC):
                            pb = ring6.get(b)
                            P.mm(pb, [(sple.lhsT(0, k, m * 128), pTb.v(k, b)) for k in range(2)])
                            P.copy("act", V(ef[:, m, 0:n], [("ef", m)]), pb)
                        while pend_norm:
                            pend_norm.pop(0)()
                        sqs = []
                        for m in range(DC):
                            sq = tb.get(n)
                            square_x(sq, V(ef[:, m, 0:n], [("ef", m)]), m)
                            sqs.append(sq)
                        r = None
                        for m in range(DC):
                            pb = ring6.get(b)
                            P.mm(pb, [(sgs[m // 4].lhsT(0, k, (m % 4) * 128), nb.v(k, b)) for k in range(DC)])
                            if m == 0:
                                sbk = statr.get(b)
                                P.mm(sbk, [(ones_v, sq) for sq in sqs])
                                r = rstd_from_stat(sbk, b, 1.0 / D, RMS_EPS)
                                for m2 in range(DC):
                                    efv = V(ef[:, m2, 0:n], [("ef", m2)])
                                    P.tt("pool", efv, efv, r, ALU.mult)
                            sg = tf.get(n)
                            P.act(sg, pb, AF.Sigmoid)
                            t1 = tf.get(n)
                            P.stt(t1, V(ef[:, m, 0:n], [("ef", m)]), g8(l, 2, m), sg, ALU.mult, ALU.mult)
                            P.tt("dve", xs.v(m, b), xs.v(m, b), t1, ALU.add)
                        if l < L - 1:
                            pend_norm.append(rmsnorm_blk(lambda c, l=l: g8(l + 1, 0, c), b, split=True))
                        else:
                            pend_norm.append(rmsnorm_blk(lambda c: g8(0, 3, c), b, inplace=True, split=True))
                    while pend_norm:
                        pend_norm.pop(0)()

            P.dma("sp", V(yT[ps_i], []), V(xs.t[:], [("xs", c, b) for c in range(DC) for b in range(NBLK)]),
                  is_output=True)

        P.finish()
        build_nc.ninst = dict(P.ninst)
    return nc


def _fmaj(a, nchunk):
    s = a.shape
    a = a.reshape(s[:-1] + (nchunk, 128))
    nd = a.ndim
    perm = list(range(nd - 3)) + [nd - 1, nd - 2, nd - 3]
    return np.ascontiguousarray(a.transpose(perm))


_NC_CACHE = {}


def kernel(x_prompt, x_sample, state_pool, state_conv, state_ffn, p_prompt, p_sample,
           g_mix, w_in, w_pool, pool_scale, w_dw, b_dw, ln_g, ln_b, w_pw,
           g_out_a, g_out_b, w_out, g_ffn, w_ffn_in, w_ffn_dw, b_ffn_dw, w_ffn_out,
           w_ple, g_ple, w_ple_gate, final_norm):
    f = lambda a: np.ascontiguousarray(np.asarray(a, dtype=np.float32))
    x_prompt, x_sample, state_pool, state_conv, state_ffn, p_prompt, p_sample = map(
        f, (x_prompt, x_sample, state_pool, state_conv, state_ffn, p_prompt, p_sample))
    ncores = 8
    if "nc" not in _NC_CACHE:
        _NC_CACHE["nc"] = build_nc()
    nc = _NC_CACHE["nc"]

    p8 = np.zeros((L, 4, 8, 128), np.float32)
    p8[:, 0] = f(g_mix).reshape(L, 8, 128)
    p8[:, 1] = f(g_ffn).reshape(L, 8, 128)
    p8[:, 2] = f(g_ple).reshape(L, 8, 128)
    p8[:, 3] = f(final_norm).reshape(1, 8, 128)
    p8 = np.ascontiguousarray(p8.transpose(3, 0, 1, 2).reshape(128, L * 4 * 8))
    p4 = np.stack([f(pool_scale), f(b_dw), f(ln_g), f(ln_b), f(g_out_a), f(g_out_b)], 1).reshape(L, 6, 4, 128)
    p4 = np.ascontiguousarray(p4.transpose(3, 0, 1, 2).reshape(128, L * 6 * 4))
    wdw = f(w_dw).reshape(L, 31, 4, 128).transpose(3, 0, 2, 1)
    wdw = np.ascontiguousarray(wdw.reshape(128, L * 4 * 31))
    wf = np.concatenate([f(w_ffn_dw), f(b_ffn_dw)[:, None, :]], 1).reshape(L, 4, NJ, 128).transpose(3, 0, 2, 1)
    wf = np.ascontiguousarray(wf.reshape(128, L * NJ * 4))
    cst = np.zeros((128, 144), np.float32)
    cst[:, :128] = np.eye(128, dtype=np.float32)
    cst[:, 128:144] = (1.0 / np.arange(1, 17, dtype=np.float32))[None, :]
    shared = {"p8": p8, "p4": p4, "wdw": wdw, "wfdw": wf, "cst": cst,
              "w_in": f(w_in), "w_pool": f(w_pool), "w_pw": f(w_pw), "w_out": f(w_out),
              "w_ffn_in": f(w_ffn_in), "w_ffn_out": f(w_ffn_out), "w_ple": f(w_ple),
              "w_ple_gate": f(w_ple_gate)}

    in_maps = []
    for c in range(ncores):
        xT = np.zeros((NPASS, DC, 128, T), np.float32)
        pT = np.zeros((L, NPASS, 2, 128, T), np.float32)
        stp = np.zeros((L, NPASS, 128, 4, SS, HP + ST), np.float32)
        stc = np.zeros((L, NPASS, 128, 4, SS, HC + ST), np.float32)
        stf = np.zeros((L, NPASS, 128, NJ, SS, HF), np.float32)
        for h in range(NPASS):
            sq = slice(16 * c + SS * h, 16 * c + SS * (h + 1))
            xp = x_prompt[c, h * PT:(h + 1) * PT]
            xsm = x_sample[sq].reshape(SS * ST, D)
            xT[h] = np.concatenate([xp, xsm], 0).T.reshape(DC, 128, T)
            for l in range(L):
                pp = np.concatenate([p_prompt[l, c, h * PT:(h + 1) * PT], p_sample[l, sq].reshape(SS * ST, DPLE)], 0)
                pT[l, h] = pp.T.reshape(2, 128, T)
                stp[l, h, :, :, :, :HP] = _fmaj(state_pool[l, sq].reshape(SS * HP, DP), 4).reshape(128, 4, SS, HP)
                stc[l, h, :, :, :, :HC] = _fmaj(state_conv[l, sq].reshape(SS * HC, DP), 4).reshape(128, 4, SS, HC)
                stf[l, h] = _fmaj(state_ffn[l, sq].reshape(SS * HF, DFF), NJ).reshape(128, NJ, SS, HF)
        m = dict(shared)
        m.update({"xT": xT, "pT": pT,
                  "stp": stp.reshape(L, NPASS, 128, 4, SS * (HP + ST)),
                  "stc": stc.reshape(L, NPASS, 128, 4, SS * (HC + ST)),
                  "stf": stf.reshape(L, NPASS, 128, NJ, SS * HF)})
        in_maps.append(m)

    res = run_bass_kernel_spmd(nc, in_maps, core_ids=list(range(ncores)))

    B, S = 8, 2048
    y_prompt = np.zeros((B, S, D), np.float32)
    y_sample = np.zeros((128, ST, D), np.float32)
    pool_p = np.zeros((L, B, HP, DP), np.float32)
    conv_p = np.zeros((L, B, HC, DP), np.float32)
    ffn_p = np.zeros((L, B, HF, DFF), np.float32)
    pool_s = np.zeros((L, 128, HP, DP), np.float32)
    conv_s = np.zeros((L, 128, HC, DP), np.float32)
    ffn_s = np.zeros((L, 128, HF, DFF), np.float32)
    for c in range(ncores):
        r = res.results[c]
        yT = np.asarray(r["yT"]).reshape(NPASS, 128, DC, T)
        for h in range(NPASS):
            sq = slice(16 * c + SS * h, 16 * c + SS * (h + 1))
            yy = yT[h].transpose(2, 1, 0).reshape(T, D)
            y_prompt[c, h * PT:(h + 1) * PT] = yy[:PT]
            y_sample[sq] = yy[PT:].reshape(SS, ST, D)
            ops_ = np.asarray(r["o_pool_s"]).reshape(L, NPASS, 128, 4, SS, HP + ST)[:, h]
            pool_s[:, sq] = ops_[..., ST:].transpose(0, 3, 4, 2, 1).reshape(L, SS, HP, DP)
            ocs = np.asarray(r["o_conv_s"]).reshape(L, NPASS, 128, 4, SS, HC + ST)[:, h]
            conv_s[:, sq] = ocs[..., ST:].transpose(0, 3, 4, 2, 1).reshape(L, SS, HC, DP)
            ofs = np.asarray(r["o_ffn_s"]).reshape(L, NPASS, 128, NJ, SS, HF)[:, h]
            ffn_s[:, sq] = ofs.transpose(0, 3, 4, 2, 1).reshape(L, SS, HF, DFF)
        pool_p[:, c] = np.asarray(r["o_pool_p"]).reshape(L, 128, 4, HP).transpose(0, 3, 2, 1).reshape(L, HP, DP)
        conv_p[:, c] = np.asarray(r["o_conv_p"]).reshape(L, 128, 4, HC).transpose(0, 3, 2, 1).reshape(L, HC, DP)
        ffn_p[:, c] = np.asarray(r["o_ffn_p"]).reshape(L, 128, NJ, HF).transpose(0, 3, 2, 1).reshape(L, HF, DFF)
    return (y_prompt, y_sample, pool_p, conv_p, ffn_p, pool_s, conv_s, ffn_s)
```

```python
import numpy as np
from contextlib import ExitStack
import concourse.bass as bass
import concourse.mybir as mybir
from concourse.bass_utils import run_bass_kernel_spmd

F32 = mybir.dt.float32
BF16 = mybir.dt.bfloat16
AF = mybir.ActivationFunctionType
ALU = mybir.AluOpType

L = 4
D = 1024
DC = 8
DP = 512
DFF = 2816
NJ = 22
DPLE = 256
PT = 1024
SS = 8
ST = 8
T = PT + SS * ST
NPASS = 2
NBLK = 3
HP, HC, HF = 15, 30, 2
RMS_EPS = 1e-6
LN_EPS = 1e-5
NSLOT = 4
SLOT_EL = 4096
FFN_GROUPS = [(0, 12), (12, 10)]
FO_W = 256

DEBUG = False
E_MIX = "dve"
E_DIAG = "pool"


def blk(b):
    return (0, 512) if b == 0 else ((512, 512) if b == 1 else (1024, 64))


class V:
    __slots__ = ("ap", "keys")

    def __init__(self, ap, keys):
        self.ap = ap
        self.keys = list(keys)


def as3(v):
    return V(v.ap.rearrange("p (s t) -> p s t", t=ST), v.keys)


class _Eng:
    def __init__(self, name, h, sem):
        self.name = name
        self.h = h
        self.sem = sem
        self.count = 0
        self.waited = {}


class Prog:
    def __init__(self, nc, stack, n_dma_sems=32):
        self.nc = nc
        self.e = {}
        for name, h in (("pe", nc.tensor), ("act", nc.scalar), ("dve", nc.vector),
                        ("pool", nc.gpsimd), ("sp", nc.sync)):
            sem = stack.enter_context(nc.semaphore("sem_" + name))
            self.e[name] = _Eng(name, h, sem)
        self.dsem = [stack.enter_context(nc.semaphore("dsem%d" % i)) for i in range(n_dma_sems)]
        self.dcnt = [0] * n_dma_sems
        self.dnext = {"sp": 0, "pool": n_dma_sems // 2}
        self.drange = {"sp": (0, n_dma_sems // 2), "pool": (n_dma_sems // 2, n_dma_sems)}
        self.res = {}
        self.fences = {}
        self.out_toks = []
        self.ninst = {k: 0 for k in self.e}

    def _need(self, e, tok):
        sem, val = tok
        if e.waited.get(sem.num, 0) >= val:
            return
        e.h.wait_ge(sem, val)
        e.waited[sem.num] = val

    def alias_fence(self, names):
        f = [(o.sem, o.count) for o in self.e.values() if o.count > 0]
        f += [(self.dsem[k], self.dcnt[k]) for k in range(len(self.dsem)) if self.dcnt[k] > 0]
        for n_ in names:
            self.fences[n_] = f
        for k in [k for k in self.res if (k[0] if isinstance(k, tuple) else k) in set(names)]:
            del self.res[k]

    def _deps(self, e, reads, writes):
        toks = []
        for k in list(reads) + list(writes):
            if k not in self.res:
                f = self.fences.get(k[0] if isinstance(k, tuple) else k)
                if f:
                    toks.extend(f)
        for k in reads:
            r = self.res.get(k)
            if r is not None and r[0] is not None:
                toks.append(r[0])
        for k in writes:
            r = self.res.get(k)
            if r is not None:
                if r[0] is not None:
                    toks.append(r[0])
                toks.extend(r[1].values())
        for t in toks:
            if t[0] is e.sem and e.name == "pe":
                continue
            self._need(e, t)

    def _record(self, rkey, tok, reads, writes):
        for k in reads:
            r = self.res.get(k)
            if r is None:
                r = [None, {}]
                self.res[k] = r
            r[1][rkey] = tok
        for k in writes:
            self.res[k] = [tok, {}]

    def op(self, ename, fn, reads, writes, signal=True):
        e = self.e[ename]
        self._deps(e, reads, writes)
        inst = fn(e.h)
        self.ninst[ename] += 1
        if signal:
            e.count += 1
            inst.then_inc(e.sem, 1)
            tok = (e.sem, e.count)
        else:
            tok = (e.sem, e.count + 1)
        self._record(ename, tok, reads, writes)
        return tok

    def dma(self, qname, out, in_, is_output=False):
        e = self.e[qname]
        reads = in_.keys
        writes = out.keys
        self._deps(e, reads, writes)
        k = self.dnext[qname]
        lo, hi = self.drange[qname]
        self.dnext[qname] = lo + (k + 1 - lo) % (hi - lo)
        sem = self.dsem[k]
        if self.dcnt[k] > 0:
            self._need(e, (sem, self.dcnt[k]))
        e.h.dma_start(out=out.ap, in_=in_.ap).then_inc(sem, 16)
        self.ninst[qname] += 1
        self.dcnt[k] += 16
        tok = (sem, self.dcnt[k])
        self._record(("dma", k), tok, reads, writes)
        if is_output:
            self.out_toks.append(tok)
        return tok

    def barrier(self):
        sp = self.e["sp"]
        for k in range(len(self.dsem)):
            if self.dcnt[k] > 0:
                self._need(sp, (self.dsem[k], self.dcnt[k]))
        for name in ("pe", "act", "dve", "pool"):
            o = self.e[name]
            if o.count > 0:
                self._need(sp, (o.sem, o.count))
        sp.count += 1
        sp.h.nop().then_inc(sp.sem, 1)
        for name in ("pe", "act", "dve", "pool"):
            self._need(self.e[name], (sp.sem, sp.count))
        self.res.clear()

    def finish(self):
        self.barrier()

    @staticmethod
    def _sk(x):
        return x.keys if isinstance(x, V) else []

    @staticmethod
    def _sa(x):
        return x.ap if isinstance(x, V) else x

    def act(self, out, in_, func, scale=1.0, bias=0.0):
        rd = in_.keys + self._sk(scale) + self._sk(bias)
        sc, bi = self._sa(scale), self._sa(bias)
        return self.op("act", lambda h: h.activation(out=out.ap, in_=in_.ap, func=func, scale=sc, bias=bi),
                       rd, out.keys)

    def tt(self, eng, out, in0, in1, op):
        return self.op(eng, lambda h: h.tensor_tensor(out=out.ap, in0=in0.ap, in1=in1.ap, op=op),
                       in0.keys + in1.keys, out.keys)

    def ts(self, eng, out, in0, s1, op0, s2=None, op1=None):
        rd = in0.keys + self._sk(s1) + self._sk(s2)
        a1, a2 = self._sa(s1), self._sa(s2)
        if op1 is None:
            return self.op(eng, lambda h: h.tensor_scalar(out=out.ap, in0=in0.ap, scalar1=a1, scalar2=None, op0=op0),
                           rd, out.keys)
        return self.op(eng, lambda h: h.tensor_scalar(out=out.ap, in0=in0.ap, scalar1=a1, scalar2=a2,
                                                      op0=op0, op1=op1), rd, out.keys)

    def stt(self, out, in0, scalar, in1, op0, op1):
        rd = in0.keys + in1.keys + self._sk(scalar)
        sc = self._sa(scalar)
        return self.op("dve", lambda h: h.scalar_tensor_tensor(out=out.ap, in0=in0.ap, scalar=sc, in1=in1.ap,
                                                               op0=op0, op1=op1), rd, out.keys)

    def copy(self, eng, out, in_):
        if eng == "act":
            return self.act(out, in_, AF.Copy)
        return self.op(eng, lambda h: h.tensor_copy(out=out.ap, in_=in_.ap), in_.keys, out.keys)

    def memset(self, eng, out, val):
        return self.op(eng, lambda h: h.memset(out.ap, val), [], out.keys)

    def mm(self, out, pairs):
        n = len(pairs)
        tok = None
        for i, (l, r) in enumerate(pairs):
            tok = self.op("pe", lambda h, l=l, r=r, i=i: h.matmul(out.ap, l.ap, r.ap, start=(i == 0),
                                                                    stop=(i == n - 1)),
                          l.keys + r.keys, out.keys, signal=(i == n - 1))
        return tok


class Plain:
    def __init__(self, t, name):
        self.t = t
        self.name = name

    def v(self, c, b, three=False):
        lo, n = blk(b)
        v = V(self.t[:, c, lo:lo + n], [(self.name, c, b)])
        if three and b == 2:
            v = as3(v)
        return v

    def cols(self, c, lo, n):
        keys = []
        for b in range(NBLK):
            blo, bn = blk(b)
            if lo < blo + bn and lo + n > blo:
                keys.append((self.name, c, b))
        return V(self.t[:, c, lo:lo + n], keys)


class Ext:
    def __init__(self, t, name, H, phys=None):
        self.t = t
        self.name = name
        self.H = H
        self.W = H + PT + SS * (H + ST)
        self.phys = phys

    def _c(self, c):
        return c if self.phys is None else c % self.phys

    def _regions(self, c, lo, hi):
        H = self.H
        keys = []
        if lo < H + 512 and hi > 0:
            keys.append((self.name, self._c(c), 0))
        if lo < H + PT and hi > H + 512:
            keys.append((self.name, self._c(c), 1))
        if hi > H + PT:
            keys.append((self.name, self._c(c), 2))
        return keys

    def new(self, c, b, shift=0):
        H = self.H
        cc = self._c(c)
        if b < 2:
            lo = H + 512 * b - shift
            return V(self.t[:, cc, lo:lo + 512], self._regions(c, lo, lo + 512))
        return self.samp(c, H - shift, ST)

    def samp(self, c, lo, n):
        H = self.H
        cc = self._c(c)
        ap = self.t[:, cc, H + PT:self.W].rearrange("p (s t) -> p s t", t=H + ST)[:, :, lo:lo + n]
        return V(ap, [(self.name, cc, 2)])

    def samp_flat(self, c):
        return V(self.t[:, self._c(c), self.H + PT:self.W], [(self.name, self._c(c), 2)])

    def pcols(self, c, lo, n):
        return V(self.t[:, self._c(c), lo:lo + n], self._regions(c, lo, lo + n))


class TilePool:
    def __init__(self, nc, st, name, n, width, dt):
        self.ts = [st.enter_context(nc.sbuf_tensor("%s%d" % (name, i), [128, width], dt)) for i in range(n)]
        self.name = name
        self.i = 0

    def get(self, n, three=False):
        i = self.i
        self.i = (self.i + 1) % len(self.ts)
        v = V(self.ts[i][:, 0:n], [(self.name, i)])
        if three:
            v = as3(v)
        return v


class WStream:
    def __init__(self, P, slots, plan):
        self.P = P
        self.slots = slots
        self.plan = plan
        self.issued = 0
        self.cur = 0

    def _issue(self, i):
        s = i % len(self.slots)
        off = 0
        for (ap, k, n) in self.plan[i]:
            dst = self.slots[s][:, off:off + k * n].rearrange("p (k n) -> p k n", n=n)
            self.P.dma("pool", V(dst, [("w", s)]), V(ap, []))
            off += k * n
        assert off <= SLOT_EL

    def acquire(self, tag_list):
        j0 = self.cur
        n = len(tag_list)
        assert n <= len(self.slots)
        for i, tag in enumerate(tag_list):
            assert self.plan_tags[j0 + i] == tag, (self.plan_tags[j0 + i], tag)
        self.cur += n
        while self.issued < min(len(self.plan), j0 + len(self.slots)):
            self._issue(self.issued)
            self.issued += 1
        return [Slab(self.slots[j % len(self.slots)], j % len(self.slots), self.plan[j]) for j in range(j0, j0 + n)]

    def next(self, tag):
        return self.acquire([tag])[0]


class Slab:
    def __init__(self, t, s, pieces):
        self.t = t
        self.s = s
        self.offs = []
        off = 0
        for (_, k, n) in pieces:
            self.offs.append((off, k, n))
            off += k * n

    def lhsT(self, piece, k, col0, ncol=128):
        off, K, n = self.offs[piece]
        o = off + k * n + col0
        return V(self.t[:, o:o + ncol], [("w", self.s)])


def build_nc():
    nc = bass.Bass("TRN2", target_bir_lowering=False)

    def din(name, shape):
        return nc.dram_tensor(name, list(shape), F32, kind="ExternalInput").ap()

    def dout(name, shape):
        return nc.dram_tensor(name, list(shape), F32, kind="ExternalOutput").ap()

    xT = din("xT", [NPASS, DC, 128, T])
    pTd = din("pT", [L, NPASS, 2, 128, T])
    stp = din("stp", [L, NPASS, 128, 4, SS * (HP + ST)])
    stc = din("stc", [L, NPASS, 128, 4, SS * (HC + ST)])
    stf = din("stf", [L, NPASS, 128, NJ, SS * HF])
    p8d = din("p8", [128, L * 4 * 8])
    p4d = din("p4", [128, L * 6 * 4])
    wdwd = din("wdw", [128, L * 4 * 31])
    wfdwd = din("wfdw", [128, L * NJ * 4])
    cstd = din("cst", [128, 128 + 16])
    w_in = din("w_in", [L, D, 1536])
    w_pool = din("w_pool", [L, 4, 128, 128])
    w_pw = din("w_pw", [L, DP, DP])
    w_out = din("w_out", [L, D, D])
    w_ffn_in = din("w_ffn_in", [L, D, 2 * DFF])
    w_ffn_out = din("w_ffn_out", [L, DFF, D])
    w_ple = din("w_ple", [L, DPLE, D])
    w_gate = din("w_ple_gate", [L, D, D])

    yT = dout("yT", [NPASS, 128, DC, T])
    o_pool_s = dout("o_pool_s", [L, NPASS, 128, 4, SS * (HP + ST)])
    o_pool_p = dout("o_pool_p", [L, 128, 4 * HP])
    o_conv_s = dout("o_conv_s", [L, NPASS, 128, 4 * SS * (HC + ST)])
    o_conv_p = dout("o_conv_p", [L, 128, 4 * HC])
    o_ffn_s = dout("o_ffn_s", [L, NPASS, 128, NJ * SS * HF])
    o_ffn_p = dout("o_ffn_p", [L, 128, NJ * HF])

    plan, tags = [], []

    def kp(ap):
        return ap.rearrange("(k p) n -> p k n", p=128)

    for ps_ in range(NPASS):
        for l in range(L):
            for nm, c0 in (("u1", 512), ("u2", 1024), ("a", 0)):
                plan.append([(kp(w_in[l, :, c0:c0 + 512]), 8, 512)]); tags.append((l, nm))
            plan.append([(kp(w_pw[l]), 4, 512)]); tags.append((l, "pw"))
            for h in range(2):
                plan.append([(kp(w_out[l, :, h * 512:(h + 1) * 512]), 8, 512)]); tags.append((l, "o%d" % h))
            for (j0, nj) in FFN_GROUPS:
                for j in range(j0, j0 + nj, 2):
                    plan.append([(kp(w_ffn_in[l, :, j * 128:j * 128 + 256]), 8, 256),
                                 (kp(w_ffn_in[l, :, DFF + j * 128:DFF + j * 128 + 256]), 8, 256)])
                    tags.append((l, "fi%d" % j))
                for q in range(D // FO_W):
                    plan.append([(kp(w_ffn_out[l, j0 * 128:(j0 + nj) * 128, q * FO_W:(q + 1) * FO_W]), nj, FO_W)])
                    tags.append((l, "fo%d_%d" % (j0, q)))
            plan.append([(kp(w_ple[l]), 2, 1024)]); tags.append((l, "ple"))
            for h in range(2):
                plan.append([(kp(w_gate[l, :, h * 512:(h + 1) * 512]), 8, 512)]); tags.append((l, "g%d" % h))

    with ExitStack() as st:
        P = Prog(nc, st)

        uniq = [0]

        def sb(name, shape, dt, stack=st):
            uniq[0] += 1
            return stack.enter_context(nc.sbuf_tensor("%s_%d" % (name, uniq[0]), list(shape), dt))

        xs = Plain(sb("xs", [128, DC, T], F32), "xs")
        nb = Plain(sb("nb", [128, DC, T], BF16), "nb")
        pTb = Plain(sb("pTb", [128, 2, T], BF16), "pTb")
        slots = [sb("wslot%d" % i, [128, SLOT_EL], BF16) for i in range(NSLOT)]
        W = WStream(P, slots, plan)
        W.plan_tags = tags
        p8 = sb("p8s", [128, L * 4 * 8], F32)
        p4 = sb("p4s", [128, L * 6 * 4], F32)
        wdw = sb("wdws", [128, L * 4 * 31], F32)
        wfdw = sb("wfdws", [128, L * NJ * 4], F32)
        cst = sb("csts", [128, 128 + 16], F32)
        identb = sb("identb", [128, 128], BF16)
        onesb = sb("onesb", [128, 128], BF16)
        wpool_all = sb("wpool_all", [128, L, 4 * 128], BF16)
        epsc = sb("epsc", [128, 2], F32)
        psga = sb("psga", [128, L * 4], F32)
        hp_pool = sb("hp_pool", [128, L * 4 * HP], F32)
        hp_conv = sb("hp_conv", [128, L * 4 * HC], F32)
        hp_ffn = sb("hp_ffn", [128, L * NJ * HF], F32)
        ps = st.enter_context(nc.psum_tensor("ps", [128, 8, 512], F32))
        tf = TilePool(nc, st, "tf", 6, 512, F32)
        tr = TilePool(nc, st, "tr", 4, 512, F32)
        tb = TilePool(nc, st, "tb", 8, 512, BF16)

        def bank(i, b, three=False):
            n = blk(b)[1]
            v = V(ps[:, i, 0:n], [("ps", i)])
            if three and b == 2:
                v = as3(v)
            return v

        class Ring:
            def __init__(self, banks):
                self.banks = banks
                self.i = 0

            def get(self, b, three=False):
                i = self.banks[self.i]
                self.i = (self.i + 1) % len(self.banks)
                return bank(i, b, three)

        ring6 = Ring([0, 1, 2, 3, 4, 5])
        ring8 = Ring([0, 1, 2, 3, 4, 5, 6, 7])
        statr = Ring([6, 7])

        def col(t, name, idx):
            return V(t[:, idx:idx + 1], [name])

        def g8(l, kind, c):
            return col(p8, "p8", (l * 4 + kind) * 8 + c)

        def g4(l, kind, c):
            return col(p4, "p4", (l * 6 + kind) * 4 + c)

        ones_v = V(onesb[:], ["onesb"])

        P.dma("sp", V(p8[:], ["p8"]), V(p8d, []))
        P.dma("sp", V(p4[:], ["p4"]), V(p4d, []))
        P.dma("sp", V(wdw[:], ["wdw"]), V(wdwd, []))
        P.dma("sp", V(wfdw[:], ["wfdw"]), V(wfdwd, []))
        P.dma("sp", V(cst[:], ["cst"]), V(cstd, []))
        P.copy("dve", V(identb[:], ["identb"]), V(cst[:, 0:128], ["cst"]))
        P.memset("dve", V(onesb[:], ["onesb"]), 1.0)
        for l_ in range(L):
            P.dma("pool", V(wpool_all[:, l_, :].rearrange("p (g c) -> p g c", c=128), ["wpool_all"]),
                  V(w_pool[l_].rearrange("g ci co -> ci g co"), []))
        P.memset("dve", V(epsc[:, 0:1], ["epsc"]), RMS_EPS)
        P.memset("dve", V(epsc[:, 1:2], ["epsc"]), LN_EPS)
        for l in range(L):
            P.tt("dve", V(psga[:, l * 4:(l + 1) * 4], ["psga"]),
                 V(p4[:, (l * 6 + 0) * 4:(l * 6 + 0) * 4 + 4], ["p4"]),
                 V(p4[:, (l * 6 + 4) * 4:(l * 6 + 4) * 4 + 4], ["p4"]), ALU.mult)

        def rstd_from_stat(sbank, b, inv_n, eps):
            n = blk(b)[1]
            t1 = tf.get(n)
            P.act(t1, sbank, AF.Ln, scale=inv_n, bias=col(epsc, "epsc", 0 if eps == RMS_EPS else 1))
            r = tr.get(n)
            P.act(r, t1, AF.Exp, scale=-0.5)
            return r

        def square_x(sq, xv, c):
            if c in (0, 4):
                P.tt("dve", sq, xv, xv, ALU.mult)
            else:
                P.act(sq, xv, AF.Square)

        def rmsnorm_blk(gfn, b, sbk=None, inplace=False, split=False):
            n = blk(b)[1]
            sqs = []
            for c in range(DC):
                sq = tb.get(n)
                square_x(sq, xs.v(c, b), c)
                sqs.append(sq)

            def rest(sbk=sbk):
                if sbk is None:
                    sbk = statr.get(b)
                P.mm(sbk, [(ones_v, sq) for sq in sqs])
                r = rstd_from_stat(sbk, b, 1.0 / D, RMS_EPS)
                for c in range(DC):
                    P.stt(xs.v(c, b) if inplace else nb.v(c, b), xs.v(c, b), gfn(c), r, ALU.mult, ALU.mult)

            if split:
                return rest
            rest()

        for ps_i in range(NPASS):
            for c in range(DC):
                P.dma("sp", V(xs.t[:, c, :], [("xs", c, b) for b in range(NBLK)]), V(xT[ps_i, c], []))

            for l in range(L):
                P.dma("pool", V(pTb.t[:], [("pTb", k, b) for k in range(2) for b in range(NBLK)]),
                      V(pTd[l, ps_i].rearrange("k p t -> p k t"), []))

                with ExitStack() as ms:
                    cext = Ext(sb("cext", [128, 4, HC + PT + SS * (HC + ST)], BF16, ms), "cext", HC)
                    cexs = sb("cexs", [128, 4 * SS * (HC + ST)], F32, ms)
                    cexp = sb("cexp", [128, 4 * HC], F32, ms)
                    aext = Ext(sb("aext", [128, 2, HP + PT + SS * (HP + ST)], F32, ms), "aext", HP, phys=2)
                    WA = HP + PT + SS * (HP + ST)
                    zb = Plain(sb("zb", [128, 4, T], BF16, ms), "zb")
                    cf = Plain(sb("cf", [128, 4, T], F32, ms), "cf")
                    diag = sb("diag", [128, 31, 128], BF16, ms)
                    opp = sb("opp", [128, 4 * HP], F32, ms)
                    actb = [sb("actb%d" % i, [128, 4, 512], BF16, ms) for i in range(2)]
                    tms = ExitStack()
                    tmpA = Ext(sb("tmpA", [128, 1, WA], F32, tms), "tmpA", HP)
                    tmpB = Ext(sb("tmpB", [128, 1, WA], F32, tms), "tmpB", HP)
                    P.alias_fence(["cext", "cexs", "cexp", "aext", "tmpA", "tmpB", "zb", "cf", "diag", "opp", "actb"])

                    if l == 0:
                        for b in range(NBLK):
                            rmsnorm_blk(lambda c: g8(0, 0, c), b)

                    CW = SS * (HC + ST)
                    P.dma("sp", V(cexs[:], ["cexs"]), V(stc[l, ps_i].rearrange("p c w -> p (c w)"), []))
                    for c in range(4):
                        if ps_i == 0:
                            P.memset("dve", cext.pcols(c, 0, HC), 0.0)
                        else:
                            P.copy("dve", cext.pcols(c, 0, HC),
                                   V(hp_conv[:, (l * 4 + c) * HC:(l * 4 + c + 1) * HC], ["hp_conv"]))
                        src = cexs[:, c * CW:(c + 1) * CW].rearrange("p (s t) -> p s t", t=HC + ST)[:, :, 0:HC]
                        P.copy("dve", cext.samp(c, 0, HC), V(src, ["cexs"]))

                    su1, su2 = W.acquire([(l, "u1"), (l, "u2")])
                    for c in range(4):
                        for b in range(NBLK):
                            n = blk(b)[1]
                            three = (b == 2)
                            pg = ring6.get(b)
                            P.mm(pg, [(su2.lhsT(0, k, c * 128), nb.v(k, b)) for k in range(DC)])
                            sg = tf.get(n)
                            P.act(sg, pg, AF.Sigmoid)
                            pu = ring6.get(b)
                            P.mm(pu, [(su1.lhsT(0, k, c * 128), nb.v(k, b)) for k in range(DC)])
                            if three:
                                P.tt("dve", cext.new(c, b), as3(pu), as3(sg), ALU.mult)
                                dst = cexs[:, c * CW:(c + 1) * CW].rearrange("p (s t) -> p s t", t=HC + ST)[:, :, HC:HC + ST]
                                P.tt("dve", V(dst, ["cexs"]), as3(pu), as3(sg), ALU.mult)
                            else:
                                P.tt("dve", cext.new(c, b), pu, sg, ALU.mult)
                                if b == 1:
                                    P.tt("dve", V(cexp[:, c * HC:(c + 1) * HC], ["cexp"]),
                                         V(pu.ap[:, 512 - HC:512], pu.keys), V(sg.ap[:, 512 - HC:512], sg.keys), ALU.mult)
                    P.dma("sp", V(o_conv_s[l, ps_i], []), V(cexs[:], ["cexs"]), is_output=True)
                    if ps_i == 0:
                        P.copy("dve", V(hp_conv[:, l * 4 * HC:(l + 1) * 4 * HC], ["hp_conv"]), V(cexp[:], ["cexp"]))
                    else:
                        P.dma("sp", V(o_conv_p[l], []), V(cexp[:], ["cexp"]), is_output=True)

                    sa = W.next((l, "a"))
                    for g in range(4):
                        w = 2 << g
                        P.dma("sp", aext.samp_flat(g), V(stp[l, ps_i, :, g, :], []))
                        if ps_i == 0:
                            P.memset("dve", aext.pcols(g, 0, HP), 0.0)
                        else:
                            P.copy("dve", aext.pcols(g, 0, HP),
                                   V(hp_pool[:, (l * 4 + g) * HP:(l * 4 + g + 1) * HP], ["hp_pool"]))
                        for b in range(NBLK):
                            pa = ring6.get(b)
                            P.mm(pa, [(sa.lhsT(0, k, g * 128), nb.v(k, b)) for k in range(DC)])
                            P.copy("act", aext.new(g, b), as3(pa) if b == 2 else pa)
                        P.dma("sp", V(o_pool_s[l, ps_i, :, g, :], []), aext.samp_flat(g), is_output=True)
                        tailv = aext.pcols(g, HP + PT - HP, HP)
                        if ps_i == 0:
                            P.copy("dve", V(hp_pool[:, (l * 4 + g) * HP:(l * 4 + g + 1) * HP], ["hp_pool"]), tailv)
                        else:
                            P.copy("dve", V(opp[:, g * HP:(g + 1) * HP], ["opp"]), tailv)
                        src = aext
                        srcc = g
                        bufs = [tmpA, tmpB]
                        for j in range(g + 1):
                            d = 1 << j
                            lo = 2 * d - 1
                            dst = bufs[j % 2]
                            npc = HP + PT - lo
                            P.tt(E_MIX, dst.pcols(0, lo, npc), src.pcols(srcc, lo, npc), src.pcols(srcc, lo - d, npc), ALU.add)
                            nsc = HP + ST - lo
                            P.tt(E_MIX, dst.samp(0, lo, nsc), src.samp(srcc, lo, nsc), src.samp(srcc, lo - d, nsc), ALU.add)
                            src, srcc = dst, 0
                        inv = 1.0 / w
                        P.stt(zb.cols(g, 0, PT), src.pcols(srcc, HP, PT), inv, aext.pcols(g, HP, PT), ALU.mult, ALU.subtract)
                        P.stt(zb.v(g, 2, three=True), src.samp(srcc, HP, ST), inv, aext.samp(g, HP, ST), ALU.mult, ALU.subtract)
                        if ps_i == 0:
                            tfx = tf.get(w - 1)
                            P.tt("dve", tfx, src.pcols(srcc, HP, w - 1), V(cst[:, 128:128 + w - 1], ["cst"]), ALU.mult)
                            P.tt("dve", zb.cols(g, 0, w - 1), tfx, aext.pcols(g, HP, w - 1), ALU.subtract)
                    if ps_i == 1:
                        P.dma("sp", V(o_pool_p[l], []), V(opp[:], ["opp"]), is_output=True)

                    tms.close()
                    diag2 = sb("diag2", [128, 31, 128], BF16, ms)
                    P.alias_fence(["diag2"])
                    dbufs = [(diag, "diag"), (diag2, "diag2")]
                    s1b = [bank(2, 0), bank(3, 1), bank(6, 2)]
                    s2b = [bank(4, 0), bank(5, 1), bank(7, 2)]
                    cring = Ring([0, 1])
                    idv = V(identb[:], ["identb"])
                    pend = []

                    def flush_stats():
                        while pend:
                            b_, c_, cb_, sq_ = pend.pop(0)
                            P.op("pe", lambda h, o=s1b[b_], r=cb_, c=c_: h.matmul(o.ap, onesb[:], r.ap, start=(c == 0), stop=(c == 3)),
                                 ["onesb"] + cb_.keys, s1b[b_].keys, signal=True)
                            P.op("pe", lambda h, o=s2b[b_], r=sq_, c=c_: h.matmul(o.ap, onesb[:], r.ap, start=(c == 0), stop=(c == 3)),
                                 ["onesb"] + sq_.keys, s2b[b_].keys, signal=True)

                    for c in range(4):
                        for k in range(31):
                            wv = col(wdw, "wdw", (l * 4 + c) * 31 + k)
                            dt_, dn_ = dbufs[c % 2]
                            dv = V(dt_[:, k, :], [(dn_, k)])
                            if E_DIAG == "pool":
                                P.ts("pool", dv, idv, wv, ALU.mult, 0.0, ALU.add)
                            else:
                                P.act(dv, idv, AF.Copy, scale=wv)
                        for b in range(NBLK):
                            n = blk(b)[1]
                            pc = cring.get(b, three=True)
                            P.mm(pc, [(V(dbufs[c % 2][0][:, k, :], [(dbufs[c % 2][1], k)]), cext.new(c, b, shift=HC - k))
                                      for k in range(31)])
                            flush_stats()
                            pcf = V(ps[:, pc.keys[0][1], 0:n], pc.keys)
                            bia = g4(l, 1, c)
                            P.act(cf.v(c, b), pcf, AF.Identity, bias=bia)
                            sq = tb.get(n)
                            P.act(sq, pcf, AF.Square, bias=bia)
                            cb = tb.get(n)
                            P.copy("pool", cb, cf.v(c, b))
                            pend.append((b, c, cb, sq))
                    flush_stats()

                    spw, so0, so1 = W.acquire([(l, "pw"), (l, "o0"), (l, "o1")])
                    oring = Ring([4, 5, 0, 1])

                    def st_E(b):
                        n = blk(b)[1]
                        mean = tr.get(n)
                        P.ts("dve", mean, s1b[b], 1.0 / DP, ALU.mult)
                        msq = tf.get(n)
                        P.tt("dve", msq, mean, mean, ALU.mult)
                        v2 = tf.get(n)
                        P.stt(v2, s2b[b], 1.0 / DP, msq, ALU.mult, ALU.subtract)
                        sd = tf.get(n)
                        P.act(sd, v2, AF.Ln, bias=col(epsc, "epsc", 1))
                        rs = tr.get(n)
                        P.act(rs, sd, AF.Exp, scale=-0.5)
                        ab = actb[b % 2]
                        for c in range(4):
                            P.tt("dve", cf.v(c, b), cf.v(c, b), mean, ALU.subtract)
                            P.tt("dve", cf.v(c, b), cf.v(c, b), rs, ALU.mult)
                            vv = tf.get(n)
                            P.act(vv, cf.v(c, b), AF.Identity, scale=g4(l, 2, c), bias=g4(l, 3, c))
                            sg = tf.get(n)
                            P.act(sg, cf.v(c, b), AF.Sigmoid, scale=g4(l, 2, c), bias=g4(l, 3, c))
                            P.tt("pool", V(ab[:, c, 0:n], [("actb", b % 2, c)]), vv, sg, ALU.mult)

                    def st_A(b):
                        n = blk(b)[1]
                        sqs = []
                        for g in range(4):
                            pb = cring.get(b)
                            P.mm(pb, [(V(wpool_all[:, l, g * 128:(g + 1) * 128], ["wpool_all"]), zb.v(g, b))])
                            sq = tb.get(n)
                            P.act(sq, pb, AF.Square, scale=g4(l, 0, g))
                            sqs.append(sq)
                            P.act(nb.v(g, b), pb, AF.Identity, scale=col(psga, "psga", l * 4 + g))
                        sbk = s2b[b]
                        P.mm(sbk, [(ones_v, sq) for sq in sqs])
                        r = rstd_from_stat(sbk, b, 1.0 / DP, RMS_EPS)
                        for g in range(4):
                            P.tt("pool" if g % 2 else "dve", nb.v(g, b), nb.v(g, b), r, ALU.mult)

                    def st_Pw(b):
                        n = blk(b)[1]
                        ab = actb[b % 2]
                        sqs = []
                        for m in range(4):
                            pb = cring.get(b)
                            P.mm(pb, [(spw.lhsT(0, k, m * 128), V(ab[:, k, 0:n], [("actb", b % 2, k)])) for k in range(4)])
                            sq = tb.get(n)
                            P.act(sq, pb, AF.Square)
                            sqs.append(sq)
                            P.act(nb.v(4 + m, b), pb, AF.Identity, scale=g4(l, 5, m))
                        sbk = s1b[b]
                        P.mm(sbk, [(ones_v, sq) for sq in sqs])
                        r = rstd_from_stat(sbk, b, 1.0 / DP, RMS_EPS)
                        for m in range(4):
                            P.tt("pool" if m % 2 else "dve", nb.v(4 + m, b), nb.v(4 + m, b), r, ALU.mult)

                    def st_O(b):
                        for m in range(DC):
                            so = so0 if m < 4 else so1
                            pb = oring.get(b)
                            P.mm(pb, [(so.lhsT(0, k, (m % 4) * 128), nb.v(k, b)) for k in range(DC)])
                            P.tt("dve", xs.v(m, b), pb, xs.v(m, b), ALU.add)

                    def st_N2(b):
                        rmsnorm_blk(lambda c: g8(l, 1, c), b, sbk=bank(2, b))

                    st_E(0); st_A(0); st_Pw(0); st_E(1); st_O(0); st_N2(0); st_A(1); st_Pw(1); st_E(2)
                    st_O(1); st_N2(1); st_A(2); st_Pw(2); st_O(2); st_N2(2)

                with ExitStack() as fs:
                    hb = Plain(sb("hb", [128, 12, T], BF16, fs), "hb")
                    gext = Ext(sb("gext", [128, 2, HF + PT + SS * (HF + ST)], F32, fs), "gext", HF, phys=2)
                    stfs = sb("stfs", [128, NJ * SS * HF], F32, fs)
                    ofs = sb("ofs", [128, NJ * SS * HF], F32, fs)
                    ofp = sb("ofp", [128, NJ * HF], F32, fs)
                    P.alias_fence(["hb", "gext", "stfs", "ofs", "ofp"])

                    P.dma("sp", V(stfs[:], ["stfs"]), V(stf[l, ps_i].rearrange("p j w -> p (j w)"), []))

                    for (j0, nj) in FFN_GROUPS:
                        for j in range(j0, j0 + nj, 2):
                            sl = W.next((l, "fi%d" % j))
                            for jj in range(2):
                                jc = j + jj
                                jl = jc - j0
                                wc = lambda t_: col(wfdw, "wfdw", (l * NJ + jc) * 4 + t_)
                                if ps_i == 0:
                                    P.memset("dve", gext.pcols(jc, 0, HF), 0.0)
                                else:
                                    P.copy("dve", gext.pcols(jc, 0, HF),
                                           V(hp_ffn[:, (l * NJ + jc) * HF:(l * NJ + jc + 1) * HF], ["hp_ffn"]))
                                hsrc = stfs[:, jc * SS * HF:(jc + 1) * SS * HF].rearrange("p (s t) -> p s t", t=HF)
                                P.copy("dve", gext.samp(jc, 0, HF), V(hsrc, ["stfs"]))
                                for b in range(NBLK):
                                    n = blk(b)[1]
                                    three = (b == 2)
                                    pg = ring8.get(b)
                                    P.mm(pg, [(sl.lhsT(0, k, jj * 128), nb.v(k, b)) for k in range(DC)])
                                    pu = ring8.get(b)
                                    P.mm(pu, [(sl.lhsT(1, k, jj * 128), nb.v(k, b)) for k in range(DC)])
                                    pg3 = as3(pg) if three else pg
                                    pu3 = as3(pu) if three else pu
                                    P.copy("act", gext.new(jc, b), pg3)
                                    t0 = tf.get(n, three)
                                    P.act(t0, pg3, AF.Identity, scale=wc(2), bias=wc(3))
                                    t1 = tf.get(n, three)
                                    P.stt(t1, gext.new(jc, b, shift=1), wc(1), t0, ALU.mult, ALU.add)
                                    t2 = tf.get(n, three)
                                    P.stt(t2, gext.new(jc, b, shift=2), wc(0), t1, ALU.mult, ALU.add)
                                    t3 = tf.get(n, three)
                                    P.tt("dve", t3, pu3, t2, ALU.mult)
                                    sg = tf.get(n, three)
                                    P.act(sg, t2, AF.Sigmoid)
                                    P.tt("dve", hb.v(jl, b, three=three), t3, sg, ALU.mult)
                                tailv = gext.pcols(jc, HF + PT - HF, HF)
                                if ps_i == 0:
                                    P.copy("dve", V(hp_ffn[:, (l * NJ + jc) * HF:(l * NJ + jc + 1) * HF], ["hp_ffn"]), tailv)
                                else:
                                    P.copy("dve", V(ofp[:, jc * HF:(jc + 1) * HF], ["ofp"]), tailv)
                                odst = ofs[:, jc * SS * HF:(jc + 1) * SS * HF].rearrange("p (s t) -> p s t", t=HF)
                                P.copy("dve", V(odst, ["ofs"]), gext.samp(jc, HF + ST - HF, HF))
                        for q in range(D // FO_W):
                            so = W.next((l, "fo%d_%d" % (j0, q)))
                            for mloc in range(FO_W // 128):
                                m = q * (FO_W // 128) + mloc
                                for b in range(NBLK):
                                    pb = ring6.get(b)
                                    P.mm(pb, [(so.lhsT(0, jl, mloc * 128), hb.v(jl, b)) for jl in range(nj)])
                                    P.tt("dve", xs.v(m, b), pb, xs.v(m, b), ALU.add)
                    P.dma("sp", V(o_ffn_s[l, ps_i], []), V(ofs[:], ["ofs"]), is_output=True)
                    if ps_i == 1:
                        P.dma("sp", V(o_ffn_p[l], []), V(ofp[:], ["ofp"]), is_output=True)

                with ExitStack() as es:
                    ef = sb("ef", [128, DC, 512], F32, es)
                    P.alias_fence(["ef"])
                    for c in range(DC):
                        for b in range(NBLK):
                            P.copy("act", nb.v(c, b), xs.v(c, b))
                    sple, sg0_, sg1_ = W.acquire([(l, "ple"), (l, "g0"), (l, "g1")])
                    sgs = [sg0_, sg1_]
                    pend_norm = []
                    for b in range(NBLK):
                        n = blk(b)[1]
                        for m in range(DC):
                            pb = ring6.get(b)
                            P.mm(pb, [(sple.lhsT(0, k, m * 128), pTb.v(k, b)) for k in range(2)])
                            P.copy("act", V(ef[:, m, 0:n], [("ef", m)]), pb)
                        while pend_norm:
                            pend_norm.pop(0)()
                        sqs = []
                        for m in range(DC):
                            sq = tb.get(n)
                            square_x(sq, V(ef[:, m, 0:n], [("ef", m)]), m)
                            sqs.append(sq)
                        r = None
                        for m in range(DC):
                            pb = ring6.get(b)
                            P.mm(pb, [(sgs[m // 4].lhsT(0, k, (m % 4) * 128), nb.v(k, b)) for k in range(DC)])
                            if m == 0:
                                sbk = statr.get(b)
                                P.mm(sbk, [(ones_v, sq) for sq in sqs])
                                r = rstd_from_stat(sbk, b, 1.0 / D, RMS_EPS)
                                for m2 in range(DC):
                                    efv = V(ef[:, m2, 0:n], [("ef", m2)])
                                    P.tt("pool", efv, efv, r, ALU.mult)
                            sg = tf.get(n)
                            P.act(sg, pb, AF.Sigmoid)
                            t1 = tf.get(n)
                            P.stt(t1, V(ef[:, m, 0:n], [("ef", m)]), g8(l, 2, m), sg, ALU.mult, ALU.mult)
                            P.tt("dve", xs.v(m, b), xs.v(m, b), t1, ALU.add)
                        if l < L - 1:
                            pend_norm.append(rmsnorm_blk(lambda c, l=l: g8(l + 1, 0, c), b, split=True))
                        else:
                            pend_norm.append(rmsnorm_blk(lambda c: g8(0, 3, c), b, inplace=True, split=True))
                    while pend_norm:
                        pend_norm.pop(0)()

            P.dma("sp", V(yT[ps_i], []), V(xs.t[:], [("xs", c, b) for c in range(DC) for b in range(NBLK)]),
                  is_output=True)

        P.finish()
        build_nc.ninst = dict(P.ninst)
    return nc


def _fmaj(a, nchunk):
    s = a.shape
    a = a.reshape(s[:-1] + (nchunk, 128))
    nd = a.ndim
    perm = list(range(nd - 3)) + [nd - 1, nd - 2, nd - 3]
    return np.ascontiguousarray(a.transpose(perm))


_NC_CACHE = {}


def kernel(x_prompt, x_sample, state_pool, state_conv, state_ffn, p_prompt, p_sample,
           g_mix, w_in, w_pool, pool_scale, w_dw, b_dw, ln_g, ln_b, w_pw,
           g_out_a, g_out_b, w_out, g_ffn, w_ffn_in, w_ffn_dw, b_ffn_dw, w_ffn_out,
           w_ple, g_ple, w_ple_gate, final_norm):
    f = lambda a: np.ascontiguousarray(np.asarray(a, dtype=np.float32))
    x_prompt, x_sample, state_pool, state_conv, state_ffn, p_prompt, p_sample = map(
        f, (x_prompt, x_sample, state_pool, state_conv, state_ffn, p_prompt, p_sample))
    ncores = 8
    if "nc" not in _NC_CACHE:
        _NC_CACHE["nc"] = build_nc()
    nc = _NC_CACHE["nc"]

    p8 = np.zeros((L, 4, 8, 128), np.float32)
    p8[:, 0] = f(g_mix).reshape(L, 8, 128)
    p8[:, 1] = f(g_ffn).reshape(L, 8, 128)
    p8[:, 2] = f(g_ple).reshape(L, 8, 128)
    p8[:, 3] = f(final_norm).reshape(1, 8, 128)
    p8 = np.ascontiguousarray(p8.transpose(3, 0, 1, 2).reshape(128, L * 4 * 8))
    p4 = np.stack([f(pool_scale), f(b_dw), f(ln_g), f(ln_b), f(g_out_a), f(g_out_b)], 1).reshape(L, 6, 4, 128)
    p4 = np.ascontiguousarray(p4.transpose(3, 0, 1, 2).reshape(128, L * 6 * 4))
    wdw = f(w_dw).reshape(L, 31, 4, 128).transpose(3, 0, 2, 1)
    wdw = np.ascontiguousarray(wdw.reshape(128, L * 4 * 31))
    wf = np.concatenate([f(w_ffn_dw), f(b_ffn_dw)[:, None, :]], 1).reshape(L, 4, NJ, 128).transpose(3, 0, 2, 1)
    wf = np.ascontiguousarray(wf.reshape(128, L * NJ * 4))
    cst = np.zeros((128, 144), np.float32)
    cst[:, :128] = np.eye(128, dtype=np.float32)
    cst[:, 128:144] = (1.0 / np.arange(1, 17, dtype=np.float32))[None, :]
    shared = {"p8": p8, "p4": p4, "wdw": wdw, "wfdw": wf, "cst": cst,
              "w_in": f(w_in), "w_pool": f(w_pool), "w_pw": f(w_pw), "w_out": f(w_out),
              "w_ffn_in": f(w_ffn_in), "w_ffn_out": f(w_ffn_out), "w_ple": f(w_ple),
              "w_ple_gate": f(w_ple_gate)}

    in_maps = []
    for c in range(ncores):
        xT = np.zeros((NPASS, DC, 128, T), np.float32)
        pT = np.zeros((L, NPASS, 2, 128, T), np.float32)
        stp = np.zeros((L, NPASS, 128, 4, SS, HP + ST), np.float32)
        stc = np.zeros((L, NPASS, 128, 4, SS, HC + ST), np.float32)
        stf = np.zeros((L, NPASS, 128, NJ, SS, HF), np.float32)
        for h in range(NPASS):
            sq = slice(16 * c + SS * h, 16 * c + SS * (h + 1))
            xp = x_prompt[c, h * PT:(h + 1) * PT]
            xsm = x_sample[sq].reshape(SS * ST, D)
            xT[h] = np.concatenate([xp, xsm], 0).T.reshape(DC, 128, T)
            for l in range(L):
                pp = np.concatenate([p_prompt[l, c, h * PT:(h + 1) * PT], p_sample[l, sq].reshape(SS * ST, DPLE)], 0)
                pT[l, h] = pp.T.reshape(2, 128, T)
                stp[l, h, :, :, :, :HP] = _fmaj(state_pool[l, sq].reshape(SS * HP, DP), 4).reshape(128, 4, SS, HP)
                stc[l, h, :, :, :, :HC] = _fmaj(state_conv[l, sq].reshape(SS * HC, DP), 4).reshape(128, 4, SS, HC)
                stf[l, h] = _fmaj(state_ffn[l, sq].reshape(SS * HF, DFF), NJ).reshape(128, NJ, SS, HF)
        m = dict(shared)
        m.update({"xT": xT, "pT": pT,
                  "stp": stp.reshape(L, NPASS, 128, 4, SS * (HP + ST)),
                  "stc": stc.reshape(L, NPASS, 128, 4, SS * (HC + ST)),
                  "stf": stf.reshape(L, NPASS, 128, NJ, SS * HF)})
        in_maps.append(m)

    res = run_bass_kernel_spmd(nc, in_maps, core_ids=list(range(ncores)))

    B, S = 8, 2048
    y_prompt = np.zeros((B, S, D), np.float32)
    y_sample = np.zeros((128, ST, D), np.float32)
    pool_p = np.zeros((L, B, HP, DP), np.float32)
    conv_p = np.zeros((L, B, HC, DP), np.float32)
    ffn_p = np.zeros((L, B, HF, DFF), np.float32)
    pool_s = np.zeros((L, 128, HP, DP), np.float32)
    conv_s = np.zeros((L, 128, HC, DP), np.float32)
    ffn_s = np.zeros((L, 128, HF, DFF), np.float32)
    for c in range(ncores):
        r = res.results[c]
        yT = np.asarray(r["yT"]).reshape(NPASS, 128, DC, T)
        for h in range(NPASS):
            sq = slice(16 * c + SS * h, 16 * c + SS * (h + 1))
            yy = yT[h].transpose(2, 1, 0).reshape(T, D)
            y_prompt[c, h * PT:(h + 1) * PT] = yy[:PT]
            y_sample[sq] = yy[PT:].reshape(SS, ST, D)
            ops_ = np.asarray(r["o_pool_s"]).reshape(L, NPASS, 128, 4, SS, HP + ST)[:, h]
            pool_s[:, sq] = ops_[..., ST:].transpose(0, 3, 4, 2, 1).reshape(L, SS, HP, DP)
            ocs = np.asarray(r["o_conv_s"]).reshape(L, NPASS, 128, 4, SS, HC + ST)[:, h]
            conv_s[:, sq] = ocs[..., ST:].transpose(0, 3, 4, 2, 1).reshape(L, SS, HC, DP)
            ofs = np.asarray(r["o_ffn_s"]).reshape(L, NPASS, 128, NJ, SS, HF)[:, h]
            ffn_s[:, sq] = ofs.transpose(0, 3, 4, 2, 1).reshape(L, SS, HF, DFF)
        pool_p[:, c] = np.asarray(r["o_pool_p"]).reshape(L, 128, 4, HP).transpose(0, 3, 2, 1).reshape(L, HP, DP)
        conv_p[:, c] = np.asarray(r["o_conv_p"]).reshape(L, 128, 4, HC).transpose(0, 3, 2, 1).reshape(L, HC, DP)
        ffn_p[:, c] = np.asarray(r["o_ffn_p"]).reshape(L, 128, NJ, HF).transpose(0, 3, 2, 1).reshape(L, HF, DFF)
    return (y_prompt, y_sample, pool_p, conv_p, ffn_p, pool_s, conv_s, ffn_s)
```

```python
import numpy as np
from contextlib import ExitStack
import concourse.bass as bass
import concourse.mybir as mybir
from concourse.bass_utils import run_bass_kernel_spmd

F32 = mybir.dt.float32
BF16 = mybir.dt.bfloat16
AF = mybir.ActivationFunctionType
ALU = mybir.AluOpType

L = 4
D = 1024
DC = 8
DP = 512
DFF = 2816
NJ = 22
DPLE = 256
PT = 1024
SS = 8
ST = 8
T = PT + SS * ST
NPASS = 2
NBLK = 3
HP, HC, HF = 15, 30, 2
RMS_EPS = 1e-6
LN_EPS = 1e-5
NSLOT = 4
SLOT_EL = 4096
FFN_GROUPS = [(0, 12), (12, 10)]
FO_W = 256

DEBUG = False
E_MIX = "dve"
E_DIAG = "pool"


def blk(b):
    return (0, 512) if b == 0 else ((512, 512) if b == 1 else (1024, 64))


class V:
    __slots__ = ("ap", "keys")

    def __init__(self, ap, keys):
        self.ap = ap
        self.keys = list(keys)


def as3(v):
    return V(v.ap.rearrange("p (s t) -> p s t", t=ST), v.keys)


class _Eng:
    def __init__(self, name, h, sem):
        self.name = name
        self.h = h
        self.sem = sem
        self.count = 0
        self.waited = {}


class Prog:
    def __init__(self, nc, stack, n_dma_sems=32):
        self.nc = nc
        self.e = {}
        for name, h in (("pe", nc.tensor), ("act", nc.scalar), ("dve", nc.vector),
                        ("pool", nc.gpsimd), ("sp", nc.sync)):
            sem = stack.enter_context(nc.semaphore("sem_" + name))
            self.e[name] = _Eng(name, h, sem)
        self.dsem = [stack.enter_context(nc.semaphore("dsem%d" % i)) for i in range(n_dma_sems)]
        self.dcnt = [0] * n_dma_sems
        self.dnext = {"sp": 0, "pool": n_dma_sems // 2}
        self.drange = {"sp": (0, n_dma_sems // 2), "pool": (n_dma_sems // 2, n_dma_sems)}
        self.res = {}
        self.fence = []
        self.scoped = set()
        self.out_toks = []
        self.ninst = {k: 0 for k in self.e}

    def _need(self, e, tok):
        sem, val = tok
        if e.waited.get(sem.num, 0) >= val:
            return
        e.h.wait_ge(sem, val)
        e.waited[sem.num] = val

    def alias_fence(self, names):
        f = [(o.sem, o.count) for o in self.e.values() if o.count > 0]
        f += [(self.dsem[k], self.dcnt[k]) for k in range(len(self.dsem)) if self.dcnt[k] > 0]
        self.fence = f
        self.scoped = set(names)
        for k in [k for k in self.res if (k[0] if isinstance(k, tuple) else k) in self.scoped]:
            del self.res[k]

    def _deps(self, e, reads, writes):
        toks = []
        for k in list(reads) + list(writes):
            if k not in self.res and (k[0] if isinstance(k, tuple) else k) in self.scoped:
                toks.extend(self.fence)
                break
        for k in reads:
            r = self.res.get(k)
            if r is not None and r[0] is not None:
                toks.append(r[0])
        for k in writes:
            r = self.res.get(k)
            if r is not None:
                if r[0] is not None:
                    toks.append(r[0])
                toks.extend(r[1].values())
        for t in toks:
            if t[0] is e.sem and e.name == "pe":
                continue
            self._need(e, t)

    def _record(self, rkey, tok, reads, writes):
        for k in reads:
            r = self.res.get(k)
            if r is None:
                r = [None, {}]
                self.res[k] = r
            r[1][rkey] = tok
        for k in writes:
            self.res[k] = [tok, {}]

    def op(self, ename, fn, reads, writes, signal=True):
        e = self.e[ename]
        self._deps(e, reads, writes)
        inst = fn(e.h)
        self.ninst[ename] += 1
        if signal:
            e.count += 1
            inst.then_inc(e.sem, 1)
            tok = (e.sem, e.count)
        else:
            tok = (e.sem, e.count + 1)
        self._record(ename, tok, reads, writes)
        return tok

    def dma(self, qname, out, in_, is_output=False):
        e = self.e[qname]
        reads = in_.keys
        writes = out.keys
        self._deps(e, reads, writes)
        k = self.dnext[qname]
        lo, hi = self.drange[qname]
        self.dnext[qname] = lo + (k + 1 - lo) % (hi - lo)
        sem = self.dsem[k]
        if self.dcnt[k] > 0:
            self._need(e, (sem, self.dcnt[k]))
        e.h.dma_start(out=out.ap, in_=in_.ap).then_inc(sem, 16)
        self.ninst[qname] += 1
        self.dcnt[k] += 16
        tok = (sem, self.dcnt[k])
        self._record(("dma", k), tok, reads, writes)
        if is_output:
            self.out_toks.append(tok)
        return tok

    def barrier(self):
        sp = self.e["sp"]
        for k in range(len(self.dsem)):
            if self.dcnt[k] > 0:
                self._need(sp, (self.dsem[k], self.dcnt[k]))
        for name in ("pe", "act", "dve", "pool"):
            o = self.e[name]
            if o.count > 0:
                self._need(sp, (o.sem, o.count))
        sp.count += 1
        sp.h.nop().then_inc(sp.sem, 1)
        for name in ("pe", "act", "dve", "pool"):
            self._need(self.e[name], (sp.sem, sp.count))
        self.res.clear()

    def finish(self):
        self.barrier()

    @staticmethod
    def _sk(x):
        return x.keys if isinstance(x, V) else []

    @staticmethod
    def _sa(x):
        return x.ap if isinstance(x, V) else x

    def act(self, out, in_, func, scale=1.0, bias=0.0):
        rd = in_.keys + self._sk(scale) + self._sk(bias)
        sc, bi = self._sa(scale), self._sa(bias)
        return self.op("act", lambda h: h.activation(out=out.ap, in_=in_.ap, func=func, scale=sc, bias=bi),
                       rd, out.keys)

    def tt(self, eng, out, in0, in1, op):
        return self.op(eng, lambda h: h.tensor_tensor(out=out.ap, in0=in0.ap, in1=in1.ap, op=op),
                       in0.keys + in1.keys, out.keys)

    def ts(self, eng, out, in0, s1, op0, s2=None, op1=None):
        rd = in0.keys + self._sk(s1) + self._sk(s2)
        a1, a2 = self._sa(s1), self._sa(s2)
        if op1 is None:
            return self.op(eng, lambda h: h.tensor_scalar(out=out.ap, in0=in0.ap, scalar1=a1, scalar2=None, op0=op0),
                           rd, out.keys)
        return self.op(eng, lambda h: h.tensor_scalar(out=out.ap, in0=in0.ap, scalar1=a1, scalar2=a2,
                                                      op0=op0, op1=op1), rd, out.keys)

    def stt(self, out, in0, scalar, in1, op0, op1):
        rd = in0.keys + in1.keys + self._sk(scalar)
        sc = self._sa(scalar)
        return self.op("dve", lambda h: h.scalar_tensor_tensor(out=out.ap, in0=in0.ap, scalar=sc, in1=in1.ap,
                                                               op0=op0, op1=op1), rd, out.keys)

    def copy(self, eng, out, in_):
        if eng == "act":
            return self.act(out, in_, AF.Copy)
        return self.op(eng, lambda h: h.tensor_copy(out=out.ap, in_=in_.ap), in_.keys, out.keys)

    def memset(self, eng, out, val):
        return self.op(eng, lambda h: h.memset(out.ap, val), [], out.keys)

    def mm(self, out, pairs):
        n = len(pairs)
        tok = None
        for i, (l, r) in enumerate(pairs):
            tok = self.op("pe", lambda h, l=l, r=r, i=i: h.matmul(out.ap, l.ap, r.ap, start=(i == 0),
                                                                    stop=(i == n - 1)),
                          l.keys + r.keys, out.keys, signal=(i == n - 1))
        return tok


class Plain:
    def __init__(self, t, name):
        self.t = t
        self.name = name

    def v(self, c, b, three=False):
        lo, n = blk(b)
        v = V(self.t[:, c, lo:lo + n], [(self.name, c, b)])
        if three and b == 2:
            v = as3(v)
        return v

    def cols(self, c, lo, n):
        keys = []
        for b in range(NBLK):
            blo, bn = blk(b)
            if lo < blo + bn and lo + n > blo:
                keys.append((self.name, c, b))
        return V(self.t[:, c, lo:lo + n], keys)


class Ext:
    def __init__(self, t, name, H, phys=None):
        self.t = t
        self.name = name
        self.H = H
        self.W = H + PT + SS * (H + ST)
        self.phys = phys

    def _c(self, c):
        return c if self.phys is None else c % self.phys

    def _regions(self, c, lo, hi):
        H = self.H
        keys = []
        if lo < H + 512 and hi > 0:
            keys.append((self.name, self._c(c), 0))
        if lo < H + PT and hi > H + 512:
            keys.append((self.name, self._c(c), 1))
        if hi > H + PT:
            keys.append((self.name, self._c(c), 2))
        return keys

    def new(self, c, b, shift=0):
        H = self.H
        cc = self._c(c)
        if b < 2:
            lo = H + 512 * b - shift
            return V(self.t[:, cc, lo:lo + 512], self._regions(c, lo, lo + 512))
        return self.samp(c, H - shift, ST)

    def samp(self, c, lo, n):
        H = self.H
        cc = self._c(c)
        ap = self.t[:, cc, H + PT:self.W].rearrange("p (s t) -> p s t", t=H + ST)[:, :, lo:lo + n]
        return V(ap, [(self.name, cc, 2)])

    def samp_flat(self, c):
        return V(self.t[:, self._c(c), self.H + PT:self.W], [(self.name, self._c(c), 2)])

    def pcols(self, c, lo, n):
        return V(self.t[:, self._c(c), lo:lo + n], self._regions(c, lo, lo + n))


class TilePool:
    def __init__(self, nc, st, name, n, width, dt):
        self.ts = [st.enter_context(nc.sbuf_tensor("%s%d" % (name, i), [128, width], dt)) for i in range(n)]
        self.name = name
        self.i = 0

    def get(self, n, three=False):
        i = self.i
        self.i = (self.i + 1) % len(self.ts)
        v = V(self.ts[i][:, 0:n], [(self.name, i)])
        if three:
            v = as3(v)
        return v


class WStream:
    def __init__(self, P, slots, plan):
        self.P = P
        self.slots = slots
        self.plan = plan
        self.issued = 0
        self.cur = 0

    def _issue(self, i):
        s = i % len(self.slots)
        off = 0
        for (ap, k, n) in self.plan[i]:
            dst = self.slots[s][:, off:off + k * n].rearrange("p (k n) -> p k n", n=n)
            self.P.dma("pool", V(dst, [("w", s)]), V(ap, []))
            off += k * n
        assert off <= SLOT_EL

    def acquire(self, tag_list):
        j0 = self.cur
        n = len(tag_list)
        assert n <= len(self.slots)
        for i, tag in enumerate(tag_list):
            assert self.plan_tags[j0 + i] == tag, (self.plan_tags[j0 + i], tag)
        self.cur += n
        while self.issued < min(len(self.plan), j0 + len(self.slots)):
            self._issue(self.issued)
            self.issued += 1
        return [Slab(self.slots[j % len(self.slots)], j % len(self.slots), self.plan[j]) for j in range(j0, j0 + n)]

    def next(self, tag):
        return self.acquire([tag])[0]


class Slab:
    def __init__(self, t, s, pieces):
        self.t = t
        self.s = s
        self.offs = []
        off = 0
        for (_, k, n) in pieces:
            self.offs.append((off, k, n))
            off += k * n

    def lhsT(self, piece, k, col0, ncol=128):
        off, K, n = self.offs[piece]
        o = off + k * n + col0
        return V(self.t[:, o:o + ncol], [("w", self.s)])


def build_nc():
    nc = bass.Bass("TRN2", target_bir_lowering=False)

    def din(name, shape):
        return nc.dram_tensor(name, list(shape), F32, kind="ExternalInput").ap()

    def dout(name, shape):
        return nc.dram_tensor(name, list(shape), F32, kind="ExternalOutput").ap()

    xT = din("xT", [NPASS, DC, 128, T])
    pTd = din("pT", [L, NPASS, 2, 128, T])
    stp = din("stp", [L, NPASS, 128, 4, SS * (HP + ST)])
    stc = din("stc", [L, NPASS, 128, 4, SS * (HC + ST)])
    stf = din("stf", [L, NPASS, 128, NJ, SS * HF])
    p8d = din("p8", [128, L * 4 * 8])
    p4d = din("p4", [128, L * 6 * 4])
    wdwd = din("wdw", [128, L * 4 * 31])
    wfdwd = din("wfdw", [128, L * NJ * 4])
    cstd = din("cst", [128, 128 + 16])
    w_in = din("w_in", [L, D, 1536])
    w_pool = din("w_pool", [L, 4, 128, 128])
    w_pw = din("w_pw", [L, DP, DP])
    w_out = din("w_out", [L, D, D])
    w_ffn_in = din("w_ffn_in", [L, D, 2 * DFF])
    w_ffn_out = din("w_ffn_out", [L, DFF, D])
    w_ple = din("w_ple", [L, DPLE, D])
    w_gate = din("w_ple_gate", [L, D, D])

    yT = dout("yT", [NPASS, 128, DC, T])
    o_pool_s = dout("o_pool_s", [L, NPASS, 128, 4, SS * (HP + ST)])
    o_pool_p = dout("o_pool_p", [L, 128, 4 * HP])
    o_conv_s = dout("o_conv_s", [L, NPASS, 128, 4 * SS * (HC + ST)])
    o_conv_p = dout("o_conv_p", [L, 128, 4 * HC])
    o_ffn_s = dout("o_ffn_s", [L, NPASS, 128, NJ * SS * HF])
    o_ffn_p = dout("o_ffn_p", [L, 128, NJ * HF])

    plan, tags = [], []

    def kp(ap):
        return ap.rearrange("(k p) n -> p k n", p=128)

    for ps_ in range(NPASS):
        for l in range(L):
            for nm, c0 in (("u1", 512), ("u2", 1024), ("a", 0)):
                plan.append([(kp(w_in[l, :, c0:c0 + 512]), 8, 512)]); tags.append((l, nm))
            plan.append([(kp(w_pw[l]), 4, 512)]); tags.append((l, "pw"))
            for h in range(2):
                plan.append([(kp(w_out[l, :, h * 512:(h + 1) * 512]), 8, 512)]); tags.append((l, "o%d" % h))
            for (j0, nj) in FFN_GROUPS:
                for j in range(j0, j0 + nj, 2):
                    plan.append([(kp(w_ffn_in[l, :, j * 128:j * 128 + 256]), 8, 256),
                                 (kp(w_ffn_in[l, :, DFF + j * 128:DFF + j * 128 + 256]), 8, 256)])
                    tags.append((l, "fi%d" % j))
                for q in range(D // FO_W):
                    plan.append([(kp(w_ffn_out[l, j0 * 128:(j0 + nj) * 128, q * FO_W:(q + 1) * FO_W]), nj, FO_W)])
                    tags.append((l, "fo%d_%d" % (j0, q)))
            plan.append([(kp(w_ple[l]), 2, 1024)]); tags.append((l, "ple"))
            for h in range(2):
                plan.append([(kp(w_gate[l, :, h * 512:(h + 1) * 512]), 8, 512)]); tags.append((l, "g%d" % h))

    with ExitStack() as st:
        P = Prog(nc, st)

        uniq = [0]

        def sb(name, shape, dt, stack=st):
            uniq[0] += 1
            return stack.enter_context(nc.sbuf_tensor("%s_%d" % (name, uniq[0]), list(shape), dt))

        xs = Plain(sb("xs", [128, DC, T], F32), "xs")
        nb = Plain(sb("nb", [128, DC, T], BF16), "nb")
        pTb = Plain(sb("pTb", [128, 2, T], BF16), "pTb")
        slots = [sb("wslot%d" % i, [128, SLOT_EL], BF16) for i in range(NSLOT)]
        W = WStream(P, slots, plan)
        W.plan_tags = tags
        p8 = sb("p8s", [128, L * 4 * 8], F32)
        p4 = sb("p4s", [128, L * 6 * 4], F32)
        wdw = sb("wdws", [128, L * 4 * 31], F32)
        wfdw = sb("wfdws", [128, L * NJ * 4], F32)
        cst = sb("csts", [128, 128 + 16], F32)
        identb = sb("identb", [128, 128], BF16)
        onesb = sb("onesb", [128, 128], BF16)
        wpool_all = sb("wpool_all", [128, L, 4 * 128], BF16)
        epsc = sb("epsc", [128, 2], F32)
        psga = sb("psga", [128, L * 4], F32)
        hp_pool = sb("hp_pool", [128, L * 4 * HP], F32)
        hp_conv = sb("hp_conv", [128, L * 4 * HC], F32)
        hp_ffn = sb("hp_ffn", [128, L * NJ * HF], F32)
        ps = st.enter_context(nc.psum_tensor("ps", [128, 8, 512], F32))
        tf = TilePool(nc, st, "tf", 6, 512, F32)
        tr = TilePool(nc, st, "tr", 4, 512, F32)
        tb = TilePool(nc, st, "tb", 8, 512, BF16)

        def bank(i, b, three=False):
            n = blk(b)[1]
            v = V(ps[:, i, 0:n], [("ps", i)])
            if three and b == 2:
                v = as3(v)
            return v

        class Ring:
            def __init__(self, banks):
                self.banks = banks
                self.i = 0

            def get(self, b, three=False):
                i = self.banks[self.i]
                self.i = (self.i + 1) % len(self.banks)
                return bank(i, b, three)

        ring6 = Ring([0, 1, 2, 3, 4, 5])
        ring8 = Ring([0, 1, 2, 3, 4, 5, 6, 7])
        statr = Ring([6, 7])

        def col(t, name, idx):
            return V(t[:, idx:idx + 1], [name])

        def g8(l, kind, c):
            return col(p8, "p8", (l * 4 + kind) * 8 + c)

        def g4(l, kind, c):
            return col(p4, "p4", (l * 6 + kind) * 4 + c)

        ones_v = V(onesb[:], ["onesb"])

        P.dma("sp", V(p8[:], ["p8"]), V(p8d, []))
        P.dma("sp", V(p4[:], ["p4"]), V(p4d, []))
        P.dma("sp", V(wdw[:], ["wdw"]), V(wdwd, []))
        P.dma("sp", V(wfdw[:], ["wfdw"]), V(wfdwd, []))
        P.dma("sp", V(cst[:], ["cst"]), V(cstd, []))
        P.copy("dve", V(identb[:], ["identb"]), V(cst[:, 0:128], ["cst"]))
        P.memset("dve", V(onesb[:], ["onesb"]), 1.0)
        for l_ in range(L):
            P.dma("pool", V(wpool_all[:, l_, :].rearrange("p (g c) -> p g c", c=128), ["wpool_all"]),
                  V(w_pool[l_].rearrange("g ci co -> ci g co"), []))
        P.memset("dve", V(epsc[:, 0:1], ["epsc"]), RMS_EPS)
        P.memset("dve", V(epsc[:, 1:2], ["epsc"]), LN_EPS)
        for l in range(L):
            P.tt("dve", V(psga[:, l * 4:(l + 1) * 4], ["psga"]),
                 V(p4[:, (l * 6 + 0) * 4:(l * 6 + 0) * 4 + 4], ["p4"]),
                 V(p4[:, (l * 6 + 4) * 4:(l * 6 + 4) * 4 + 4], ["p4"]), ALU.mult)

        def rstd_from_stat(sbank, b, inv_n, eps):
            n = blk(b)[1]
            t1 = tf.get(n)
            P.act(t1, sbank, AF.Ln, scale=inv_n, bias=col(epsc, "epsc", 0 if eps == RMS_EPS else 1))
            r = tr.get(n)
            P.act(r, t1, AF.Exp, scale=-0.5)
            return r

        def square_x(sq, xv, c):
            if c in (0, 4):
                P.tt("dve", sq, xv, xv, ALU.mult)
            else:
                P.act(sq, xv, AF.Square)

        def rmsnorm_blk(gfn, b, sbk=None, inplace=False, split=False):
            n = blk(b)[1]
            sqs = []
            for c in range(DC):
                sq = tb.get(n)
                square_x(sq, xs.v(c, b), c)
                sqs.append(sq)

            def rest(sbk=sbk):
                if sbk is None:
                    sbk = statr.get(b)
                P.mm(sbk, [(ones_v, sq) for sq in sqs])
                r = rstd_from_stat(sbk, b, 1.0 / D, RMS_EPS)
                for c in range(DC):
                    P.stt(xs.v(c, b) if inplace else nb.v(c, b), xs.v(c, b), gfn(c), r, ALU.mult, ALU.mult)

            if split:
                return rest
            rest()

        for ps_i in range(NPASS):
            for c in range(DC):
                P.dma("sp", V(xs.t[:, c, :], [("xs", c, b) for b in range(NBLK)]), V(xT[ps_i, c], []))

            for l in range(L):
                P.dma("pool", V(pTb.t[:], [("pTb", k, b) for k in range(2) for b in range(NBLK)]),
                      V(pTd[l, ps_i].rearrange("k p t -> p k t"), []))

                with ExitStack() as ms:
                    cext = Ext(sb("cext", [128, 4, HC + PT + SS * (HC + ST)], BF16, ms), "cext", HC)
                    cexs = sb("cexs", [128, 4 * SS * (HC + ST)], F32, ms)
                    cexp = sb("cexp", [128, 4 * HC], F32, ms)
                    aext = Ext(sb("aext", [128, 2, HP + PT + SS * (HP + ST)], F32, ms), "aext", HP, phys=2)
                    WA = HP + PT + SS * (HP + ST)
                    tmpA = Ext(sb("tmpA", [128, 1, WA], F32, ms), "tmpA", HP)
                    tmpB = Ext(sb("tmpB", [128, 1, WA], F32, ms), "tmpB", HP)
                    zb = Plain(sb("zb", [128, 4, T], BF16, ms), "zb")
                    cf = Plain(sb("cf", [128, 4, T], F32, ms), "cf")
                    diag = sb("diag", [128, 31, 128], BF16, ms)
                    opp = sb("opp", [128, 4 * HP], F32, ms)
                    actb = [sb("actb%d" % i, [128, 4, 512], BF16, ms) for i in range(2)]
                    P.alias_fence(["cext", "cexs", "cexp", "aext", "tmpA", "tmpB", "zb", "cf", "diag", "opp", "actb"])

                    if l == 0:
                        for b in range(NBLK):
                            rmsnorm_blk(lambda c: g8(0, 0, c), b)

                    CW = SS * (HC + ST)
                    P.dma("sp", V(cexs[:], ["cexs"]), V(stc[l, ps_i].rearrange("p c w -> p (c w)"), []))
                    for c in range(4):
                        if ps_i == 0:
                            P.memset("dve", cext.pcols(c, 0, HC), 0.0)
                        else:
                            P.copy("dve", cext.pcols(c, 0, HC),
                                   V(hp_conv[:, (l * 4 + c) * HC:(l * 4 + c + 1) * HC], ["hp_conv"]))
                        src = cexs[:, c * CW:(c + 1) * CW].rearrange("p (s t) -> p s t", t=HC + ST)[:, :, 0:HC]
                        P.copy("dve", cext.samp(c, 0, HC), V(src, ["cexs"]))

                    su1, su2 = W.acquire([(l, "u1"), (l, "u2")])
                    for c in range(4):
                        for b in range(NBLK):
                            n = blk(b)[1]
                            three = (b == 2)
                            pg = ring8.get(b)
                            P.mm(pg, [(su2.lhsT(0, k, c * 128), nb.v(k, b)) for k in range(DC)])
                            sg = tf.get(n)
                            P.act(sg, pg, AF.Sigmoid)
                            pu = ring8.get(b)
                            P.mm(pu, [(su1.lhsT(0, k, c * 128), nb.v(k, b)) for k in range(DC)])
                            if three:
                                P.tt("dve", cext.new(c, b), as3(pu), as3(sg), ALU.mult)
                                dst = cexs[:, c * CW:(c + 1) * CW].rearrange("p (s t) -> p s t", t=HC + ST)[:, :, HC:HC + ST]
                                P.tt("dve", V(dst, ["cexs"]), as3(pu), as3(sg), ALU.mult)
                            else:
                                P.tt("dve", cext.new(c, b), pu, sg, ALU.mult)
                                if b == 1:
                                    P.tt("dve", V(cexp[:, c * HC:(c + 1) * HC], ["cexp"]),
                                         V(pu.ap[:, 512 - HC:512], pu.keys), V(sg.ap[:, 512 - HC:512], sg.keys), ALU.mult)
                    P.dma("sp", V(o_conv_s[l, ps_i], []), V(cexs[:], ["cexs"]), is_output=True)
                    if ps_i == 0:
                        P.copy("dve", V(hp_conv[:, l * 4 * HC:(l + 1) * 4 * HC], ["hp_conv"]), V(cexp[:], ["cexp"]))
                    else:
                        P.dma("sp", V(o_conv_p[l], []), V(cexp[:], ["cexp"]), is_output=True)

                    sa = W.next((l, "a"))
                    for g in range(4):
                        w = 2 << g
                        P.dma("sp", aext.samp_flat(g), V(stp[l, ps_i, :, g, :], []))
                        if ps_i == 0:
                            P.memset("dve", aext.pcols(g, 0, HP), 0.0)
                        else:
                            P.copy("dve", aext.pcols(g, 0, HP),
                                   V(hp_pool[:, (l * 4 + g) * HP:(l * 4 + g + 1) * HP], ["hp_pool"]))
                        for b in range(NBLK):
                            pa = ring8.get(b)
                            P.mm(pa, [(sa.lhsT(0, k, g * 128), nb.v(k, b)) for k in range(DC)])
                            P.copy("act", aext.new(g, b), as3(pa) if b == 2 else pa)
                        P.dma("sp", V(o_pool_s[l, ps_i, :, g, :], []), aext.samp_flat(g), is_output=True)
                        tailv = aext.pcols(g, HP + PT - HP, HP)
                        if ps_i == 0:
                            P.copy("dve", V(hp_pool[:, (l * 4 + g) * HP:(l * 4 + g + 1) * HP], ["hp_pool"]), tailv)
                        else:
                            P.copy("dve", V(opp[:, g * HP:(g + 1) * HP], ["opp"]), tailv)
                        src = aext
                        srcc = g
                        bufs = [tmpA, tmpB]
                        for j in range(g + 1):
                            d = 1 << j
                            lo = 2 * d - 1
                            dst = bufs[j % 2]
                            npc = HP + PT - lo
                            P.tt(E_MIX, dst.pcols(0, lo, npc), src.pcols(srcc, lo, npc), src.pcols(srcc, lo - d, npc), ALU.add)
                            nsc = HP + ST - lo
                            P.tt(E_MIX, dst.samp(0, lo, nsc), src.samp(srcc, lo, nsc), src.samp(srcc, lo - d, nsc), ALU.add)
                            src, srcc = dst, 0
                        inv = 1.0 / w
                        P.stt(zb.cols(g, 0, PT), src.pcols(srcc, HP, PT), inv, aext.pcols(g, HP, PT), ALU.mult, ALU.subtract)
                        P.stt(zb.v(g, 2, three=True), src.samp(srcc, HP, ST), inv, aext.samp(g, HP, ST), ALU.mult, ALU.subtract)
                        if ps_i == 0:
                            tfx = tf.get(w - 1)
                            P.tt("dve", tfx, src.pcols(srcc, HP, w - 1), V(cst[:, 128:128 + w - 1], ["cst"]), ALU.mult)
                            P.tt("dve", zb.cols(g, 0, w - 1), tfx, aext.pcols(g, HP, w - 1), ALU.subtract)
                    if ps_i == 1:
                        P.dma("sp", V(o_pool_p[l], []), V(opp[:], ["opp"]), is_output=True)

                    s1b = [bank(2, 0), bank(3, 1), bank(6, 2)]
                    s2b = [bank(4, 0), bank(5, 1), bank(7, 2)]
                    cring = Ring([0, 1])
                    idv = V(identb[:], ["identb"])
                    pend = []

                    def flush_stats():
                        while pend:
                            b_, c_, cb_, sq_ = pend.pop(0)
                            P.op("pe", lambda h, o=s1b[b_], r=cb_, c=c_: h.matmul(o.ap, onesb[:], r.ap, start=(c == 0), stop=(c == 3)),
                                 ["onesb"] + cb_.keys, s1b[b_].keys, signal=True)
                            P.op("pe", lambda h, o=s2b[b_], r=sq_, c=c_: h.matmul(o.ap, onesb[:], r.ap, start=(c == 0), stop=(c == 3)),
                                 ["onesb"] + sq_.keys, s2b[b_].keys, signal=True)

                    for c in range(4):
                        for k in range(31):
                            wv = col(wdw, "wdw", (l * 4 + c) * 31 + k)
                            dv = V(diag[:, k, :], [("diag", k)])
                            if E_DIAG == "pool":
                                P.ts("pool", dv, idv, wv, ALU.mult, 0.0, ALU.add)
                            else:
                                P.act(dv, idv, AF.Copy, scale=wv)
                        for b in range(NBLK):
                            n = blk(b)[1]
                            pc = cring.get(b, three=True)
                            P.mm(pc, [(V(diag[:, k, :], [("diag", k)]), cext.new(c, b, shift=HC - k)) for k in range(31)])
                            flush_stats()
                            pcf = V(ps[:, pc.keys[0][1], 0:n], pc.keys)
                            bia = g4(l, 1, c)
                            P.act(cf.v(c, b), pcf, AF.Identity, bias=bia)
                            sq = tb.get(n)
                            P.act(sq, pcf, AF.Square, bias=bia)
                            cb = tb.get(n)
                            P.copy("pool", cb, cf.v(c, b))
                            pend.append((b, c, cb, sq))
                    flush_stats()

                    spw, so0, so1 = W.acquire([(l, "pw"), (l, "o0"), (l, "o1")])
                    oring = Ring([4, 5, 0, 1])

                    def st_E(b):
                        n = blk(b)[1]
                        mean = tr.get(n)
                        P.ts("dve", mean, s1b[b], 1.0 / DP, ALU.mult)
                        msq = tf.get(n)
                        P.tt("dve", msq, mean, mean, ALU.mult)
                        v2 = tf.get(n)
                        P.stt(v2, s2b[b], 1.0 / DP, msq, ALU.mult, ALU.subtract)
                        sd = tf.get(n)
                        P.act(sd, v2, AF.Ln, bias=col(epsc, "epsc", 1))
                        rs = tr.get(n)
                        P.act(rs, sd, AF.Exp, scale=-0.5)
                        ab = actb[b % 2]
                        for c in range(4):
                            P.tt("dve", cf.v(c, b), cf.v(c, b), mean, ALU.subtract)
                            P.tt("dve", cf.v(c, b), cf.v(c, b), rs, ALU.mult)
                            vv = tf.get(n)
                            P.act(vv, cf.v(c, b), AF.Identity, scale=g4(l, 2, c), bias=g4(l, 3, c))
                            sg = tf.get(n)
                            P.act(sg, cf.v(c, b), AF.Sigmoid, scale=g4(l, 2, c), bias=g4(l, 3, c))
                            P.tt("pool", V(ab[:, c, 0:n], [("actb", b % 2, c)]), vv, sg, ALU.mult)

                    def st_A(b):
                        n = blk(b)[1]
                        sqs = []
                        for g in range(4):
                            pb = cring.get(b)
                            P.mm(pb, [(V(wpool_all[:, l, g * 128:(g + 1) * 128], ["wpool_all"]), zb.v(g, b))])
                            sq = tb.get(n)
                            P.act(sq, pb, AF.Square, scale=g4(l, 0, g))
                            sqs.append(sq)
                            P.act(nb.v(g, b), pb, AF.Identity, scale=col(psga, "psga", l * 4 + g))
                        sbk = s2b[b]
                        P.mm(sbk, [(ones_v, sq) for sq in sqs])
                        r = rstd_from_stat(sbk, b, 1.0 / DP, RMS_EPS)
                        for g in range(4):
                            P.tt("pool" if g % 2 else "dve", nb.v(g, b), nb.v(g, b), r, ALU.mult)

                    def st_Pw(b):
                        n = blk(b)[1]
                        ab = actb[b % 2]
                        sqs = []
                        for m in range(4):
                            pb = cring.get(b)
                            P.mm(pb, [(spw.lhsT(0, k, m * 128), V(ab[:, k, 0:n], [("actb", b % 2, k)])) for k in range(4)])
                            sq = tb.get(n)
                            P.act(sq, pb, AF.Square)
                            sqs.append(sq)
                            P.act(nb.v(4 + m, b), pb, AF.Identity, scale=g4(l, 5, m))
                        sbk = s1b[b]
                        P.mm(sbk, [(ones_v, sq) for sq in sqs])
                        r = rstd_from_stat(sbk, b, 1.0 / DP, RMS_EPS)
                        for m in range(4):
                            P.tt("pool" if m % 2 else "dve", nb.v(4 + m, b), nb.v(4 + m, b), r, ALU.mult)

                    def st_O(b):
                        for m in range(DC):
                            so = so0 if m < 4 else so1
                            pb = oring.get(b)
                            P.mm(pb, [(so.lhsT(0, k, (m % 4) * 128), nb.v(k, b)) for k in range(DC)])
                            P.tt("dve", xs.v(m, b), pb, xs.v(m, b), ALU.add)

                    def st_N2(b):
                        rmsnorm_blk(lambda c: g8(l, 1, c), b, sbk=bank(2, b))

                    st_E(0); st_A(0); st_Pw(0); st_E(1); st_O(0); st_N2(0); st_A(1); st_Pw(1); st_E(2)
                    st_O(1); st_N2(1); st_A(2); st_Pw(2); st_O(2); st_N2(2)

                with ExitStack() as fs:
                    hb = Plain(sb("hb", [128, 12, T], BF16, fs), "hb")
                    gext = Ext(sb("gext", [128, 2, HF + PT + SS * (HF + ST)], F32, fs), "gext", HF, phys=2)
                    stfs = sb("stfs", [128, NJ * SS * HF], F32, fs)
                    ofs = sb("ofs", [128, NJ * SS * HF], F32, fs)
                    ofp = sb("ofp", [128, NJ * HF], F32, fs)
                    P.alias_fence(["hb", "gext", "stfs", "ofs", "ofp"])

                    P.dma("sp", V(stfs[:], ["stfs"]), V(stf[l, ps_i].rearrange("p j w -> p (j w)"), []))

                    for (j0, nj) in FFN_GROUPS:
                        for j in range(j0, j0 + nj, 2):
                            sl = W.next((l, "fi%d" % j))
                            for jj in range(2):
                                jc = j + jj
                                jl = jc - j0
                                wc = lambda t_: col(wfdw, "wfdw", (l * NJ + jc) * 4 + t_)
                                if ps_i == 0:
                                    P.memset("dve", gext.pcols(jc, 0, HF), 0.0)
                                else:
                                    P.copy("dve", gext.pcols(jc, 0, HF),
                                           V(hp_ffn[:, (l * NJ + jc) * HF:(l * NJ + jc + 1) * HF], ["hp_ffn"]))
                                hsrc = stfs[:, jc * SS * HF:(jc + 1) * SS * HF].rearrange("p (s t) -> p s t", t=HF)
                                P.copy("dve", gext.samp(jc, 0, HF), V(hsrc, ["stfs"]))
                                for b in range(NBLK):
                                    n = blk(b)[1]
                                    three = (b == 2)
                                    pg = ring8.get(b)
                                    P.mm(pg, [(sl.lhsT(0, k, jj * 128), nb.v(k, b)) for k in range(DC)])
                                    pu = ring8.get(b)
                                    P.mm(pu, [(sl.lhsT(1, k, jj * 128), nb.v(k, b)) for k in range(DC)])
                                    pg3 = as3(pg) if three else pg
                                    pu3 = as3(pu) if three else pu
                                    P.copy("act", gext.new(jc, b), pg3)
                                    t0 = tf.get(n, three)
                                    P.act(t0, pg3, AF.Identity, scale=wc(2), bias=wc(3))
                                    t1 = tf.get(n, three)
                                    P.stt(t1, gext.new(jc, b, shift=1), wc(1), t0, ALU.mult, ALU.add)
                                    t2 = tf.get(n, three)
                                    P.stt(t2, gext.new(jc, b, shift=2), wc(0), t1, ALU.mult, ALU.add)
                                    t3 = tf.get(n, three)
                                    P.tt("dve", t3, pu3, t2, ALU.mult)
                                    sg = tf.get(n, three)
                                    P.act(sg, t2, AF.Sigmoid)
                                    P.tt("dve", hb.v(jl, b, three=three), t3, sg, ALU.mult)
                                tailv = gext.pcols(jc, HF + PT - HF, HF)
                                if ps_i == 0:
                                    P.copy("dve", V(hp_ffn[:, (l * NJ + jc) * HF:(l * NJ + jc + 1) * HF], ["hp_ffn"]), tailv)
                                else:
                                    P.copy("dve", V(ofp[:, jc * HF:(jc + 1) * HF], ["ofp"]), tailv)
                                odst = ofs[:, jc * SS * HF:(jc + 1) * SS * HF].rearrange("p (s t) -> p s t", t=HF)
                                P.copy("dve", V(odst, ["ofs"]), gext.samp(jc, HF + ST - HF, HF))
                        for q in range(D // FO_W):
                            so = W.next((l, "fo%d_%d" % (j0, q)))
                            for mloc in range(FO_W // 128):
                                m = q * (FO_W // 128) + mloc
                                for b in range(NBLK):
                                    pb = ring6.get(b)
                                    P.mm(pb, [(so.lhsT(0, jl, mloc * 128), hb.v(jl, b)) for jl in range(nj)])
                                    P.tt("dve", xs.v(m, b), pb, xs.v(m, b), ALU.add)
                    P.dma("sp", V(o_ffn_s[l, ps_i], []), V(ofs[:], ["ofs"]), is_output=True)
                    if ps_i == 1:
                        P.dma("sp", V(o_ffn_p[l], []), V(ofp[:], ["ofp"]), is_output=True)

                with ExitStack() as es:
                    ef = sb("ef", [128, DC, 512], F32, es)
                    P.alias_fence(["ef"])
                    for c in range(DC):
                        for b in range(NBLK):
                            P.copy("act", nb.v(c, b), xs.v(c, b))
                    sple, sg0_, sg1_ = W.acquire([(l, "ple"), (l, "g0"), (l, "g1")])
                    sgs = [sg0_, sg1_]
                    pend_norm = []
                    for b in range(NBLK):
                        n = blk(b)[1]
                        for m in range(DC):
                            pb = ring6.get(b)
                            P.mm(pb, [(sple.lhsT(0, k, m * 128), pTb.v(k, b)) for k in range(2)])
                            P.copy("act", V(ef[:, m, 0:n], [("ef", m)]), pb)
                        while pend_norm:
                            pend_norm.pop(0)()
                        sqs = []
                        for m in range(DC):
                            sq = tb.get(n)
                            square_x(sq, V(ef[:, m, 0:n], [("ef", m)]), m)
                            sqs.append(sq)
                        r = None
                        for m in range(DC):
                            pb = ring6.get(b)
                            P.mm(pb, [(sgs[m // 4].lhsT(0, k, (m % 4) * 128), nb.v(k, b)) for k in range(DC)])
                            if m == 0:
                                sbk = statr.get(b)
                                P.mm(sbk, [(ones_v, sq) for sq in sqs])
                                r = rstd_from_stat(sbk, b, 1.0 / D, RMS_EPS)
                                for m2 in range(DC):
                                    efv = V(ef[:, m2, 0:n], [("ef", m2)])
                                    P.tt("pool", efv, efv, r, ALU.mult)
                            sg = tf.get(n)
                            P.act(sg, pb, AF.Sigmoid)
                            t1 = tf.get(n)
                            P.stt(t1, V(ef[:, m, 0:n], [("ef", m)]), g8(l, 2, m), sg, ALU.mult, ALU.mult)
                            P.tt("dve", xs.v(m, b), xs.v(m, b), t1, ALU.add)
                        if l < L - 1:
                            pend_norm.append(rmsnorm_blk(lambda c, l=l: g8(l + 1, 0, c), b, split=True))
                        else:
                            pend_norm.append(rmsnorm_blk(lambda c: g8(0, 3, c), b, inplace=True, split=True))
                    while pend_norm:
                        pend_norm.pop(0)()

            P.dma("sp", V(yT[ps_i], []), V(xs.t[:], [("xs", c, b) for c in range(DC) for b in range(NBLK)]),
                  is_output=True)

        P.finish()
        build_nc.ninst = dict(P.ninst)
    return nc


def _fmaj(a, nchunk):
    s = a.shape
    a = a.reshape(s[:-1] + (nchunk, 128))
    nd = a.ndim
    perm = list(range(nd - 3)) + [nd - 1, nd - 2, nd - 3]
    return np.ascontiguousarray(a.transpose(perm))


_NC_CACHE = {}


def kernel(x_prompt, x_sample, state_pool, state_conv, state_ffn, p_prompt, p_sample,
           g_mix, w_in, w_pool, pool_scale, w_dw, b_dw, ln_g, ln_b, w_pw,
           g_out_a, g_out_b, w_out, g_ffn, w_ffn_in, w_ffn_dw, b_ffn_dw, w_ffn_out,
           w_ple, g_ple, w_ple_gate, final_norm):
    f = lambda a: np.ascontiguousarray(np.asarray(a, dtype=np.float32))
    x_prompt, x_sample, state_pool, state_conv, state_ffn, p_prompt, p_sample = map(
        f, (x_prompt, x_sample, state_pool, state_conv, state_ffn, p_prompt, p_sample))
    ncores = 8
    if "nc" not in _NC_CACHE:
        _NC_CACHE["nc"] = build_nc()
    nc = _NC_CACHE["nc"]

    p8 = np.zeros((L, 4, 8, 128), np.float32)
    p8[:, 0] = f(g_mix).reshape(L, 8, 128)
    p8[:, 1] = f(g_ffn).reshape(L, 8, 128)
    p8[:, 2] = f(g_ple).reshape(L, 8, 128)
    p8[:, 3] = f(final_norm).reshape(1, 8, 128)
    p8 = np.ascontiguousarray(p8.transpose(3, 0, 1, 2).reshape(128, L * 4 * 8))
    p4 = np.stack([f(pool_scale), f(b_dw), f(ln_g), f(ln_b), f(g_out_a), f(g_out_b)], 1).reshape(L, 6, 4, 128)
    p4 = np.ascontiguousarray(p4.transpose(3, 0, 1, 2).reshape(128, L * 6 * 4))
    wdw = f(w_dw).reshape(L, 31, 4, 128).transpose(3, 0, 2, 1)
    wdw = np.ascontiguousarray(wdw.reshape(128, L * 4 * 31))
    wf = np.concatenate([f(w_ffn_dw), f(b_ffn_dw)[:, None, :]], 1).reshape(L, 4, NJ, 128).transpose(3, 0, 2, 1)
    wf = np.ascontiguousarray(wf.reshape(128, L * NJ * 4))
    cst = np.zeros((128, 144), np.float32)
    cst[:, :128] = np.eye(128, dtype=np.float32)
    cst[:, 128:144] = (1.0 / np.arange(1, 17, dtype=np.float32))[None, :]
    shared = {"p8": p8, "p4": p4, "wdw": wdw, "wfdw": wf, "cst": cst,
              "w_in": f(w_in), "w_pool": f(w_pool), "w_pw": f(w_pw), "w_out": f(w_out),
              "w_ffn_in": f(w_ffn_in), "w_ffn_out": f(w_ffn_out), "w_ple": f(w_ple),
              "w_ple_gate": f(w_ple_gate)}

    in_maps = []
    for c in range(ncores):
        xT = np.zeros((NPASS, DC, 128, T), np.float32)
        pT = np.zeros((L, NPASS, 2, 128, T), np.float32)
        stp = np.zeros((L, NPASS, 128, 4, SS, HP + ST), np.float32)
        stc = np.zeros((L, NPASS, 128, 4, SS, HC + ST), np.float32)
        stf = np.zeros((L, NPASS, 128, NJ, SS, HF), np.float32)
        for h in range(NPASS):
            sq = slice(16 * c + SS * h, 16 * c + SS * (h + 1))
            xp = x_prompt[c, h * PT:(h + 1) * PT]
            xsm = x_sample[sq].reshape(SS * ST, D)
            xT[h] = np.concatenate([xp, xsm], 0).T.reshape(DC, 128, T)
            for l in range(L):
                pp = np.concatenate([p_prompt[l, c, h * PT:(h + 1) * PT], p_sample[l, sq].reshape(SS * ST, DPLE)], 0)
                pT[l, h] = pp.T.reshape(2, 128, T)
                stp[l, h, :, :, :, :HP] = _fmaj(state_pool[l, sq].reshape(SS * HP, DP), 4).reshape(128, 4, SS, HP)
                stc[l, h, :, :, :, :HC] = _fmaj(state_conv[l, sq].reshape(SS * HC, DP), 4).reshape(128, 4, SS, HC)
                stf[l, h] = _fmaj(state_ffn[l, sq].reshape(SS * HF, DFF), NJ).reshape(128, NJ, SS, HF)
        m = dict(shared)
        m.update({"xT": xT, "pT": pT,
                  "stp": stp.reshape(L, NPASS, 128, 4, SS * (HP + ST)),
                  "stc": stc.reshape(L, NPASS, 128, 4, SS * (HC + ST)),
                  "stf": stf.reshape(L, NPASS, 128, NJ, SS * HF)})
        in_maps.append(m)

    res = run_bass_kernel_spmd(nc, in_maps, core_ids=list(range(ncores)))

    B, S = 8, 2048
    y_prompt = np.zeros((B, S, D), np.float32)
    y_sample = np.zeros((128, ST, D), np.float32)
    pool_p = np.zeros((L, B, HP, DP), np.float32)
    conv_p = np.zeros((L, B, HC, DP), np.float32)
    ffn_p = np.zeros((L, B, HF, DFF), np.float32)
    pool_s = np.zeros((L, 128, HP, DP), np.float32)
    conv_s = np.zeros((L, 128, HC, DP), np.float32)
    ffn_s = np.zeros((L, 128, HF, DFF), np.float32)
    for c in range(ncores):
        r = res.results[c]
        yT = np.asarray(r["yT"]).reshape(NPASS, 128, DC, T)
        for h in range(NPASS):
            sq = slice(16 * c + SS * h, 16 * c + SS * (h + 1))
            yy = yT[h].transpose(2, 1, 0).reshape(T, D)
            y_prompt[c, h * PT:(h + 1) * PT] = yy[:PT]
            y_sample[sq] = yy[PT:].reshape(SS, ST, D)
            ops_ = np.asarray(r["o_pool_s"]).reshape(L, NPASS, 128, 4, SS, HP + ST)[:, h]
            pool_s[:, sq] = ops_[..., ST:].transpose(0, 3, 4, 2, 1).reshape(L, SS, HP, DP)
            ocs = np.asarray(r["o_conv_s"]).reshape(L, NPASS, 128, 4, SS, HC + ST)[:, h]
            conv_s[:, sq] = ocs[..., ST:].transpose(0, 3, 4, 2, 1).reshape(L, SS, HC, DP)
            ofs = np.asarray(r["o_ffn_s"]).reshape(L, NPASS, 128, NJ, SS, HF)[:, h]
            ffn_s[:, sq] = ofs.transpose(0, 3, 4, 2, 1).reshape(L, SS, HF, DFF)
        pool_p[:, c] = np.asarray(r["o_pool_p"]).reshape(L, 128, 4, HP).transpose(0, 3, 2, 1).reshape(L, HP, DP)
        conv_p[:, c] = np.asarray(r["o_conv_p"]).reshape(L, 128, 4, HC).transpose(0, 3, 2, 1).reshape(L, HC, DP)
        ffn_p[:, c] = np.asarray(r["o_ffn_p"]).reshape(L, 128, NJ, HF).transpose(0, 3, 2, 1).reshape(L, HF, DFF)
    return (y_prompt, y_sample, pool_p, conv_p, ffn_p, pool_s, conv_s, ffn_s)
```

```python
import numpy as np
from contextlib import ExitStack
import concourse.bass as bass
import concourse.mybir as mybir
from concourse.bass_utils import run_bass_kernel_spmd

F32 = mybir.dt.float32
BF16 = mybir.dt.bfloat16
AF = mybir.ActivationFunctionType
ALU = mybir.AluOpType

L = 4
D = 1024
DC = 8
DP = 512
DFF = 2816
NJ = 22
DPLE = 256
PT = 1024
SS = 8
ST = 8
T = PT + SS * ST
NPASS = 2
NBLK = 3
HP, HC, HF = 15, 30, 2
RMS_EPS = 1e-6
LN_EPS = 1e-5
NSLOT = 4
SLOT_EL = 4096
FFN_GROUPS = [(0, 12), (12, 10)]
FO_W = 256

DEBUG = False
E_MIX = "dve"
E_DIAG = "pool"


def blk(b):
    return (0, 512) if b == 0 else ((512, 512) if b == 1 else (1024, 64))


class V:
    __slots__ = ("ap", "keys")

    def __init__(self, ap, keys):
        self.ap = ap
        self.keys = list(keys)


def as3(v):
    return V(v.ap.rearrange("p (s t) -> p s t", t=ST), v.keys)


class _Eng:
    def __init__(self, name, h, sem):
        self.name = name
        self.h = h
        self.sem = sem
        self.count = 0
        self.waited = {}


class Prog:
    def __init__(self, nc, stack, n_dma_sems=32):
        self.nc = nc
        self.e = {}
        for name, h in (("pe", nc.tensor), ("act", nc.scalar), ("dve", nc.vector),
                        ("pool", nc.gpsimd), ("sp", nc.sync)):
            sem = stack.enter_context(nc.semaphore("sem_" + name))
            self.e[name] = _Eng(name, h, sem)
        self.dsem = [stack.enter_context(nc.semaphore("dsem%d" % i)) for i in range(n_dma_sems)]
        self.dcnt = [0] * n_dma_sems
        self.dnext = {"sp": 0, "pool": n_dma_sems // 2}
        self.drange = {"sp": (0, n_dma_sems // 2), "pool": (n_dma_sems // 2, n_dma_sems)}
        self.res = {}
        self.fence = []
        self.scoped = set()
        self.out_toks = []
        self.ninst = {k: 0 for k in self.e}

    def _need(self, e, tok):
        sem, val = tok
        if e.waited.get(sem.num, 0) >= val:
            return
        e.h.wait_ge(sem, val)
        e.waited[sem.num] = val

    def alias_fence(self, names):
        f = [(o.sem, o.count) for o in self.e.values() if o.count > 0]
        f += [(self.dsem[k], self.dcnt[k]) for k in range(len(self.dsem)) if self.dcnt[k] > 0]
        self.fence = f
        self.scoped = set(names)
        for k in [k for k in self.res if (k[0] if isinstance(k, tuple) else k) in self.scoped]:
            del self.res[k]

    def _deps(self, e, reads, writes):
        toks = []
        for k in list(reads) + list(writes):
            if k not in self.res and (k[0] if isinstance(k, tuple) else k) in self.scoped:
                toks.extend(self.fence)
                break
        for k in reads:
            r = self.res.get(k)
            if r is not None and r[0] is not None:
                toks.append(r[0])
        for k in writes:
            r = self.res.get(k)
            if r is not None:
                if r[0] is not None:
                    toks.append(r[0])
                toks.extend(r[1].values())
        for t in toks:
            if t[0] is e.sem and e.name == "pe":
                continue
            self._need(e, t)

    def _record(self, rkey, tok, reads, writes):
        for k in reads:
            r = self.res.get(k)
            if r is None:
                r = [None, {}]
                self.res[k] = r
            r[1][rkey] = tok
        for k in writes:
            self.res[k] = [tok, {}]

    def op(self, ename, fn, reads, writes, signal=True):
        e = self.e[ename]
        self._deps(e, reads, writes)
        inst = fn(e.h)
        self.ninst[ename] += 1
        if signal:
            e.count += 1
            inst.then_inc(e.sem, 1)
            tok = (e.sem, e.count)
        else:
            tok = (e.sem, e.count + 1)
        self._record(ename, tok, reads, writes)
        return tok

    def dma(self, qname, out, in_, is_output=False):
        e = self.e[qname]
        reads = in_.keys
        writes = out.keys
        self._deps(e, reads, writes)
        k = self.dnext[qname]
        lo, hi = self.drange[qname]
        self.dnext[qname] = lo + (k + 1 - lo) % (hi - lo)
        sem = self.dsem[k]
        if self.dcnt[k] > 0:
            self._need(e, (sem, self.dcnt[k]))
        e.h.dma_start(out=out.ap, in_=in_.ap).then_inc(sem, 16)
        self.ninst[qname] += 1
        self.dcnt[k] += 16
        tok = (sem, self.dcnt[k])
        self._record(("dma", k), tok, reads, writes)
        if is_output:
            self.out_toks.append(tok)
        return tok

    def barrier(self):
        sp = self.e["sp"]
        for k in range(len(self.dsem)):
            if self.dcnt[k] > 0:
                self._need(sp, (self.dsem[k], self.dcnt[k]))
        for name in ("pe", "act", "dve", "pool"):
            o = self.e[name]
            if o.count > 0:
                self._need(sp, (o.sem, o.count))
        sp.count += 1
        sp.h.nop().then_inc(sp.sem, 1)
        for name in ("pe", "act", "dve", "pool"):
            self._need(self.e[name], (sp.sem, sp.count))
        self.res.clear()

    def finish(self):
        self.barrier()

    @staticmethod
    def _sk(x):
        return x.keys if isinstance(x, V) else []

    @staticmethod
    def _sa(x):
        return x.ap if isinstance(x, V) else x

    def act(self, out, in_, func, scale=1.0, bias=0.0):
        rd = in_.keys + self._sk(scale) + self._sk(bias)
        sc, bi = self._sa(scale), self._sa(bias)
        return self.op("act", lambda h: h.activation(out=out.ap, in_=in_.ap, func=func, scale=sc, bias=bi),
                       rd, out.keys)

    def tt(self, eng, out, in0, in1, op):
        return self.op(eng, lambda h: h.tensor_tensor(out=out.ap, in0=in0.ap, in1=in1.ap, op=op),
                       in0.keys + in1.keys, out.keys)

    def ts(self, eng, out, in0, s1, op0, s2=None, op1=None):
        rd = in0.keys + self._sk(s1) + self._sk(s2)
        a1, a2 = self._sa(s1), self._sa(s2)
        if op1 is None:
            return self.op(eng, lambda h: h.tensor_scalar(out=out.ap, in0=in0.ap, scalar1=a1, scalar2=None, op0=op0),
                           rd, out.keys)
        return self.op(eng, lambda h: h.tensor_scalar(out=out.ap, in0=in0.ap, scalar1=a1, scalar2=a2,
                                                      op0=op0, op1=op1), rd, out.keys)

    def stt(self, out, in0, scalar, in1, op0, op1):
        rd = in0.keys + in1.keys + self._sk(scalar)
        sc = self._sa(scalar)
        return self.op("dve", lambda h: h.scalar_tensor_tensor(out=out.ap, in0=in0.ap, scalar=sc, in1=in1.ap,
                                                               op0=op0, op1=op1), rd, out.keys)

    def copy(self, eng, out, in_):
        if eng == "act":
            return self.act(out, in_, AF.Copy)
        return self.op(eng, lambda h: h.tensor_copy(out=out.ap, in_=in_.ap), in_.keys, out.keys)

    def memset(self, eng, out, val):
        return self.op(eng, lambda h: h.memset(out.ap, val), [], out.keys)

    def mm(self, out, pairs):
        n = len(pairs)
        tok = None
        for i, (l, r) in enumerate(pairs):
            tok = self.op("pe", lambda h, l=l, r=r, i=i: h.matmul(out.ap, l.ap, r.ap, start=(i == 0),
                                                                    stop=(i == n - 1)),
                          l.keys + r.keys, out.keys, signal=(i == n - 1))
        return tok


class Plain:
    def __init__(self, t, name):
        self.t = t
        self.name = name

    def v(self, c, b, three=False):
        lo, n = blk(b)
        v = V(self.t[:, c, lo:lo + n], [(self.name, c, b)])
        if three and b == 2:
            v = as3(v)
        return v

    def cols(self, c, lo, n):
        keys = []
        for b in range(NBLK):
            blo, bn = blk(b)
            if lo < blo + bn and lo + n > blo:
                keys.append((self.name, c, b))
        return V(self.t[:, c, lo:lo + n], keys)


class Ext:
    def __init__(self, t, name, H, phys=None):
        self.t = t
        self.name = name
        self.H = H
        self.W = H + PT + SS * (H + ST)
        self.phys = phys

    def _c(self, c):
        return c if self.phys is None else c % self.phys

    def _regions(self, c, lo, hi):
        H = self.H
        keys = []
        if lo < H + 512 and hi > 0:
            keys.append((self.name, self._c(c), 0))
        if lo < H + PT and hi > H + 512:
            keys.append((self.name, self._c(c), 1))
        if hi > H + PT:
            keys.append((self.name, self._c(c), 2))
        return keys

    def new(self, c, b, shift=0):
        H = self.H
        cc = self._c(c)
        if b < 2:
            lo = H + 512 * b - shift
            return V(self.t[:, cc, lo:lo + 512], self._regions(c, lo, lo + 512))
        return self.samp(c, H - shift, ST)

    def samp(self, c, lo, n):
        H = self.H
        cc = self._c(c)
        ap = self.t[:, cc, H + PT:self.W].rearrange("p (s t) -> p s t", t=H + ST)[:, :, lo:lo + n]
        return V(ap, [(self.name, cc, 2)])

    def samp_flat(self, c):
        return V(self.t[:, self._c(c), self.H + PT:self.W], [(self.name, self._c(c), 2)])

    def pcols(self, c, lo, n):
        return V(self.t[:, self._c(c), lo:lo + n], self._regions(c, lo, lo + n))


class TilePool:
    def __init__(self, nc, st, name, n, width, dt):
        self.ts = [st.enter_context(nc.sbuf_tensor("%s%d" % (name, i), [128, width], dt)) for i in range(n)]
        self.name = name
        self.i = 0

    def get(self, n, three=False):
        i = self.i
        self.i = (self.i + 1) % len(self.ts)
        v = V(self.ts[i][:, 0:n], [(self.name, i)])
        if three:
            v = as3(v)
        return v


class WStream:
    def __init__(self, P, slots, plan):
        self.P = P
        self.slots = slots
        self.plan = plan
        self.issued = 0
        self.cur = 0

    def _issue(self, i):
        s = i % len(self.slots)
        off = 0
        for (ap, k, n) in self.plan[i]:
            dst = self.slots[s][:, off:off + k * n].rearrange("p (k n) -> p k n", n=n)
            self.P.dma("pool", V(dst, [("w", s)]), V(ap, []))
            off += k * n
        assert off <= SLOT_EL

    def acquire(self, tag_list):
        j0 = self.cur
        n = len(tag_list)
        assert n <= len(self.slots)
        for i, tag in enumerate(tag_list):
            assert self.plan_tags[j0 + i] == tag, (self.plan_tags[j0 + i], tag)
        self.cur += n
        while self.issued < min(len(self.plan), j0 + len(self.slots)):
            self._issue(self.issued)
            self.issued += 1
        return [Slab(self.slots[j % len(self.slots)], j % len(self.slots), self.plan[j]) for j in range(j0, j0 + n)]

    def next(self, tag):
        return self.acquire([tag])[0]


class Slab:
    def __init__(self, t, s, pieces):
        self.t = t
        self.s = s
        self.offs = []
        off = 0
        for (_, k, n) in pieces:
            self.offs.append((off, k, n))
            off += k * n

    def lhsT(self, piece, k, col0, ncol=128):
        off, K, n = self.offs[piece]
        o = off + k * n + col0
        return V(self.t[:, o:o + ncol], [("w", self.s)])


def build_nc():
    nc = bass.Bass("TRN2", target_bir_lowering=False)

    def din(name, shape):
        return nc.dram_tensor(name, list(shape), F32, kind="ExternalInput").ap()

    def dout(name, shape):
        return nc.dram_tensor(name, list(shape), F32, kind="ExternalOutput").ap()

    xT = din("xT", [NPASS, DC, 128, T])
    pTd = din("pT", [L, NPASS, 2, 128, T])
    stp = din("stp", [L, NPASS, 128, 4, SS * (HP + ST)])
    stc = din("stc", [L, NPASS, 128, 4, SS * (HC + ST)])
    stf = din("stf", [L, NPASS, 128, NJ, SS * HF])
    p8d = din("p8", [128, L * 4 * 8])
    p4d = din("p4", [128, L * 6 * 4])
    wdwd = din("wdw", [128, L * 4 * 31])
    wfdwd = din("wfdw", [128, L * NJ * 4])
    cstd = din("cst", [128, 128 + 16])
    w_in = din("w_in", [L, D, 1536])
    w_pool = din("w_pool", [L, 4, 128, 128])
    w_pw = din("w_pw", [L, DP, DP])
    w_out = din("w_out", [L, D, D])
    w_ffn_in = din("w_ffn_in", [L, D, 2 * DFF])
    w_ffn_out = din("w_ffn_out", [L, DFF, D])
    w_ple = din("w_ple", [L, DPLE, D])
    w_gate = din("w_ple_gate", [L, D, D])

    yT = dout("yT", [NPASS, 128, DC, T])
    o_pool_s = dout("o_pool_s", [L, NPASS, 128, 4, SS * (HP + ST)])
    o_pool_p = dout("o_pool_p", [L, 128, 4 * HP])
    o_conv_s = dout("o_conv_s", [L, NPASS, 128, 4 * SS * (HC + ST)])
    o_conv_p = dout("o_conv_p", [L, 128, 4 * HC])
    o_ffn_s = dout("o_ffn_s", [L, NPASS, 128, NJ * SS * HF])
    o_ffn_p = dout("o_ffn_p", [L, 128, NJ * HF])

    plan, tags = [], []

    def kp(ap):
        return ap.rearrange("(k p) n -> p k n", p=128)

    for ps_ in range(NPASS):
        for l in range(L):
            for nm, c0 in (("u1", 512), ("u2", 1024), ("a", 0)):
                plan.append([(kp(w_in[l, :, c0:c0 + 512]), 8, 512)]); tags.append((l, nm))
            plan.append([(kp(w_pw[l]), 4, 512)]); tags.append((l, "pw"))
            for h in range(2):
                plan.append([(kp(w_out[l, :, h * 512:(h + 1) * 512]), 8, 512)]); tags.append((l, "o%d" % h))
            for (j0, nj) in FFN_GROUPS:
                for j in range(j0, j0 + nj, 2):
                    plan.append([(kp(w_ffn_in[l, :, j * 128:j * 128 + 256]), 8, 256),
                                 (kp(w_ffn_in[l, :, DFF + j * 128:DFF + j * 128 + 256]), 8, 256)])
                    tags.append((l, "fi%d" % j))
                for q in range(D // FO_W):
                    plan.append([(kp(w_ffn_out[l, j0 * 128:(j0 + nj) * 128, q * FO_W:(q + 1) * FO_W]), nj, FO_W)])
                    tags.append((l, "fo%d_%d" % (j0, q)))
            plan.append([(kp(w_ple[l]), 2, 1024)]); tags.append((l, "ple"))
            for h in range(2):
                plan.append([(kp(w_gate[l, :, h * 512:(h + 1) * 512]), 8, 512)]); tags.append((l, "g%d" % h))

    with ExitStack() as st:
        P = Prog(nc, st)

        uniq = [0]

        def sb(name, shape, dt, stack=st):
            uniq[0] += 1
            return stack.enter_context(nc.sbuf_tensor("%s_%d" % (name, uniq[0]), list(shape), dt))

        xs = Plain(sb("xs", [128, DC, T], F32), "xs")
        nb = Plain(sb("nb", [128, DC, T], BF16), "nb")
        pTb = Plain(sb("pTb", [128, 2, T], BF16), "pTb")
        slots = [sb("wslot%d" % i, [128, SLOT_EL], BF16) for i in range(NSLOT)]
        W = WStream(P, slots, plan)
        W.plan_tags = tags
        p8 = sb("p8s", [128, L * 4 * 8], F32)
        p4 = sb("p4s", [128, L * 6 * 4], F32)
        wdw = sb("wdws", [128, L * 4 * 31], F32)
        wfdw = sb("wfdws", [128, L * NJ * 4], F32)
        cst = sb("csts", [128, 128 + 16], F32)
        identb = sb("identb", [128, 128], BF16)
        onesb = sb("onesb", [128, 128], BF16)
        wpool_all = sb("wpool_all", [128, L, 4 * 128], BF16)
        epsc = sb("epsc", [128, 2], F32)
        psga = sb("psga", [128, L * 4], F32)
        hp_pool = sb("hp_pool", [128, L * 4 * HP], F32)
        hp_conv = sb("hp_conv", [128, L * 4 * HC], F32)
        hp_ffn = sb("hp_ffn", [128, L * NJ * HF], F32)
        ps = st.enter_context(nc.psum_tensor("ps", [128, 8, 512], F32))
        tf = TilePool(nc, st, "tf", 6, 512, F32)
        tr = TilePool(nc, st, "tr", 4, 512, F32)
        tb = TilePool(nc, st, "tb", 8, 512, BF16)

        def bank(i, b, three=False):
            n = blk(b)[1]
            v = V(ps[:, i, 0:n], [("ps", i)])
            if three and b == 2:
                v = as3(v)
            return v

        class Ring:
            def __init__(self, banks):
                self.banks = banks
                self.i = 0

            def get(self, b, three=False):
                i = self.banks[self.i]
                self.i = (self.i + 1) % len(self.banks)
                return bank(i, b, three)

        ring6 = Ring([0, 1, 2, 3, 4, 5])
        ring8 = Ring([0, 1, 2, 3, 4, 5, 6, 7])
        statr = Ring([6, 7])

        def col(t, name, idx):
            return V(t[:, idx:idx + 1], [name])

        def g8(l, kind, c):
            return col(p8, "p8", (l * 4 + kind) * 8 + c)

        def g4(l, kind, c):
            return col(p4, "p4", (l * 6 + kind) * 4 + c)

        ones_v = V(onesb[:], ["onesb"])

        P.dma("sp", V(p8[:], ["p8"]), V(p8d, []))
        P.dma("sp", V(p4[:], ["p4"]), V(p4d, []))
        P.dma("sp", V(wdw[:], ["wdw"]), V(wdwd, []))
        P.dma("sp", V(wfdw[:], ["wfdw"]), V(wfdwd, []))
        P.dma("sp", V(cst[:], ["cst"]), V(cstd, []))
        P.copy("dve", V(identb[:], ["identb"]), V(cst[:, 0:128], ["cst"]))
        P.memset("dve", V(onesb[:], ["onesb"]), 1.0)
        for l_ in range(L):
            P.dma("pool", V(wpool_all[:, l_, :].rearrange("p (g c) -> p g c", c=128), ["wpool_all"]),
                  V(w_pool[l_].rearrange("g ci co -> ci g co"), []))
        P.memset("dve", V(epsc[:, 0:1], ["epsc"]), RMS_EPS)
        P.memset("dve", V(epsc[:, 1:2], ["epsc"]), LN_EPS)
        for l in range(L):
            P.tt("dve", V(psga[:, l * 4:(l + 1) * 4], ["psga"]),
                 V(p4[:, (l * 6 + 0) * 4:(l * 6 + 0) * 4 + 4], ["p4"]),
                 V(p4[:, (l * 6 + 4) * 4:(l * 6 + 4) * 4 + 4], ["p4"]), ALU.mult)

        def rstd_from_stat(sbank, b, inv_n, eps):
            n = blk(b)[1]
            t1 = tf.get(n)
            P.act(t1, sbank, AF.Ln, scale=inv_n, bias=col(epsc, "epsc", 0 if eps == RMS_EPS else 1))
            r = tr.get(n)
            P.act(r, t1, AF.Exp, scale=-0.5)
            return r

        def square_x(sq, xv, c):
            if c in (0, 4):
                P.tt("dve", sq, xv, xv, ALU.mult)
            else:
                P.act(sq, xv, AF.Square)

        def rmsnorm_blk(gfn, b, sbk=None, inplace=False, split=False):
            n = blk(b)[1]
            sqs = []
            for c in range(DC):
                sq = tb.get(n)
                square_x(sq, xs.v(c, b), c)
                sqs.append(sq)

            def rest(sbk=sbk):
                if sbk is None:
                    sbk = statr.get(b)
                P.mm(sbk, [(ones_v, sq) for sq in sqs])
                r = rstd_from_stat(sbk, b, 1.0 / D, RMS_EPS)
                for c in range(DC):
                    P.stt(xs.v(c, b) if inplace else nb.v(c, b), xs.v(c, b), gfn(c), r, ALU.mult, ALU.mult)

            if split:
                return rest
            rest()

        for ps_i in range(NPASS):
            for c in range(DC):
                P.dma("sp", V(xs.t[:, c, :], [("xs", c, b) for b in range(NBLK)]), V(xT[ps_i, c], []))

            for l in range(L):
                P.dma("pool", V(pTb.t[:], [("pTb", k, b) for k in range(2) for b in range(NBLK)]),
                      V(pTd[l, ps_i].rearrange("k p t -> p k t"), []))

                with ExitStack() as ms:
                    cext = Ext(sb("cext", [128, 4, HC + PT + SS * (HC + ST)], BF16, ms), "cext", HC)
                    cexs = sb("cexs", [128, 4 * SS * (HC + ST)], F32, ms)
                    cexp = sb("cexp", [128, 4 * HC], F32, ms)
                    aext = Ext(sb("aext", [128, 2, HP + PT + SS * (HP + ST)], F32, ms), "aext", HP, phys=2)
                    WA = HP + PT + SS * (HP + ST)
                    tmpA = Ext(sb("tmpA", [128, 1, WA], F32, ms), "tmpA", HP)
                    tmpB = Ext(sb("tmpB", [128, 1, WA], F32, ms), "tmpB", HP)
                    zb = Plain(sb("zb", [128, 4, T], BF16, ms), "zb")
                    cf = Plain(sb("cf", [128, 4, T], F32, ms), "cf")
                    diag = sb("diag", [128, 31, 128], BF16, ms)
                    opp = sb("opp", [128, 4 * HP], F32, ms)
                    actb = [sb("actb%d" % i, [128, 4, 512], BF16, ms) for i in range(2)]
                    P.alias_fence(["cext", "cexs", "cexp", "aext", "tmpA", "tmpB", "zb", "cf", "diag", "opp", "actb"])

                    if l == 0:
                        for b in range(NBLK):
                            rmsnorm_blk(lambda c: g8(0, 0, c), b)

                    CW = SS * (HC + ST)
                    P.dma("sp", V(cexs[:], ["cexs"]), V(stc[l, ps_i].rearrange("p c w -> p (c w)"), []))
                    for c in range(4):
                        if ps_i == 0:
                            P.memset("dve", cext.pcols(c, 0, HC), 0.0)
                        else:
                            P.copy("dve", cext.pcols(c, 0, HC),
                                   V(hp_conv[:, (l * 4 + c) * HC:(l * 4 + c + 1) * HC], ["hp_conv"]))
                        src = cexs[:, c * CW:(c + 1) * CW].rearrange("p (s t) -> p s t", t=HC + ST)[:, :, 0:HC]
                        P.copy("dve", cext.samp(c, 0, HC), V(src, ["cexs"]))

                    su1, su2 = W.acquire([(l, "u1"), (l, "u2")])
                    for c in range(4):
                        for b in range(NBLK):
                            n = blk(b)[1]
                            three = (b == 2)
                            pg = ring6.get(b)
                            P.mm(pg, [(su2.lhsT(0, k, c * 128), nb.v(k, b)) for k in range(DC)])
                            sg = tf.get(n)
                            P.act(sg, pg, AF.Sigmoid)
                            pu = ring6.get(b)
                            P.mm(pu, [(su1.lhsT(0, k, c * 128), nb.v(k, b)) for k in range(DC)])
                            if three:
                                P.tt("dve", cext.new(c, b), as3(pu), as3(sg), ALU.mult)
                                dst = cexs[:, c * CW:(c + 1) * CW].rearrange("p (s t) -> p s t", t=HC + ST)[:, :, HC:HC + ST]
                                P.tt("dve", V(dst, ["cexs"]), as3(pu), as3(sg), ALU.mult)
                            else:
                                P.tt("dve", cext.new(c, b), pu, sg, ALU.mult)
                                if b == 1:
                                    P.tt("dve", V(cexp[:, c * HC:(c + 1) * HC], ["cexp"]),
                                         V(pu.ap[:, 512 - HC:512], pu.keys), V(sg.ap[:, 512 - HC:512], sg.keys), ALU.mult)
                    P.dma("sp", V(o_conv_s[l, ps_i], []), V(cexs[:], ["cexs"]), is_output=True)
                    if ps_i == 0:
                        P.copy("dve", V(hp_conv[:, l * 4 * HC:(l + 1) * 4 * HC], ["hp_conv"]), V(cexp[:], ["cexp"]))
                    else:
                        P.dma("sp", V(o_conv_p[l], []), V(cexp[:], ["cexp"]), is_output=True)

                    sa = W.next((l, "a"))
                    for g in range(4):
                        w = 2 << g
                        P.dma("sp", aext.samp_flat(g), V(stp[l, ps_i, :, g, :], []))
                        if ps_i == 0:
                            P.memset("dve", aext.pcols(g, 0, HP), 0.0)
                        else:
                            P.copy("dve", aext.pcols(g, 0, HP),
                                   V(hp_pool[:, (l * 4 + g) * HP:(l * 4 + g + 1) * HP], ["hp_pool"]))
                        for b in range(NBLK):
                            pa = ring6.get(b)
                            P.mm(pa, [(sa.lhsT(0, k, g * 128), nb.v(k, b)) for k in range(DC)])
                            P.copy("act", aext.new(g, b), as3(pa) if b == 2 else pa)
                        P.dma("sp", V(o_pool_s[l, ps_i, :, g, :], []), aext.samp_flat(g), is_output=True)
                        tailv = aext.pcols(g, HP + PT - HP, HP)
                        if ps_i == 0:
                            P.copy("dve", V(hp_pool[:, (l * 4 + g) * HP:(l * 4 + g + 1) * HP], ["hp_pool"]), tailv)
                        else:
                            P.copy("dve", V(opp[:, g * HP:(g + 1) * HP], ["opp"]), tailv)
                        src = aext
                        srcc = g
                        bufs = [tmpA, tmpB]
                        for j in range(g + 1):
                            d = 1 << j
                            lo = 2 * d - 1
                            dst = bufs[j % 2]
                            npc = HP + PT - lo
                            P.tt(E_MIX, dst.pcols(0, lo, npc), src.pcols(srcc, lo, npc), src.pcols(srcc, lo - d, npc), ALU.add)
                            nsc = HP + ST - lo
                            P.tt(E_MIX, dst.samp(0, lo, nsc), src.samp(srcc, lo, nsc), src.samp(srcc, lo - d, nsc), ALU.add)
                            src, srcc = dst, 0
                        inv = 1.0 / w
                        P.stt(zb.cols(g, 0, PT), src.pcols(srcc, HP, PT), inv, aext.pcols(g, HP, PT), ALU.mult, ALU.subtract)
                        P.stt(zb.v(g, 2, three=True), src.samp(srcc, HP, ST), inv, aext.samp(g, HP, ST), ALU.mult, ALU.subtract)
                        if ps_i == 0:
                            tfx = tf.get(w - 1)
                            P.tt("dve", tfx, src.pcols(srcc, HP, w - 1), V(cst[:, 128:128 + w - 1], ["cst"]), ALU.mult)
                            P.tt("dve", zb.cols(g, 0, w - 1), tfx, aext.pcols(g, HP, w - 1), ALU.subtract)
                    if ps_i == 1:
                        P.dma("sp", V(o_pool_p[l], []), V(opp[:], ["opp"]), is_output=True)

                    s1b = [bank(2, 0), bank(3, 1), bank(6, 2)]
                    s2b = [bank(4, 0), bank(5, 1), bank(7, 2)]
                    cring = Ring([0, 1])
                    idv = V(identb[:], ["identb"])
                    pend = []

                    def flush_stats():
                        while pend:
                            b_, c_, cb_, sq_ = pend.pop(0)
                            P.op("pe", lambda h, o=s1b[b_], r=cb_, c=c_: h.matmul(o.ap, onesb[:], r.ap, start=(c == 0), stop=(c == 3)),
                                 ["onesb"] + cb_.keys, s1b[b_].keys, signal=True)
                            P.op("pe", lambda h, o=s2b[b_], r=sq_, c=c_: h.matmul(o.ap, onesb[:], r.ap, start=(c == 0), stop=(c == 3)),
                                 ["onesb"] + sq_.keys, s2b[b_].keys, signal=True)

                    for c in range(4):
                        for k in range(31):
                            wv = col(wdw, "wdw", (l * 4 + c) * 31 + k)
                            dv = V(diag[:, k, :], [("diag", k)])
                            if E_DIAG == "pool":
                                P.ts("pool", dv, idv, wv, ALU.mult, 0.0, ALU.add)
                            else:
                                P.act(dv, idv, AF.Copy, scale=wv)
                        for b in range(NBLK):
                            n = blk(b)[1]
                            pc = cring.get(b, three=True)
                            P.mm(pc, [(V(diag[:, k, :], [("diag", k)]), cext.new(c, b, shift=HC - k)) for k in range(31)])
                            flush_stats()
                            pcf = V(ps[:, pc.keys[0][1], 0:n], pc.keys)
                            bia = g4(l, 1, c)
                            P.act(cf.v(c, b), pcf, AF.Identity, bias=bia)
                            sq = tb.get(n)
                            P.act(sq, pcf, AF.Square, bias=bia)
                            cb = tb.get(n)
                            P.copy("pool", cb, cf.v(c, b))
                            pend.append((b, c, cb, sq))
                    flush_stats()

                    spw, so0, so1 = W.acquire([(l, "pw"), (l, "o0"), (l, "o1")])
                    oring = Ring([4, 5, 0, 1])

                    def st_E(b):
                        n = blk(b)[1]
                        mean = tr.get(n)
                        P.ts("dve", mean, s1b[b], 1.0 / DP, ALU.mult)
                        msq = tf.get(n)
                        P.tt("dve", msq, mean, mean, ALU.mult)
                        v2 = tf.get(n)
                        P.stt(v2, s2b[b], 1.0 / DP, msq, ALU.mult, ALU.subtract)
                        sd = tf.get(n)
                        P.act(sd, v2, AF.Ln, bias=col(epsc, "epsc", 1))
                        rs = tr.get(n)
                        P.act(rs, sd, AF.Exp, scale=-0.5)
                        ab = actb[b % 2]
                        for c in range(4):
                            P.tt("dve", cf.v(c, b), cf.v(c, b), mean, ALU.subtract)
                            P.tt("dve", cf.v(c, b), cf.v(c, b), rs, ALU.mult)
                            vv = tf.get(n)
                            P.act(vv, cf.v(c, b), AF.Identity, scale=g4(l, 2, c), bias=g4(l, 3, c))
                            sg = tf.get(n)
                            P.act(sg, cf.v(c, b), AF.Sigmoid, scale=g4(l, 2, c), bias=g4(l, 3, c))
                            P.tt("pool", V(ab[:, c, 0:n], [("actb", b % 2, c)]), vv, sg, ALU.mult)

                    def st_A(b):
                        n = blk(b)[1]
                        sqs = []
                        for g in range(4):
                            pb = cring.get(b)
                            P.mm(pb, [(V(wpool_all[:, l, g * 128:(g + 1) * 128], ["wpool_all"]), zb.v(g, b))])
                            sq = tb.get(n)
                            P.act(sq, pb, AF.Square, scale=g4(l, 0, g))
                            sqs.append(sq)
                            P.act(nb.v(g, b), pb, AF.Identity, scale=col(psga, "psga", l * 4 + g))
                        sbk = s2b[b]
                        P.mm(sbk, [(ones_v, sq) for sq in sqs])
                        r = rstd_from_stat(sbk, b, 1.0 / DP, RMS_EPS)
                        for g in range(4):
                            P.tt("pool" if g % 2 else "dve", nb.v(g, b), nb.v(g, b), r, ALU.mult)

                    def st_Pw(b):
                        n = blk(b)[1]
                        ab = actb[b % 2]
                        sqs = []
                        for m in range(4):
                            pb = cring.get(b)
                            P.mm(pb, [(spw.lhsT(0, k, m * 128), V(ab[:, k, 0:n], [("actb", b % 2, k)])) for k in range(4)])
                            sq = tb.get(n)
                            P.act(sq, pb, AF.Square)
                            sqs.append(sq)
                            P.act(nb.v(4 + m, b), pb, AF.Identity, scale=g4(l, 5, m))
                        sbk = s1b[b]
                        P.mm(sbk, [(ones_v, sq) for sq in sqs])
                        r = rstd_from_stat(sbk, b, 1.0 / DP, RMS_EPS)
                        for m in range(4):
                            P.tt("pool" if m % 2 else "dve", nb.v(4 + m, b), nb.v(4 + m, b), r, ALU.mult)

                    def st_O(b):
                        for m in range(DC):
                            so = so0 if m < 4 else so1
                            pb = oring.get(b)
                            P.mm(pb, [(so.lhsT(0, k, (m % 4) * 128), nb.v(k, b)) for k in range(DC)])
                            P.tt("dve", xs.v(m, b), pb, xs.v(m, b), ALU.add)

                    def st_N2(b):
                        rmsnorm_blk(lambda c: g8(l, 1, c), b, sbk=bank(2, b))

                    st_E(0); st_A(0); st_Pw(0); st_E(1); st_O(0); st_N2(0); st_A(1); st_Pw(1); st_E(2)
                    st_O(1); st_N2(1); st_A(2); st_Pw(2); st_O(2); st_N2(2)

                with ExitStack() as fs:
                    hb = Plain(sb("hb", [128, 12, T], BF16, fs), "hb")
                    gext = Ext(sb("gext", [128, 2, HF + PT + SS * (HF + ST)], F32, fs), "gext", HF, phys=2)
                    stfs = sb("stfs", [128, NJ * SS * HF], F32, fs)
                    ofs = sb("ofs", [128, NJ * SS * HF], F32, fs)
                    ofp = sb("ofp", [128, NJ * HF], F32, fs)
                    P.alias_fence(["hb", "gext", "stfs", "ofs", "ofp"])

                    P.dma("sp", V(stfs[:], ["stfs"]), V(stf[l, ps_i].rearrange("p j w -> p (j w)"), []))

                    for (j0, nj) in FFN_GROUPS:
                        for j in range(j0, j0 + nj, 2):
                            sl = W.next((l, "fi%d" % j))
                            for jj in range(2):
                                jc = j + jj
                                jl = jc - j0
                                wc = lambda t_: col(wfdw, "wfdw", (l * NJ + jc) * 4 + t_)
                                if ps_i == 0:
                                    P.memset("dve", gext.pcols(jc, 0, HF), 0.0)
                                else:
                                    P.copy("dve", gext.pcols(jc, 0, HF),
                                           V(hp_ffn[:, (l * NJ + jc) * HF:(l * NJ + jc + 1) * HF], ["hp_ffn"]))
                                hsrc = stfs[:, jc * SS * HF:(jc + 1) * SS * HF].rearrange("p (s t) -> p s t", t=HF)
                                P.copy("dve", gext.samp(jc, 0, HF), V(hsrc, ["stfs"]))
                                for b in range(NBLK):
                                    n = blk(b)[1]
                                    three = (b == 2)
                                    pg = ring8.get(b)
                                    P.mm(pg, [(sl.lhsT(0, k, jj * 128), nb.v(k, b)) for k in range(DC)])
                                    pu = ring8.get(b)
                                    P.mm(pu, [(sl.lhsT(1, k, jj * 128), nb.v(k, b)) for k in range(DC)])
                                    pg3 = as3(pg) if three else pg
                                    pu3 = as3(pu) if three else pu
                                    P.copy("act", gext.new(jc, b), pg3)
                                    t0 = tf.get(n, three)
                                    P.act(t0, pg3, AF.Identity, scale=wc(2), bias=wc(3))
                                    t1 = tf.get(n, three)
                                    P.stt(t1, gext.new(jc, b, shift=1), wc(1), t0, ALU.mult, ALU.add)
                                    t2 = tf.get(n, three)
                                    P.stt(t2, gext.new(jc, b, shift=2), wc(0), t1, ALU.mult, ALU.add)
                                    t3 = tf.get(n, three)
                                    P.tt("dve", t3, pu3, t2, ALU.mult)
                                    sg = tf.get(n, three)
                                    P.act(sg, t2, AF.Sigmoid)
                                    P.tt("dve", hb.v(jl, b, three=three), t3, sg, ALU.mult)
                                tailv = gext.pcols(jc, HF + PT - HF, HF)
                                if ps_i == 0:
                                    P.copy("dve", V(hp_ffn[:, (l * NJ + jc) * HF:(l * NJ + jc + 1) * HF], ["hp_ffn"]), tailv)
                                else:
                                    P.copy("dve", V(ofp[:, jc * HF:(jc + 1) * HF], ["ofp"]), tailv)
                                odst = ofs[:, jc * SS * HF:(jc + 1) * SS * HF].rearrange("p (s t) -> p s t", t=HF)
                                P.copy("dve", V(odst, ["ofs"]), gext.samp(jc, HF + ST - HF, HF))
                        for q in range(D // FO_W):
                            so = W.next((l, "fo%d_%d" % (j0, q)))
                            for mloc in range(FO_W // 128):
                                m = q * (FO_W // 128) + mloc
                                for b in range(NBLK):
                                    pb = ring6.get(b)
                                    P.mm(pb, [(so.lhsT(0, jl, mloc * 128), hb.v(jl, b)) for jl in range(nj)])
                                    P.tt("dve", xs.v(m, b), pb, xs.v(m, b), ALU.add)
                    P.dma("sp", V(o_ffn_s[l, ps_i], []), V(ofs[:], ["ofs"]), is_output=True)
                    if ps_i == 1:
                        P.dma("sp", V(o_ffn_p[l], []), V(ofp[:], ["ofp"]), is_output=True)

                with ExitStack() as es:
                    ef = sb("ef", [128, DC, T], F32, es)
                    P.alias_fence(["ef"])

                    def efv(m, b):
                        lo, n = blk(b)
                        return V(ef[:, m, lo:lo + n], [("ef", m, b)])

                    sple, sg0_, sg1_ = W.acquire([(l, "ple"), (l, "g0"), (l, "g1")])
                    sgs = [sg0_, sg1_]
                    for b in range(NBLK):
                        for m in range(DC):
                            pb = ring6.get(b)
                            P.mm(pb, [(sple.lhsT(0, k, m * 128), pTb.v(k, b)) for k in range(2)])
                            P.copy("act", efv(m, b), pb)
                    for b in range(NBLK):
                        n = blk(b)[1]
                        sqs = []
                        for m in range(DC):
                            sq = tb.get(n)
                            square_x(sq, efv(m, b), m)
                            sqs.append(sq)
                        sbk = statr.get(b)
                        P.mm(sbk, [(ones_v, sq) for sq in sqs])
                        r = rstd_from_stat(sbk, b, 1.0 / D, RMS_EPS)
                        for m in range(DC):
                            P.tt("pool", efv(m, b), efv(m, b), r, ALU.mult)
                    for b in range(NBLK):
                        for c in range(DC):
                            P.copy("act", nb.v(c, b), xs.v(c, b))
                    pend_norm = []
                    for b in range(NBLK):
                        n = blk(b)[1]
                        for m in range(DC):
                            pb = ring6.get(b)
                            P.mm(pb, [(sgs[m // 4].lhsT(0, k, (m % 4) * 128), nb.v(k, b)) for k in range(DC)])
                            if m == 2:
                                while pend_norm:
                                    pend_norm.pop(0)()
                            sg = tf.get(n)
                            P.act(sg, pb, AF.Sigmoid)
                            t1 = tf.get(n)
                            P.stt(t1, efv(m, b), g8(l, 2, m), sg, ALU.mult, ALU.mult)
                            P.tt("dve", xs.v(m, b), xs.v(m, b), t1, ALU.add)
                        if l < L - 1:
                            pend_norm.append(rmsnorm_blk(lambda c, l=l: g8(l + 1, 0, c), b, split=True))
                        else:
                            pend_norm.append(rmsnorm_blk(lambda c: g8(0, 3, c), b, inplace=True, split=True))
                    while pend_norm:
                        pend_norm.pop(0)()

            P.dma("sp", V(yT[ps_i], []), V(xs.t[:], [("xs", c, b) for c in range(DC) for b in range(NBLK)]),
                  is_output=True)

        P.finish()
        build_nc.ninst = dict(P.ninst)
    return nc


def _fmaj(a, nchunk):
    s = a.shape
    a = a.reshape(s[:-1] + (nchunk, 128))
    nd = a.ndim
    perm = list(range(nd - 3)) + [nd - 1, nd - 2, nd - 3]
    return np.ascontiguousarray(a.transpose(perm))


_NC_CACHE = {}


def kernel(x_prompt, x_sample, state_pool, state_conv, state_ffn, p_prompt, p_sample,
           g_mix, w_in, w_pool, pool_scale, w_dw, b_dw, ln_g, ln_b, w_pw,
           g_out_a, g_out_b, w_out, g_ffn, w_ffn_in, w_ffn_dw, b_ffn_dw, w_ffn_out,
           w_ple, g_ple, w_ple_gate, final_norm):
    f = lambda a: np.ascontiguousarray(np.asarray(a, dtype=np.float32))
    x_prompt, x_sample, state_pool, state_conv, state_ffn, p_prompt, p_sample = map(
        f, (x_prompt, x_sample, state_pool, state_conv, state_ffn, p_prompt, p_sample))
    ncores = 8
    if "nc" not in _NC_CACHE:
        _NC_CACHE["nc"] = build_nc()
    nc = _NC_CACHE["nc"]

    p8 = np.zeros((L, 4, 8, 128), np.float32)
    p8[:, 0] = f(g_mix).reshape(L, 8, 128)
    p8[:, 1] = f(g_ffn).reshape(L, 8, 128)
    p8[:, 2] = f(g_ple).reshape(L, 8, 128)
    p8[:, 3] = f(final_norm).reshape(1, 8, 128)
    p8 = np.ascontiguousarray(p8.transpose(3, 0, 1, 2).reshape(128, L * 4 * 8))
    p4 = np.stack([f(pool_scale), f(b_dw), f(ln_g), f(ln_b), f(g_out_a), f(g_out_b)], 1).reshape(L, 6, 4, 128)
    p4 = np.ascontiguousarray(p4.transpose(3, 0, 1, 2).reshape(128, L * 6 * 4))
    wdw = f(w_dw).reshape(L, 31, 4, 128).transpose(3, 0, 2, 1)
    wdw = np.ascontiguousarray(wdw.reshape(128, L * 4 * 31))
    wf = np.concatenate([f(w_ffn_dw), f(b_ffn_dw)[:, None, :]], 1).reshape(L, 4, NJ, 128).transpose(3, 0, 2, 1)
    wf = np.ascontiguousarray(wf.reshape(128, L * NJ * 4))
    cst = np.zeros((128, 144), np.float32)
    cst[:, :128] = np.eye(128, dtype=np.float32)
    cst[:, 128:144] = (1.0 / np.arange(1, 17, dtype=np.float32))[None, :]
    shared = {"p8": p8, "p4": p4, "wdw": wdw, "wfdw": wf, "cst": cst,
              "w_in": f(w_in), "w_pool": f(w_pool), "w_pw": f(w_pw), "w_out": f(w_out),
              "w_ffn_in": f(w_ffn_in), "w_ffn_out": f(w_ffn_out), "w_ple": f(w_ple),
              "w_ple_gate": f(w_ple_gate)}

    in_maps = []
    for c in range(ncores):
        xT = np.zeros((NPASS, DC, 128, T), np.float32)
        pT = np.zeros((L, NPASS, 2, 128, T), np.float32)
        stp = np.zeros((L, NPASS, 128, 4, SS, HP + ST), np.float32)
        stc = np.zeros((L, NPASS, 128, 4, SS, HC + ST), np.float32)
        stf = np.zeros((L, NPASS, 128, NJ, SS, HF), np.float32)
        for h in range(NPASS):
            sq = slice(16 * c + SS * h, 16 * c + SS * (h + 1))
            xp = x_prompt[c, h * PT:(h + 1) * PT]
            xsm = x_sample[sq].reshape(SS * ST, D)
            xT[h] = np.concatenate([xp, xsm], 0).T.reshape(DC, 128, T)
            for l in range(L):
                pp = np.concatenate([p_prompt[l, c, h * PT:(h + 1) * PT], p_sample[l, sq].reshape(SS * ST, DPLE)], 0)
                pT[l, h] = pp.T.reshape(2, 128, T)
                stp[l, h, :, :, :, :HP] = _fmaj(state_pool[l, sq].reshape(SS * HP, DP), 4).reshape(128, 4, SS, HP)
                stc[l, h, :, :, :, :HC] = _fmaj(state_conv[l, sq].reshape(SS * HC, DP), 4).reshape(128, 4, SS, HC)
                stf[l, h] = _fmaj(state_ffn[l, sq].reshape(SS * HF, DFF), NJ).reshape(128, NJ, SS, HF)
        m = dict(shared)
        m.update({"xT": xT, "pT": pT,
                  "stp": stp.reshape(L, NPASS, 128, 4, SS * (HP + ST)),
                  "stc": stc.reshape(L, NPASS, 128, 4, SS * (HC + ST)),
                  "stf": stf.reshape(L, NPASS, 128, NJ, SS * HF)})
        in_maps.append(m)

    res = run_bass_kernel_spmd(nc, in_maps, core_ids=list(range(ncores)))

    B, S = 8, 2048
    y_prompt = np.zeros((B, S, D), np.float32)
    y_sample = np.zeros((128, ST, D), np.float32)
    pool_p = np.zeros((L, B, HP, DP), np.float32)
    conv_p = np.zeros((L, B, HC, DP), np.float32)
    ffn_p = np.zeros((L, B, HF, DFF), np.float32)
    pool_s = np.zeros((L, 128, HP, DP), np.float32)
    conv_s = np.zeros((L, 128, HC, DP), np.float32)
    ffn_s = np.zeros((L, 128, HF, DFF), np.float32)
    for c in range(ncores):
        r = res.results[c]
        yT = np.asarray(r["yT"]).reshape(NPASS, 128, DC, T)
        for h in range(NPASS):
            sq = slice(16 * c + SS * h, 16 * c + SS * (h + 1))
            yy = yT[h].transpose(2, 1, 0).reshape(T, D)
            y_prompt[c, h * PT:(h + 1) * PT] = yy[:PT]
            y_sample[sq] = yy[PT:].reshape(SS, ST, D)
            ops_ = np.asarray(r["o_pool_s"]).reshape(L, NPASS, 128, 4, SS, HP + ST)[:, h]
            pool_s[:, sq] = ops_[..., ST:].transpose(0, 3, 4, 2, 1).reshape(L, SS, HP, DP)
            ocs = np.asarray(r["o_conv_s"]).reshape(L, NPASS, 128, 4, SS, HC + ST)[:, h]
            conv_s[:, sq] = ocs[..., ST:].transpose(0, 3, 4, 2, 1).reshape(L, SS, HC, DP)
            ofs = np.asarray(r["o_ffn_s"]).reshape(L, NPASS, 128, NJ, SS, HF)[:, h]
            ffn_s[:, sq] = ofs.transpose(0, 3, 4, 2, 1).reshape(L, SS, HF, DFF)
        pool_p[:, c] = np.asarray(r["o_pool_p"]).reshape(L, 128, 4, HP).transpose(0, 3, 2, 1).reshape(L, HP, DP)
        conv_p[:, c] = np.asarray(r["o_conv_p"]).reshape(L, 128, 4, HC).transpose(0, 3, 2, 1).reshape(L, HC, DP)
        ffn_p[:, c] = np.asarray(r["o_ffn_p"]).reshape(L, 128, NJ, HF).transpose(0, 3, 2, 1).reshape(L, HF, DFF)
    return (y_prompt, y_sample, pool_p, conv_p, ffn_p, pool_s, conv_s, ffn_s)
```
